# Optimizing a Trainium2 kernel written in Bass

```python
import math
import jax, jax.numpy as jnp
from jax import lax
import numpy as np

D_MODEL = 2048
BATCH = 8
SEQ = 2048
DEPTH = 1

SGU_WIDTH = D_MODEL // 2
SGU_CHUNK = 128
SGU_GROUPS = 8
SGU_GROUP_DIM = SGU_WIDTH // SGU_GROUPS
HY_WIDTH = D_MODEL // 2
HY_ORDER = 2
HY_SHORT = 3
HY_EMB = 33
HY_BANDS = (HY_EMB - 1) // 2
HY_FILTER_HIDDEN = 64
HY_DIRS = 2
HY_DECAY_TARGET = 1e-2
HY_FAST_DECAY = 0.3
HY_SLOW_DECAY = 1.5
HY_FILTER_INIT_SCALE = 0.05
N_BRANCHES = 2
D_FF = 4 * D_MODEL
IN_WIDTH = 2 * SGU_WIDTH + (HY_ORDER + 1) * HY_WIDTH + N_BRANCHES * D_MODEL
N_MOD = 6
DEEPNORM_ALPHA = (2.0 * DEPTH) ** 0.25
DEEPNORM_BETA = (8.0 * DEPTH) ** -0.25
LN_EPS = 1e-5

kernel_name = "hybrid_gmlp_hyena_deepnorm_adaln_block"


def layer_norm(x, g, b):
    xf = x.astype(jnp.float32)
    mu = jnp.mean(xf, axis=-1, keepdims=True)
    var = jnp.mean(jnp.square(xf - mu), axis=-1, keepdims=True)
    y = (xf - mu) * lax.rsqrt(var + LN_EPS) * g.astype(jnp.float32) + b.astype(jnp.float32)
    return y.astype(x.dtype)


def spatial_gating(u, v, ln_g, ln_b, w_s, b_s):
    bsz, s, _ = v.shape
    n_chunks = s // SGU_CHUNK
    vn = layer_norm(v, ln_g, ln_b).reshape(bsz, n_chunks, SGU_CHUNK, SGU_GROUPS, SGU_GROUP_DIM)
    mixed = jnp.einsum('gqp,bnpgc->bnqgc', w_s, vn) + b_s.T[None, None, :, :, None]
    return u * mixed.reshape(bsz, s, SGU_WIDTH)


def short_conv(z, w, b):
    s = z.shape[1]
    pad = HY_SHORT // 2
    zp = jnp.pad(z, ((0, 0), (pad, HY_SHORT - 1 - pad), (0, 0)))
    return sum(w[j] * zp[:, j:j + s] for j in range(HY_SHORT)) + b


def hyena_filters(L, w1, b1, w2, b2, freq, w3):
    f32 = jnp.float32
    t = jnp.linspace(0.0, 1.0, L, dtype=f32)[:, None]
    omega = 2.0 * math.pi * jnp.arange(L, dtype=f32)[:, None] / L
    bands = jnp.linspace(1e-4, HY_BANDS - 1, HY_BANDS, dtype=f32)[None, :]
    feats = jnp.concatenate([t, jnp.cos(bands * omega), -jnp.sin(bands * omega)], axis=-1)
    fr = freq.astype(f32)
    h = jnp.sin(fr * (feats @ w1.astype(f32) + b1.astype(f32)))
    h = jnp.sin(fr * (h @ w2.astype(f32) + b2.astype(f32)))
    h = h @ w3.astype(f32)
    min_decay = math.log(HY_DECAY_TARGET) / HY_SLOW_DECAY
    max_decay = math.log(HY_DECAY_TARGET) / HY_FAST_DECAY
    deltas = jnp.abs(jnp.linspace(min_decay, max_decay, HY_WIDTH, dtype=f32))
    window = jnp.exp(-t * deltas)
    return h.reshape(L, HY_DIRS, HY_ORDER, HY_WIDTH) * window[:, None, None, :]


def two_sided_spectrum(h_fwd, h_bwd):
    L, ch = h_fwd.shape
    k = jnp.concatenate([h_fwd.at[0].add(h_bwd[0]), jnp.zeros((1, ch), h_fwd.dtype), h_bwd[:0:-1]], axis=0)
    return jnp.fft.rfft(k, axis=0)


def fft_long_conv(u, k_spec, skip):
    L = u.shape[1]
    uf = u.astype(jnp.float32)
    y = jnp.fft.irfft(jnp.fft.rfft(uf, n=2 * L, axis=1) * k_spec[None], n=2 * L, axis=1)[:, :L]
    return (y + uf * skip.astype(jnp.float32)).astype(u.dtype)


def hyena(z, conv_w, conv_b, w1, b1, w2, b2, freq, w3, skip):
    z = short_conv(z, conv_w, conv_b)
    v, x1, x2 = jnp.split(z, HY_ORDER + 1, axis=-1)
    filt = hyena_filters(z.shape[1], w1, b1, w2, b2, freq, w3)
    y = v
    for n, gate in enumerate((x1, x2)):
        k_spec = two_sided_spectrum(filt[:, 0, n], filt[:, 1, n])
        y = gate * fft_long_conv(y, k_spec, skip[n])
    return y


def setup_inputs(seed: int = 0) -> dict:
    key = jax.random.key(seed)
    keys = iter(jax.random.split(key, 40))
    D = D_MODEL
    L = DEPTH

    def nrm(shape, scale):
        return jax.random.normal(next(keys), shape, jnp.float32) * scale

    return {
        "x": nrm((BATCH, SEQ, D), 1.0),
        "c": nrm((BATCH, D), 1.0),
        "w_ada": nrm((L, D, N_MOD * D), 0.5 * D ** -0.5),
        "b_ada": nrm((L, N_MOD * D), 0.02),
        "w_in": nrm((L, D, IN_WIDTH), D ** -0.5),
        "b_in": nrm((L, IN_WIDTH), 0.02),
        "sgu_ln_g": 1.0 + nrm((L, SGU_WIDTH), 0.05),
        "sgu_ln_b": nrm((L, SGU_WIDTH), 0.02),
        "sgu_w": nrm((L, SGU_GROUPS, SGU_CHUNK, SGU_CHUNK), SGU_CHUNK ** -0.5),
        "sgu_b": 1.0 + nrm((L, SGU_GROUPS, SGU_CHUNK), 0.1),
        "hy_conv_w": nrm((L, HY_SHORT, (HY_ORDER + 1) * HY_WIDTH), HY_SHORT ** -0.5),
        "hy_conv_b": nrm((L, (HY_ORDER + 1) * HY_WIDTH), 0.02),
        "hy_w1": nrm((L, HY_EMB, HY_FILTER_HIDDEN), HY_EMB ** -0.5),
        "hy_b1": nrm((L, HY_FILTER_HIDDEN), 0.1),
        "hy_w2": nrm((L, HY_FILTER_HIDDEN, HY_FILTER_HIDDEN), HY_FILTER_HIDDEN ** -0.5),
        "hy_b2": nrm((L, HY_FILTER_HIDDEN), 0.1),
        "hy_freq": 1.0 + nrm((L, HY_FILTER_HIDDEN), 0.1),
        "hy_w3": nrm((L, HY_FILTER_HIDDEN, HY_DIRS * HY_ORDER * HY_WIDTH), HY_FILTER_INIT_SCALE * HY_FILTER_HIDDEN ** -0.5),
        "hy_skip": nrm((L, HY_ORDER, HY_WIDTH), 0.5),
        "w_branch_a": nrm((L, SGU_WIDTH, D), DEEPNORM_BETA * SGU_WIDTH ** -0.5),
        "w_branch_b": nrm((L, HY_WIDTH, D), DEEPNORM_BETA * HY_WIDTH ** -0.5),
        "w_o": nrm((L, D, D), DEEPNORM_BETA * D ** -0.5),
        "b_o": nrm((L, D), 0.02),
        "ln1_g": 1.0 + nrm((L, D), 0.05),
        "ln1_b": nrm((L, D), 0.02),
        "w_m1": nrm((L, D, D_FF), DEEPNORM_BETA * D ** -0.5),
        "b_m1": nrm((L, D_FF), 0.02),
        "w_m2": nrm((L, D_FF, D), DEEPNORM_BETA * D_FF ** -0.5),
        "b_m2": nrm((L, D), 0.02),
        "ln2_g": 1.0 + nrm((L, D), 0.05),
        "ln2_b": nrm((L, D), 0.02),
    }


def reference(x, c, w_ada, b_ada, w_in, b_in, sgu_ln_g, sgu_ln_b, sgu_w, sgu_b,
              hy_conv_w, hy_conv_b, hy_w1, hy_b1, hy_w2, hy_b2, hy_freq, hy_w3, hy_skip,
              w_branch_a, w_branch_b, w_o, b_o, ln1_g, ln1_b,
              w_m1, b_m1, w_m2, b_m2, ln2_g, ln2_b):
    split_pts = [SGU_WIDTH, 2 * SGU_WIDTH, 2 * SGU_WIDTH + (HY_ORDER + 1) * HY_WIDTH]
    cond = jax.nn.silu(c)
    for l in range(DEPTH):
        mod = (cond @ w_ada[l] + b_ada[l])[:, None, :]
        sh1, sc1, g1, sh2, sc2, g2 = jnp.split(mod, N_MOD, axis=-1)

        h = x * (1.0 + sc1) + sh1
        z = h @ w_in[l] + b_in[l]
        z_u, z_v, z_h, z_g = jnp.split(z, split_pts, axis=-1)
        y_a = spatial_gating(jax.nn.gelu(z_u, approximate=False), jax.nn.gelu(z_v, approximate=False),
                             sgu_ln_g[l], sgu_ln_b[l], sgu_w[l], sgu_b[l]) @ w_branch_a[l]
        y_b = hyena(z_h, hy_conv_w[l], hy_conv_b[l], hy_w1[l], hy_b1[l], hy_w2[l], hy_b2[l],
                    hy_freq[l], hy_w3[l], hy_skip[l]) @ w_branch_b[l]
        gate_a, gate_b = jnp.split(jax.nn.sigmoid(z_g), N_BRANCHES, axis=-1)
        mix = (gate_a * y_a + gate_b * y_b) @ w_o[l] + b_o[l]
        x = layer_norm(DEEPNORM_ALPHA * x + g1 * mix, ln1_g[l], ln1_b[l])

        h = x * (1.0 + sc2) + sh2
        f = jnp.square(jax.nn.relu(h @ w_m1[l] + b_m1[l])) @ w_m2[l] + b_m2[l]
        x = layer_norm(DEEPNORM_ALPHA * x + g2 * f, ln2_g[l], ln2_b[l])
    return x
```

```python
import contextlib
import math
import numpy as np
import ml_dtypes
import concourse.bass as bass
import concourse.mybir as mybir
from concourse.bass_utils import run_bass_kernel_spmd

F32 = mybir.dt.float32
BF16 = mybir.dt.bfloat16
AF = mybir.ActivationFunctionType
ALU = mybir.AluOpType
AX = mybir.AxisListType

D = 2048
S_LEN = 2048
NB = 8
HW = 1024
DFF = 8192
INW = 9216
NFFT = 4096
ALPHA = 2.0 ** 0.25
EPS = 1e-5
ENGS = ("pe", "act", "dve", "pool", "sp")


class Buf:
    __slots__ = ("name", "last_w", "readers")

    def __init__(self, name):
        self.name = name
        self.last_w = None
        self.readers = []


class Op:
    __slots__ = ("eng", "fn", "deps", "signal", "count", "is_dma", "sem_key", "dma_count", "waits", "wkey")

    def __init__(self, eng, fn, is_dma=False):
        self.eng = eng
        self.fn = fn
        self.deps = []
        self.signal = False
        self.count = 0
        self.is_dma = is_dma
        self.sem_key = None
        self.dma_count = 0
        self.waits = None
        self.wkey = None


class Sched:
    def __init__(self, nc, stack):
        self.nc = nc
        self.stack = stack
        self.ops = {e: [] for e in ENGS}
        self.esem = {e: stack.enter_context(nc.semaphore("s_" + e)) for e in ("pe", "act", "dve", "pool")}
        self.dsem = {}
        self.free_sems = {}
        self.nsem = 0
        self.all_bufs = []

    def buf(self, name):
        b = Buf(name)
        self.all_bufs.append(b)
        return b

    def bufs(self, name, n):
        return [self.buf(f"{name}{i}") for i in range(n)]

    def _track(self, op, reads, writes):
        deps = []
        for b in reads:
            if b.last_w is not None:
                deps.append(b.last_w)
        for b in writes:
            if b.last_w is not None:
                deps.append(b.last_w)
            deps.extend(b.readers)
        for b in reads:
            if not op.is_dma:
                b.readers = [r for r in b.readers if r.is_dma or r.eng != op.eng]
            b.readers.append(op)
        for b in writes:
            b.last_w = op
            b.readers = []
        seen = set()
        for d in deps:
            if id(d) in seen or d is op:
                continue
            seen.add(id(d))
            if (not d.is_dma) and (not op.is_dma) and d.eng == "pe" and op.eng == "pe":
                continue
            op.deps.append(d)
            if not d.is_dma:
                d.signal = True

    def op(self, eng, fn, reads=(), writes=()):
        o = Op(eng, fn)
        self._track(o, reads, writes)
        self.ops[eng].append(o)
        return o

    def dma(self, queue, fn, reads=(), writes=(), key=None, par=False):
        o = Op(queue, fn, is_dma=True)
        self._track(o, reads, writes)
        if par:
            wset = set(id(b) for b in writes)
            o.deps = [d for d in o.deps if not (d.is_dma and getattr(d, "wkey", None) == tuple(sorted(wset)))]
        o.wkey = tuple(sorted(id(b) for b in writes))
        if key is None:
            key = (list(writes) + list(reads))[0]
        kid = id(key)
        if kid not in self.dsem:
            fl = self.free_sems.setdefault(queue, [])
            if fl:
                sem, cnt = fl.pop()
            else:
                sem, cnt = self.stack.enter_context(self.nc.semaphore(f"d{self.nsem}")), 0
                self.nsem += 1
            self.dsem[kid] = [sem, cnt, queue]
        ent = self.dsem[kid]
        assert ent[2] == queue, "a DMA semaphore key must stay on one queue"
        ent[1] += 16
        o.sem_key = ent[0]
        o.dma_count = ent[1]
        self.ops[queue].append(o)
        return o

    def barrier(self):
        lasts = []
        for e in ("pe", "act", "dve", "pool"):
            for o in reversed(self.ops[e]):
                if not o.is_dma and o.fn is not None:
                    o.signal = True
                    lasts.append(o)
                    break
        dmas = [(ent[0], ent[1]) for ent in self.dsem.values() if ent[1] > 0]
        dmas += [(sem, cnt) for fl in self.free_sems.values() for sem, cnt in fl if cnt > 0]
        for e in ENGS:
            o = Op(e, None)
            o.deps = list(lasts)
            o.waits = dmas
            self.ops[e].append(o)
        for b in self.all_bufs:
            b.last_w = None
            b.readers = []
        for ent in self.dsem.values():
            self.free_sems.setdefault(ent[2], []).append((ent[0], ent[1]))
        self.dsem = {}

    def emit(self, block):
        for e in ("pe", "act", "dve", "pool"):
            c = 0
            for o in self.ops[e]:
                if o.is_dma or o.fn is None:
                    continue
                if o.signal:
                    c += 1
                    o.count = c
        handles = {"pe": block.tensor, "act": block.scalar, "dve": block.vector, "pool": block.gpsimd, "sp": block.sync}
        for e in ENGS:
            ops = self.ops[e]
            if not ops:
                continue

            def body(eng, ops=ops):
                waited = {}
                for o in ops:
                    need = {}
                    for d in o.deps:
                        if d.is_dma:
                            sem, val = d.sem_key, d.dma_count
                        else:
                            sem, val = self.esem[d.eng], d.count
                        k = id(sem)
                        if val > need.get(k, (None, 0))[1]:
                            need[k] = (sem, val)
                    if o.waits:
                        for sem, val in o.waits:
                            k = id(sem)
                            if val > need.get(k, (None, 0))[1]:
                                need[k] = (sem, val)
                    for k, (sem, val) in need.items():
                        if waited.get(k, 0) >= val:
                            continue
                        waited[k] = val
                        eng.wait_ge(sem, val)
                    if o.fn is None:
                        continue
                    ins = o.fn(eng)
                    if o.is_dma:
                        ins.then_inc(o.sem_key, 16)
                    elif o.signal:
                        ins.then_inc(self.esem[o.eng], 1)

            handles[e](body)


def build_program(stop=99, debug=False):
    nc = bass.Bass("TRN2", target_bir_lowering=False)

    def din(name, shape, dt=F32):
        return nc.dram_tensor(name, list(shape), dt, kind="ExternalInput").ap()

    def dscr(name, shape, dt):
        return nc.dram_tensor(name, list(shape), dt, kind="ExternalOutput" if debug else "Internal").ap()

    x = din("x", [S_LEN, D])
    c_pp = din("c_pp", [128, 16])
    w_ada = din("w_ada", [D, 6 * D])
    acc0 = din("acc0", [128, 6 * D])
    w_in = din("w_in", [D, INW])
    b_in_pp = din("b_in_pp", [128, 72])
    b_in_v_bc = din("b_in_v_bc", [128, HW])
    lng_bc = din("lng_bc", [128, HW])
    lnb_bc = din("lnb_bc", [128, HW])
    wsT = din("wsT", [128, 8, 128])
    bs_bc = din("bs_bc", [128, HW])
    cw_pp = din("cw_pp", [128, 3, 24])
    cb_pp = din("cb_pp", [128, 24])
    featsT = din("featsT", [33, S_LEN])
    hw1 = din("hw1", [33, 64])
    hb1_pp = din("hb1_pp", [64, 1])
    hfr_pp = din("hfr_pp", [64, 1])
    hw2 = din("hw2", [64, 64])
    hb2_pp = din("hb2_pp", [64, 1])
    hw3 = din("hw3", [64, 4096])
    window = din("window", [S_LEN, HW])
    skip_bc = din("skip_bc", [128, 2, HW])
    FTb = din("FTb", [16, 128, 16, 256], BF16)
    GTb = din("GTb", [8, 128, 32, 256], BF16)
    FTz = din("FTz", [128, 16, 256], BF16)
    nyqE = din("nyqE", [128, 16, 128], BF16)
    w_a = din("w_a", [HW, D])
    w_b = din("w_b", [HW, D])
    w_o = din("w_o", [D, D])
    bo_bc = din("bo_bc", [128, D])
    ln1g_bc = din("ln1g_bc", [128, D])
    ln1b_bc = din("ln1b_bc", [128, D])
    ln1g_pp = din("ln1g_pp", [128, 16])
    ln1b_pp = din("ln1b_pp", [128, 16])
    w_m1 = din("w_m1", [D, DFF])
    b_m1_pp = din("b_m1_pp", [128, 64])
    w_m2 = din("w_m2", [DFF, D])
    b_m2_pp = din("b_m2_pp", [128, 16])
    ln2g_bc = din("ln2g_bc", [128, D])
    ln2b_bc = din("ln2b_bc", [128, D])
    ident_d = din("ident", [128, 128])
    ones_d = din("ones", [128, 128])
    out = nc.dram_tensor("out", [S_LEN, D], F32, kind="ExternalOutput").ap()

    uT_d = dscr("uT_d", [8, 128, S_LEN], BF16)
    sguT_d = dscr("sguT_d", [8, 128, S_LEN], BF16)
    v_tm_d = dscr("v_tm_d", [S_LEN, HW], BF16)
    x1_tm_d = dscr("x1_tm_d", [S_LEN, HW], BF16)
    x2T_d = dscr("x2T_d", [8, 128, S_LEN], BF16)
    gT_d = dscr("gT_d", [32, 128, S_LEN], BF16)
    filt_d = dscr("filt_d", [S_LEN, 4096], BF16)
    K_d = dscr("K_d", [2, 32, 128, HW], F32)
    ybT_d = dscr("ybT_d", [8, 128, S_LEN], BF16)
    x1_d = dscr("x1_d", [S_LEN, D], F32)
    h2T_d = dscr("h2T_d", [16, 128, S_LEN], BF16)
    wm1b_d = nc.dram_tensor("wm1b_d", [32, 128, 16, 256], BF16).ap()
    wm2b_d = nc.dram_tensor("wm2b_d", [32, 128, 32, 128], BF16).ap()
    dbg_d = dscr("dbg_d", [128, 4 * 16 + 2 * D], F32) if debug else None
    hT_dbg = dscr("hT_dbg", [16, 128, S_LEN], BF16) if debug else None

    with contextlib.ExitStack() as st:
        S = Sched(nc, st)

        def sb(name, shape, dt, stack=st):
            return stack.enter_context(nc.sbuf_tensor("sb_" + name, list(shape), dt))

        banks = [st.enter_context(nc.psum_tensor(f"bank{i}", [128, 512], F32)) for i in range(8)]
        bkb = S.bufs("bank", 8)

        class TP:
            def __init__(self, name, shape, dt, n, stack):
                self.t = [sb(f"{name}_{i}", shape, dt, stack) for i in range(n)]
                self.b = S.bufs(name, n)
                self.n = n

            def get(self, i):
                return self.t[i % self.n], self.b[i % self.n]

        ident = sb("ident", [128, 128], F32)
        ones = sb("ones", [128, 128], F32)
        b_cst = S.buf("cst")
        S.dma("sp", lambda e: e.dma_start(out=ident[:], in_=ident_d), writes=[b_cst])
        S.dma("sp", lambda e: e.dma_start(out=ones[:], in_=ones_d), writes=[b_cst])
        binpp = sb("binpp", [128, 72], F32)
        S.dma("sp", lambda e: e.dma_start(out=binpp[:], in_=b_in_pp), writes=[b_cst])
        modpp = sb("modpp", [128, 4, 16], F32)
        b_modpp = S.buf("modpp")
        g2_bc = sb("g2_bc", [128, D], F32)
        eps_t = sb("eps_t", [128, 2], F32)
        identb = sb("identb", [128, 128], BF16)
        stG1 = contextlib.ExitStack()
        g1_bc = sb("g1_bc", [128, D], F32, stG1)
        b_g1 = S.buf("g1bc")
        b_g2 = S.buf("g2bc")
        S.op("dve", lambda e: e.memset(eps_t[:, 0:1], EPS), writes=[b_cst])
        S.op("dve", lambda e: e.memset(eps_t[:, 1:2], -0.5), writes=[b_cst])
        S.op("act", lambda e: e.activation(out=identb[:], in_=ident[:], func=AF.Identity), reads=[b_cst], writes=[b_cst])

        def bkbf(bi):
            return banks[bi][:, :].bitcast(BF16)

        W_in_v = w_in.rearrange("(kc p) n -> p kc n", p=128)

        stMod = contextlib.ExitStack()
        def mod_steps():
            acc = sb("acc", [128, 3 * D], F32, stMod)
            b_acc = S.buf("acc")
            cond = sb("cond", [128, 16], F32, stMod)
            cl = sb("cl", [128, 16], F32, stMod)
            b_cond = S.buf("cond")
            tbc = sb("tbc", [128, D], F32, stMod)
            b_tbc = S.buf("tbc")
            dg = sb("dg", [128, 16, 128], F32, stMod)
            b_dg = S.buf("dg")
            wt = TP("wada", [128, D], F32, 3, stMod)
            yield
            S.dma("pool", lambda e: e.dma_start(out=cl[:], in_=c_pp), writes=[b_cond])
            S.op("act", lambda e: e.activation(out=cond[:], in_=cl[:], func=AF.Silu), reads=[b_cond], writes=[b_cond])
            names = ["sh1", "sc1", "g1", "sh2", "sc2", "g2"]
            ppidx = {"sc1": 0, "sh1": 1, "sc2": 2, "sh2": 3}
            it = [0]
            for half in range(2):
                lo = half * 3 * D
                S.dma("pool", lambda e, lo=lo: e.dma_start(out=acc[:], in_=acc0[:, lo:lo + 3 * D]), writes=[b_acc])
                items = [(cb, kc) for cb in range(3) for kc in range(16)]

                def load(i, lo=lo, items=items):
                    cb, kc = items[i]
                    t, b = wt.get(it[0] + i)
                    S.dma("pool", lambda e, t=t, kc=kc, cb=cb: e.dma_start(out=t[:], in_=w_ada[kc * 128:(kc + 1) * 128, lo + cb * D: lo + (cb + 1) * D]), writes=[b])

                load(0)
                load(1)
                for i, (cb, kc) in enumerate(items):
                    if i + 2 < len(items):
                        load(i + 2)
                    t, b = wt.get(it[0] + i)
                    S.op("dve", lambda e, t=t, kc=kc, cb=cb: e.scalar_tensor_tensor(
                        out=acc[:, cb * D:(cb + 1) * D], in0=t[:], scalar=cond[:, kc:kc + 1], in1=acc[:, cb * D:(cb + 1) * D],
                        op0=ALU.mult, op1=ALU.add), reads=[b, b_cond, b_acc], writes=[b_acc])
                    yield
                it[0] += len(items)
                for _ in range(10):
                    yield
                for cb in range(3):
                    nm = names[half * 3 + cb]
                    dst, bd = {"g1": (g1_bc, b_g1), "g2": (g2_bc, b_g2)}.get(nm, (tbc, b_tbc))
                    for j in range(4):
                        bi = 6 + (j % 2)
                        S.op("pe", lambda e, bi=bi, cb=cb, j=j: e.matmul(banks[bi][:, :], lhsT=ones[:], rhs=acc[:, cb * D + j * 512: cb * D + (j + 1) * 512], start=True, stop=True),
                             reads=[b_acc, b_cst], writes=[bkb[bi]])
                        S.op("act", lambda e, dst=dst, j=j, bi=bi: e.activation(out=dst[:, j * 512:(j + 1) * 512], in_=banks[bi][:, :], func=AF.Identity),
                             reads=[bkb[bi]], writes=[bd])
                    if nm in ppidx:
                        idx = ppidx[nm]
                        for kc in range(16):
                            S.op("dve", lambda e, kc=kc: e.tensor_tensor(out=dg[:, kc, :], in0=tbc[:, kc * 128:(kc + 1) * 128], in1=ident[:], op=ALU.mult),
                                 reads=[b_tbc, b_cst], writes=[b_dg])
                        S.op("dve", lambda e, idx=idx: e.tensor_reduce(out=modpp[:, idx, :], in_=dg[:], axis=AX.X, op=ALU.add), reads=[b_dg], writes=[b_modpp])
                        if nm.startswith("sc"):
                            S.op("dve", lambda e, idx=idx: e.tensor_scalar(out=modpp[:, idx, :], in0=modpp[:, idx, :], scalar1=1.0, scalar2=None, op0=ALU.add),
                                 reads=[b_modpp], writes=[b_modpp])
                    yield

        mgen = mod_steps()

        def advance(k):
            for _ in range(k):
                try:
                    next(mgen)
                except StopIteration:
                    return False
            return True

        advance(1)
        with contextlib.ExitStack() as stk:
            cst4 = S.buf("cst4")
            fT = sb("fT", [33, S_LEN], F32, stk)
            w1t = sb("w1t", [33, 64], F32, stk)
            w2t = sb("w2t", [64, 64], F32, stk)
            w3f = sb("w3f", [64, 4096], F32, stk)
            w3b = sb("w3b", [64, 4096], BF16, stk)
            hv = sb("hv", [64, 8], F32, stk)
            for dst, src in ((fT[:], featsT), (w1t[:], hw1), (w2t[:], hw2), (w3f[:], hw3), (hv[:, 0:1], hb1_pp), (hv[:, 1:2], hfr_pp), (hv[:, 2:3], hb2_pp)):
                S.dma("sp", lambda e, dst=dst, src=src: e.dma_start(out=dst, in_=src), writes=[cst4])
            S.op("dve", lambda e: e.tensor_tensor(out=hv[:, 3:4], in0=hv[:, 0:1], in1=hv[:, 1:2], op=ALU.mult), reads=[cst4], writes=[cst4])
            S.op("dve", lambda e: e.tensor_tensor(out=hv[:, 4:5], in0=hv[:, 2:3], in1=hv[:, 1:2], op=ALU.mult), reads=[cst4], writes=[cst4])
            S.op("dve", lambda e: e.tensor_tensor(out=w3b[:, 0:2048], in0=w3f[:, 0:2048], in1=w3f[:, 2048:4096], op=ALU.add), reads=[cst4], writes=[cst4])
            S.op("dve", lambda e: e.tensor_tensor(out=w3b[:, 2048:4096], in0=w3f[:, 0:2048], in1=w3f[:, 2048:4096], op=ALU.subtract), reads=[cst4], writes=[cst4])
            arg = sb("arg", [64, S_LEN], F32, stk)
            wr1 = sb("wr1", [64, S_LEN], F32, stk)
            wr2 = sb("wr2", [64, S_LEN], F32, stk)
            h1 = sb("h1", [64, S_LEN], F32, stk)
            h2b = sb("h2b", [64, S_LEN], BF16, stk)
            b_arg = S.buf("arg")
            b_h1 = S.buf("h1")
            b_h2 = S.buf("h2")
            TWO_PI = 2.0 * math.pi

            def sin_layer(lhs, kdim, rhs_t, rhs_b, bias_col, dst, dst_b):
                for tt in range(4):
                    S.op("pe", lambda e, tt=tt: e.matmul(banks[tt][0:64, :], lhsT=lhs[0:kdim, :], rhs=rhs_t[0:kdim, tt * 512:(tt + 1) * 512], start=True, stop=True),
                         reads=[cst4, rhs_b], writes=[bkb[tt]])
                    S.op("act", lambda e, tt=tt: e.activation(out=arg[:, tt * 512:(tt + 1) * 512], in_=banks[tt][0:64, :], func=AF.Identity, scale=hv[:, 1:2], bias=hv[:, bias_col:bias_col + 1]),
                         reads=[bkb[tt], cst4], writes=[b_arg])
                S.op("dve", lambda e: e.tensor_scalar(out=wr1[:], in0=arg[:], scalar1=math.pi, scalar2=-TWO_PI, op0=ALU.is_gt, op1=ALU.mult), reads=[b_arg], writes=[b_arg])
                S.op("dve", lambda e: e.tensor_scalar(out=wr2[:], in0=arg[:], scalar1=-math.pi, scalar2=TWO_PI, op0=ALU.is_lt, op1=ALU.mult), reads=[b_arg], writes=[b_arg])
                S.op("dve", lambda e: e.tensor_tensor(out=wr1[:], in0=wr1[:], in1=wr2[:], op=ALU.add), reads=[b_arg], writes=[b_arg])
                S.op("dve", lambda e: e.tensor_tensor(out=arg[:], in0=arg[:], in1=wr1[:], op=ALU.add), reads=[b_arg], writes=[b_arg])
                S.op("act", lambda e: e.activation(out=dst[:], in_=arg[:], func=AF.Sin), reads=[b_arg], writes=[dst_b])

            sin_layer(w1t, 33, fT, cst4, 3, h1, b_h1)
            sin_layer(w2t, 64, h1, b_h1, 4, h2b, b_h2)
            wint = TP("wint", [128, HW], F32, 2, stk)
            fo = TP("fo", [128, 4096], BF16, 2, stk)

            def load_win(tc):
                wt_, wb_ = wint.get(tc)
                S.dma("sp", lambda e, wt_=wt_, tc=tc: e.dma_start(out=wt_[:], in_=window[tc * 128:(tc + 1) * 128, :]), writes=[wb_])

            def comp_filt(tc):
                wt_, wb_ = wint.get(tc)
                f, fb = fo.get(tc)
                for blk in range(8):
                    bi = blk
                    S.op("pe", lambda e, tc=tc, blk=blk, bi=bi: e.matmul(banks[bi][:, :], lhsT=h2b[:, tc * 128:(tc + 1) * 128], rhs=w3b[:, blk * 512:(blk + 1) * 512], start=True, stop=True),
                         reads=[b_h2, cst4], writes=[bkb[bi]])
                    S.op("dve", lambda e, f=f, wt_=wt_, blk=blk, bi=bi: e.tensor_tensor(out=f[:, blk * 512:(blk + 1) * 512], in0=banks[bi][:, :], in1=wt_[:, (blk % 2) * 512:(blk % 2 + 1) * 512], op=ALU.mult),
                         reads=[bkb[bi], wb_], writes=[fb])
                S.dma("sp", lambda e, f=f, tc=tc: e.dma_start(out=filt_d[tc * 128:(tc + 1) * 128, :], in_=f[:]), reads=[fb])

            pipelined(16, load_win, comp_filt, pf=1)
            S.barrier()

        with contextlib.ExitStack() as stk:
            skb = sb("skb", [128, 2, HW], F32, stk)
            cst5 = S.buf("cst5")
            S.dma("sp", lambda e: e.dma_start(out=skb[:], in_=skip_bc), writes=[cst5])
            nyq = sb("nyq", [128, 16, 128], BF16, stk)
            S.dma("sp", lambda e: e.dma_start(out=nyq[:], in_=nyqE), writes=[cst5])
            hs = sb("hs", [128, 16, HW], BF16, stk)
            hd = sb("hd", [128, 16, HW], BF16, stk)
            b_hs = S.buf("hs")
            ftA = TP("ftA", [128, 16, 256], BF16, 2, stk)
            ko = TP("ko", [128, HW], F32, 3, stk)
            fv = filt_d.rearrange("(tc p) c -> p tc c", p=128)
            items = [(n, j2) for n in range(2) for j2 in range(16)]

            def load_ft(i):
                n, j2 = items[i]
                a, ab = ftA.get(i)
                src = FTz if j2 == 8 else FTb[j2]
                S.dma("sp", lambda e, a=a, src=src: e.dma_start(out=a[:], in_=src), writes=[ab])

            def comp_k(i):
                n, j2 = items[i]
                if j2 == 0:
                    S.dma("sp", lambda e, n=n: e.dma_start(out=hs[:], in_=fv[:, :, n * HW:(n + 1) * HW]), writes=[b_hs])
                    S.dma("sp", lambda e, n=n: e.dma_start(out=hd[:], in_=fv[:, :, 2048 + n * HW: 2048 + (n + 1) * HW]), writes=[b_hs])
                a, ab = ftA.get(i)
                for fl in range(2):
                    fch = j2 * 2 + fl
                    src = hs if fch < 16 else hd
                    pr = (i * 2 + fl) % 3
                    bset = [pr * 2, pr * 2 + 1]
                    for hh in range(2):
                        bi = bset[hh]
                        for tc in range(16):
                            S.op("pe", lambda e, a=a, tc=tc, fl=fl, hh=hh, bi=bi, src=src, fch=fch: e.matmul(banks[bi][:, :], lhsT=a[:, tc, fl * 128:(fl + 1) * 128], rhs=src[:, tc, hh * 512:(hh + 1) * 512], start=(tc == 0), stop=(tc == 15 and fch != 16)),
                                 reads=[ab, b_hs], writes=[bkb[bi]])
                        if fch == 16:
                            for tc in range(16):
                                S.op("pe", lambda e, tc=tc, hh=hh, bi=bi: e.matmul(banks[bi][:, :], lhsT=nyq[:, tc, :], rhs=hs[:, tc, hh * 512:(hh + 1) * 512], start=False, stop=(tc == 15)),
                                     reads=[cst5, b_hs], writes=[bkb[bi]])
                    k, kb = ko.get(i * 2 + fl)
                    for hh in range(2):
                        bi = bset[hh]
                        if fch < 16:
                            S.op("dve", lambda e, k=k, hh=hh, bi=bi, n=n: e.tensor_tensor(out=k[:, hh * 512:(hh + 1) * 512], in0=banks[bi][:, :], in1=skb[:, n, hh * 512:(hh + 1) * 512], op=ALU.add),
                                 reads=[bkb[bi], cst5], writes=[kb])
                        else:
                            S.op("act", lambda e, k=k, hh=hh, bi=bi: e.activation(out=k[:, hh * 512:(hh + 1) * 512], in_=banks[bi][:, :], func=AF.Identity),
                                 reads=[bkb[bi]], writes=[kb])
                    if fch == 16:
                        S.op("dve", lambda e, k=k, n=n: e.tensor_tensor(out=k[0:1, :], in0=k[0:1, :], in1=skb[0:1, n, :], op=ALU.add), reads=[kb, cst5], writes=[kb])
                    S.dma("sp", lambda e, k=k, n=n, fch=fch: e.dma_start(out=K_d[n, fch], in_=k[:]), reads=[kb])
                advance(4)

            pipelined(len(items), load_ft, comp_k, pf=1)
            while advance(8):
                pass
            S.barrier()
        stMod.close()
        if stop <= 0:
            return finish(nc, S)

        with contextlib.ExitStack() as stA:
            hT = sb("hT", [128, 16, S_LEN], BF16, stA)
            b_hT = [[S.buf(f"hT{fc}_{tt}") for tt in range(4)] for fc in range(16)]
            with contextlib.ExitStack() as stk:
                xt = TP("xt", [128, 4, D], BF16, 2, stk)
                xv = x.rearrange("(g j p) d -> g p j d", j=4, p=128)

                def load_x(tg):
                    t, b = xt.get(tg)
                    for j in range(4):
                        S.dma("pool", lambda e, t=t, tg=tg, j=j: e.dma_start(out=t[:, j, :], in_=xv[tg][:, j, :]), writes=[b], par=(j > 0))

                def comp_x(tg):
                    t, b = xt.get(tg)
                    for fc in range(16):
                        bi = fc % 8
                        for j in range(4):
                            S.op("pe", lambda e, t=t, j=j, fc=fc, bi=bi: e.transpose(
                                out=bkbf(bi)[:, j * 128:(j + 1) * 128], in_=t[:, j, fc * 128:(fc + 1) * 128], identity=identb[:]),
                                reads=[b, b_cst], writes=[bkb[bi]])
                        S.op("act", lambda e, fc=fc, tg=tg, bi=bi: e.activation(
                            out=hT[:, fc, tg * 512:(tg + 1) * 512], in_=bkbf(bi)[:, 0:512], func=AF.Identity,
                            scale=modpp[:, 0, fc:fc + 1], bias=modpp[:, 1, fc:fc + 1]),
                            reads=[bkb[bi], b_modpp], writes=[b_hT[fc][tg]])

                pipelined(4, load_x, comp_x, pf=1)
                S.barrier()
            if debug:
                bdbg = S.buf("dbg")
                S.dma("sp", lambda e: e.dma_start(out=dbg_d[:, 0:64], in_=modpp[:].rearrange("p a b -> p (a b)")), reads=[b_modpp], key=bdbg)
                S.dma("sp", lambda e: e.dma_start(out=dbg_d[:, 64:64 + D], in_=g1_bc[:]), reads=[b_g1], key=bdbg)
                S.dma("sp", lambda e: e.dma_start(out=dbg_d[:, 64 + D:64 + 2 * D], in_=g2_bc[:]), reads=[b_g2], key=bdbg)
                S.dma("sp", lambda e: e.dma_start(out=hT_dbg.rearrange("f p t -> p f t"), in_=hT[:]), reads=[b for row in b_hT for b in row], key=bdbg)
                S.barrier()
            if stop <= 1:
                return finish(nc, S)

            def gemm_ws(Wv, cols, n_kc, rhs, n_tt, N, epilogue, stk, nm, blk=512, nslots=3):
                wt = TP("w" + nm, [128, n_kc, blk], BF16, nslots, stk)
                blocks = []
                for c in cols:
                    if blocks and blocks[-1][0] + blk > c >= blocks[-1][0] and c == blocks[-1][1][-1] + 128:
                        blocks[-1][1].append(c)
                    else:
                        blocks.append((c, [c]))

                def load(i):
                    c0, cl = blocks[i]
                    t, b = wt.get(i)
                    w = len(cl) * 128
                    S.dma("pool", lambda e, t=t, c0=c0, w=w: e.dma_start(out=t[:, :, 0:w], in_=Wv[:, :, c0:c0 + w]), writes=[b])

                load(0)
                if len(blocks) > 1:
                    load(1)
                ci = 0
                pending = [None]
                for i, (c0, cl) in enumerate(blocks):
                    if i + 2 < len(blocks):
                        load(i + 2)
                    t, b = wt.get(i)
                    for c in cl:
                        bset = [(ci % (8 // n_tt)) * n_tt + tt for tt in range(n_tt)]
                        for kc in range(n_kc):
                            for tt in range(n_tt):
                                r_ap, r_b = rhs(kc, tt)
                                S.op("pe", lambda e, t=t, kc=kc, o=c - c0, bi=bset[tt], r_ap=r_ap: e.matmul(
                                    banks[bi][:, 0:N], lhsT=t[:, kc, o:o + 128], rhs=r_ap, start=(kc == 0), stop=(kc == n_kc - 1)),
                                    reads=[b, r_b], writes=[bkb[bset[tt]]])
                        if pending[0] is not None:
                            pending[0]()
                        pending[0] = epilogue(ci, c, bset)
                        ci += 1
                if pending[0] is not None:
                    pending[0]()

            def rhs_hT(kc, tt):
                return hT[:, kc, tt * 512:(tt + 1) * 512], b_hT[kc][tt]

            with contextlib.ExitStack() as stk:
                ut = TP("ut", [128, S_LEN], BF16, 2, stk)

                def ep_u(ci, c, bset):
                    t, b = ut.get(ci)
                    for tt in range(4):
                        S.op("act", lambda e, t=t, tt=tt, bi=bset[tt], ch=c // 128: e.activation(
                            out=t[:, tt * 512:(tt + 1) * 512], in_=banks[bi][:, :], func=AF.Gelu, bias=binpp[:, ch:ch + 1]),
                            reads=[bkb[bset[tt]], b_cst], writes=[b])
                    S.dma("sp", lambda e, t=t, ci=ci: e.dma_start(out=uT_d[ci], in_=t[:]), reads=[b])

                gemm_ws(W_in_v, [i * 128 for i in range(8)], 16, rhs_hT, 4, 512, ep_u, stk, "u")
                S.barrier()
            if stop <= 2:
                return finish(nc, S)

            with contextlib.ExitStack() as stk:
                Wv_t = sb("Wv_t", [128, 16, HW], BF16, stk)
                b_Wv = S.buf("Wv")
                for hh in range(2):
                    S.dma("pool", lambda e, hh=hh: e.dma_start(out=Wv_t[:, :, hh * 512:(hh + 1) * 512], in_=W_in_v[:, :, HW + hh * 512: HW + (hh + 1) * 512]),
                          writes=[b_Wv], par=(hh > 0))
                cst2 = S.buf("cst2")
                binv = sb("binv", [128, HW], F32, stk)
                lng = sb("lng", [128, HW], F32, stk)
                lnb = sb("lnb", [128, HW], F32, stk)
                bsb = sb("bsb", [128, HW], F32, stk)
                wsf = sb("wsf", [128, 8, 128], F32, stk)
                wsb = sb("wsb", [128, 8, 128], BF16, stk)
                for dst, src in ((binv, b_in_v_bc), (lng, lng_bc), (lnb, lnb_bc), (bsb, bs_bc), (wsf, wsT)):
                    S.dma("sp", lambda e, dst=dst, src=src: e.dma_start(out=dst[:], in_=src), writes=[cst2])
                S.op("act", lambda e: e.activation(out=wsb[:], in_=wsf[:], func=AF.Identity), reads=[cst2], writes=[cst2])
                zv = TP("zv", [128, HW], F32, 3, stk)
                gv = TP("gv", [128, HW], F32, 2, stk)
                st6 = TP("st6", [128, 2, 6], F32, 2, stk)
                sm = TP("sm", [128, 8], F32, 2, stk)
                vn0 = TP("vn0", [128, HW], F32, 2, stk)
                vn = TP("vn", [128, HW], BF16, 2, stk)
                utile = TP("utile", [128, 8, 128], BF16, 5, stk)
                sgt = TP("sgt", [128, 8, 128], F32, 2, stk)
                sgo = TP("sgo", [128, 8, 128], BF16, 2, stk)
                def load_u(t_):
                    ut_, ub_ = utile.get(t_)
                    S.dma("sp", lambda e, ut_=ut_, t_=t_: e.dma_start(out=ut_[:], in_=uT_d[:, :, t_ * 128:(t_ + 1) * 128].rearrange("g p t -> p g t")), writes=[ub_])

                def zmm(t_):
                    bz = [(t_ % 3) * 2, (t_ % 3) * 2 + 1]
                    for hh in range(2):
                        for kc in range(16):
                            S.op("pe", lambda e, hh=hh, kc=kc, bi=bz[hh], t_=t_: e.matmul(
                                banks[bi][:, :], lhsT=hT[:, kc, t_ * 128:(t_ + 1) * 128], rhs=Wv_t[:, kc, hh * 512:(hh + 1) * 512],
                                start=(kc == 0), stop=(kc == 15)), reads=[b_hT[kc][t_ // 4], b_Wv], writes=[bkb[bz[hh]]])

                def chain(t_):
                    bz = [(t_ % 3) * 2, (t_ % 3) * 2 + 1]
                    zt, zb = zv.get(t_)
                    gt_, gb_ = gv.get(t_)
                    s6, s6b = st6.get(t_)
                    smt, smb = sm.get(t_)
                    v0, v0b = vn0.get(t_)
                    vt, vb = vn.get(t_)
                    for hh in range(2):
                        S.op("dve", lambda e, hh=hh, zt=zt, bi=bz[hh]: e.tensor_tensor(out=zt[:, hh * 512:(hh + 1) * 512], in0=banks[bi][:, :], in1=binv[:, hh * 512:(hh + 1) * 512], op=ALU.add),
                             reads=[bkb[bz[hh]], cst2], writes=[zb])
                    S.op("act", lambda e, zt=zt, gt_=gt_: e.activation(out=gt_[:], in_=zt[:], func=AF.Gelu), reads=[zb], writes=[gb_])
                    yield
                    for hh in range(2):
                        S.op("dve", lambda e, hh=hh, s6=s6, gt_=gt_: e.bn_stats(out=s6[:, hh, :], in_=gt_[:, hh * 512:(hh + 1) * 512]), reads=[gb_], writes=[s6b])
                    S.op("dve", lambda e: e.bn_aggr(out=smt[:, 0:2], in_=s6[:].rearrange("p a b -> p (a b)")), reads=[s6b], writes=[smb])
                    S.op("dve", lambda e: e.tensor_scalar(out=smt[:, 2:3], in0=smt[:, 1:2], scalar1=EPS, scalar2=None, op0=ALU.add), reads=[smb], writes=[smb])
                    S.op("pool", lambda e: e.tensor_tensor(out=smt[:, 4:5], in0=smt[:, 2:3], in1=eps_t[:, 1:2], op=ALU.pow), reads=[smb, b_cst], writes=[smb])
                    yield
                    S.op("dve", lambda e: e.tensor_scalar(out=smt[:, 5:6], in0=smt[:, 0:1], scalar1=-1.0, scalar2=smt[:, 4:5], op0=ALU.mult, op1=ALU.mult), reads=[smb], writes=[smb])
                    S.op("act", lambda e, gt_=gt_, v0=v0, smt=smt: e.activation(out=v0[:], in_=gt_[:], func=AF.Identity, scale=smt[:, 4:5], bias=smt[:, 5:6]),
                         reads=[gb_, smb], writes=[v0b])
                    yield
                    S.op("dve", lambda e, v0=v0: e.tensor_tensor(out=v0[:], in0=v0[:], in1=lng[:], op=ALU.mult), reads=[v0b, cst2], writes=[v0b])
                    S.op("pool", lambda e, v0=v0, vt=vt: e.tensor_tensor(out=vt[:], in0=v0[:], in1=lnb[:], op=ALU.add), reads=[v0b, cst2], writes=[vb])
                    yield

                def sgu_tail(t_):
                    ut_, ub_ = utile.get(t_)
                    vt, vb = vn.get(t_)
                    bs_ = [6, 7]
                    for g in range(8):
                        bi = bs_[g // 4]
                        S.op("pe", lambda e, g=g, bi=bi, vt=vt: e.matmul(banks[bi][:, (g % 4) * 128:(g % 4 + 1) * 128], lhsT=vt[:, g * 128:(g + 1) * 128], rhs=wsb[:, g, :], start=True, stop=True),
                             reads=[vb, cst2], writes=[bkb[bi]])
                    sg, sgb = sgt.get(t_)
                    so, sob = sgo.get(t_)
                    for hh in range(2):
                        S.op("dve", lambda e, hh=hh, sg=sg, bi=bs_[hh]: e.tensor_tensor(
                            out=sg[:, hh * 4:(hh + 1) * 4, :].rearrange("p a b -> p (a b)"), in0=banks[bi][:, :], in1=bsb[:, hh * 512:(hh + 1) * 512], op=ALU.add),
                            reads=[bkb[bs_[hh]], cst2], writes=[sgb])
                    S.op("dve", lambda e, sg=sg, so=so, ut_=ut_: e.tensor_tensor(out=so[:], in0=sg[:], in1=ut_[:], op=ALU.mult), reads=[sgb, ub_], writes=[sob])
                    S.dma("sp", lambda e, so=so, t_=t_: e.dma_start(out=sguT_d[:, :, t_ * 128:(t_ + 1) * 128].rearrange("g p t -> p g t"), in_=so[:]), reads=[sob])

                load_u(0)
                load_u(1)
                zmm(0)
                zmm(1)
                for p in range(8):
                    t0_ = 2 * p
                    if t0_ + 2 < 16:
                        load_u(t0_ + 2)
                        load_u(t0_ + 3)
                    for _ in zip(chain(t0_), chain(t0_ + 1)):
                        pass
                    if t0_ + 2 < 16:
                        zmm(t0_ + 2)
                        zmm(t0_ + 3)
                    sgu_tail(t0_)
                    sgu_tail(t0_ + 1)
                S.barrier()
            if stop <= 3:
                return finish(nc, S)

            with contextlib.ExitStack() as stk:
                cwt = sb("cwt", [128, 3, 24], F32, stk)
                cbt = sb("cbt", [128, 24], F32, stk)
                cst3 = S.buf("cst3")
                S.dma("sp", lambda e: e.dma_start(out=cwt[:], in_=cw_pp), writes=[cst3])
                S.dma("sp", lambda e: e.dma_start(out=cbt[:], in_=cb_pp), writes=[cst3])
                zp = TP("zp", [128, S_LEN + 2], F32, 2, stk)
                for i in range(2):
                    S.op("dve", lambda e, i=i: e.memset(zp.t[i][:], 0.0), writes=[zp.b[i]])
                cvA = TP("cvA", [128, S_LEN], F32, 1, stk)
                cvB = TP("cvB", [128, S_LEN], F32, 1, stk)
                cvC = TP("cvC", [128, S_LEN], BF16, 2, stk)
                cvO = TP("cvO", [128, S_LEN], BF16, 1, stk)
                tmo = TP("tmo", [128, 16, 128], BF16, 2, stk)

                def ep_h(ci, c, bset):
                    ch = c // 128
                    hc = ch - 16
                    sel = hc // 8
                    cc = hc % 8
                    z, zb = zp.get(ci)
                    for tt in range(4):
                        S.op("act", lambda e, z=z, tt=tt, bi=bset[tt], ch=ch: e.activation(
                            out=z[:, 1 + tt * 512: 1 + (tt + 1) * 512], in_=banks[bi][:, :], func=AF.Identity, bias=binpp[:, ch:ch + 1]),
                            reads=[bkb[bset[tt]], b_cst], writes=[zb])
                    a, ab = cvA.get(ci)
                    bb_, bbb = cvB.get(ci)
                    S.op("act", lambda e, z=z, a=a, hc=hc: e.activation(out=a[:], in_=z[:, 1:S_LEN + 1], func=AF.Identity, scale=cwt[:, 1, hc:hc + 1], bias=cbt[:, hc:hc + 1]),
                         reads=[zb, cst3], writes=[ab])
                    S.op("dve", lambda e, z=z, a=a, bb_=bb_, hc=hc: e.scalar_tensor_tensor(out=bb_[:], in0=z[:, 0:S_LEN], scalar=cwt[:, 0, hc:hc + 1], in1=a[:], op0=ALU.mult, op1=ALU.add),
                         reads=[zb, ab, cst3], writes=[bbb])
                    if sel == 2:
                        o, ob = cvO.get(ci)
                        S.op("dve", lambda e, z=z, bb_=bb_, o=o, hc=hc: e.scalar_tensor_tensor(out=o[:], in0=z[:, 2:S_LEN + 2], scalar=cwt[:, 2, hc:hc + 1], in1=bb_[:], op0=ALU.mult, op1=ALU.add),
                             reads=[zb, bbb, cst3], writes=[ob])
                        S.dma("sp", lambda e, o=o, cc=cc: e.dma_start(out=x2T_d[cc], in_=o[:]), reads=[ob])
                    else:
                        cv, cvb = cvC.get(ci)
                        S.op("dve", lambda e, z=z, bb_=bb_, cv=cv, hc=hc: e.scalar_tensor_tensor(out=cv[:], in0=z[:, 2:S_LEN + 2], scalar=cwt[:, 2, hc:hc + 1], in1=bb_[:], op0=ALU.mult, op1=ALU.add),
                             reads=[zb, bbb, cst3], writes=[cvb])
                        tm, tmb = tmo.get(ci)

                        def later(cv=cv, cvb=cvb, tm=tm, tmb=tmb, bset=bset, sel=sel, cc=cc):
                            for j4 in range(4):
                                bi = bset[j4]
                                for jj in range(4):
                                    j = j4 * 4 + jj
                                    S.op("pe", lambda e, cv=cv, j=j, jj=jj, bi=bi: e.transpose(out=bkbf(bi)[:, jj * 128:(jj + 1) * 128], in_=cv[:, j * 128:(j + 1) * 128], identity=identb[:]),
                                         reads=[cvb, b_cst], writes=[bkb[bi]])
                                S.op("dve", lambda e, tm=tm, j4=j4, bi=bi: e.tensor_copy(out=tm[:, j4 * 4:(j4 + 1) * 4, :].rearrange("p a b -> p (a b)"), in_=bkbf(bi)[:, 0:512]),
                                     reads=[bkb[bi]], writes=[tmb])
                            dst = v_tm_d if sel == 0 else x1_tm_d
                            S.dma("sp", lambda e, tm=tm, dst=dst, cc=cc: e.dma_start(out=dst.rearrange("(j p) c -> p j c", p=128)[:, :, cc * 128:(cc + 1) * 128], in_=tm[:]), reads=[tmb])

                        return later
                        for j4 in range(4):
                            bi = bset[j4]
                            for jj in range(4):
                                j = j4 * 4 + jj
                                S.op("pe", lambda e, cv=cv, j=j, jj=jj, bi=bi: e.transpose(out=bkbf(bi)[:, jj * 128:(jj + 1) * 128], in_=cv[:, j * 128:(j + 1) * 128], identity=identb[:]),
                                     reads=[cvb, b_cst], writes=[bkb[bi]])
                            S.op("dve", lambda e, tm=tm, j4=j4, bi=bi: e.tensor_copy(out=tm[:, j4 * 4:(j4 + 1) * 4, :].rearrange("p a b -> p (a b)"), in_=bkbf(bi)[:, 0:512]),
                                 reads=[bkb[bi]], writes=[tmb])
                        dst = v_tm_d if sel == 0 else x1_tm_d
                        S.dma("sp", lambda e, tm=tm, dst=dst, cc=cc: e.dma_start(out=dst.rearrange("(j p) c -> p j c", p=128)[:, :, cc * 128:(cc + 1) * 128], in_=tm[:]), reads=[tmb])

                gemm_ws(W_in_v, [2048 + i * 128 for i in range(24)], 16, rhs_hT, 4, 512, ep_h, stk, "h")
                S.barrier()
            if stop <= 4:
                return finish(nc, S)

            with contextlib.ExitStack() as stk:
                gt2 = TP("gt2", [128, S_LEN], BF16, 2, stk)

                def ep_g(ci, c, bset):
                    t, b = gt2.get(ci)
                    for tt in range(4):
                        S.op("act", lambda e, t=t, tt=tt, bi=bset[tt], ch=c // 128: e.activation(
                            out=t[:, tt * 512:(tt + 1) * 512], in_=banks[bi][:, :], func=AF.Sigmoid, bias=binpp[:, ch:ch + 1]),
                            reads=[bkb[bset[tt]], b_cst], writes=[b])
                    S.dma("sp", lambda e, t=t, ci=ci: e.dma_start(out=gT_d[ci], in_=t[:]), reads=[b])

                gemm_ws(W_in_v, [5120 + i * 128 for i in range(32)], 16, rhs_hT, 4, 512, ep_g, stk, "g")
                S.barrier()
        if stop <= 5:
            return finish(nc, S)

        Wm1_v = w_m1.rearrange("(kc p) n -> p kc n", p=128)
        Wm2_v = w_m2.rearrange("(kc p) n -> p kc n", p=128)
        b_wconv = S.bufs("wconv", 2)

        wc_next = [0]

        def wconv_step(n):
            lo = wc_next[0]
            hi = min(64, lo + n)
            wc_next[0] = hi
            for k in range(lo, hi):
                if k < 32:
                    S.dma("pool", lambda e, k=k: e.dma_start(out=wm1b_d[k], in_=Wm1_v[:, :, k * 256:(k + 1) * 256]), writes=[b_wconv[k % 2]])
                else:
                    j = k - 32
                    S.dma("pool", lambda e, j=j: e.dma_start(out=wm2b_d[j], in_=Wm2_v[:, (j // 16) * 32:(j // 16 + 1) * 32, (j % 16) * 128:(j % 16 + 1) * 128]), writes=[b_wconv[k % 2]])

        with contextlib.ExitStack() as stC:
            Y = sb("Y", [128, 32, HW], BF16, stC)
            b_Y = [S.buf(f"Y{j}") for j in range(16)]
            utm = sb("utm", [128, 16, HW], BF16, stC)
            b_utm = [S.buf(f"utm{j}") for j in range(16)]

            def forward(n, stk):
                ftA = TP(f"cA{n}", [128, 16, 256], BF16, 2, stk)
                ftB = TP(f"cB{n}", [128, 16, 256], BF16, 2, stk)
                kre = TP(f"kre{n}", [128, 2, HW], F32, 2, stk)
                kim = TP(f"kim{n}", [128, 2, HW], F32, 2, stk)
                tmp = TP(f"ctmp{n}", [128, 4, 512], F32, 2, stk)

                def load(j2):
                    a, ab = ftA.get(j2)
                    b2, bb2 = ftB.get(j2)
                    kr, krb = kre.get(j2)
                    ki, kib = kim.get(j2)
                    S.dma("sp", lambda e, a=a, j2=j2: e.dma_start(out=a[:], in_=FTb[j2]), writes=[ab])
                    S.dma("sp", lambda e, b2=b2, j2=j2: e.dma_start(out=b2[:], in_=FTb[8 + j2]), writes=[bb2])
                    S.dma("sp", lambda e, kr=kr, j2=j2, n=n: e.dma_start(out=kr[:], in_=K_d[n, 2 * j2:2 * j2 + 2].rearrange("a p c -> p a c")), writes=[krb])
                    S.dma("sp", lambda e, ki=ki, j2=j2, n=n: e.dma_start(out=ki[:], in_=K_d[n, 16 + 2 * j2:16 + 2 * j2 + 2].rearrange("a p c -> p a c")), writes=[kib])

                def comp(j2):
                    wconv_step(2)
                    a, ab = ftA.get(j2)
                    b2, bb2 = ftB.get(j2)
                    kr, krb = kre.get(j2)
                    ki, kib = kim.get(j2)
                    for fl in range(2):
                        j = j2 * 2 + fl
                        base = (j % 2) * 4
                        for ri, (ft, ftb) in enumerate(((a, ab), (b2, bb2))):
                            for hh in range(2):
                                bi = base + ri * 2 + hh
                                for tc in range(16):
                                    S.op("pe", lambda e, ft=ft, tc=tc, fl=fl, hh=hh, bi=bi: e.matmul(banks[bi][:, :], lhsT=ft[:, tc, fl * 128:(fl + 1) * 128], rhs=utm[:, tc, hh * 512:(hh + 1) * 512], start=(tc == 0), stop=(tc == 15)),
                                         reads=[ftb, b_utm[tc]], writes=[bkb[bi]])
                        for hh in range(2):
                            t, tb = tmp.get(j * 2 + hh)
                            ure, uim = base + hh, base + 2 + hh
                            cs = slice(hh * 512, (hh + 1) * 512)
                            S.op("dve", lambda e, t=t, ure=ure, kr=kr, fl=fl, cs=cs: e.tensor_tensor(out=t[:, 0, :], in0=banks[ure][:, :], in1=kr[:, fl, cs], op=ALU.mult), reads=[bkb[ure], krb], writes=[tb])
                            S.op("dve", lambda e, t=t, uim=uim, ki=ki, fl=fl, cs=cs: e.tensor_tensor(out=t[:, 1, :], in0=banks[uim][:, :], in1=ki[:, fl, cs], op=ALU.mult), reads=[bkb[uim], kib], writes=[tb])
                            S.op("dve", lambda e, t=t, ure=ure, ki=ki, fl=fl, cs=cs: e.tensor_tensor(out=t[:, 2, :], in0=banks[ure][:, :], in1=ki[:, fl, cs], op=ALU.mult), reads=[bkb[ure], kib], writes=[tb])
                            S.op("dve", lambda e, t=t, uim=uim, kr=kr, fl=fl, cs=cs: e.tensor_tensor(out=t[:, 3, :], in0=banks[uim][:, :], in1=kr[:, fl, cs], op=ALU.mult), reads=[bkb[uim], krb], writes=[tb])
                            S.op("dve", lambda e, t=t, j=j, cs=cs: e.tensor_tensor(out=Y[:, j, cs], in0=t[:, 0, :], in1=t[:, 1, :], op=ALU.subtract), reads=[tb], writes=[b_Y[j]])
                            S.op("dve", lambda e, t=t, j=j, cs=cs: e.tensor_tensor(out=Y[:, 16 + j, cs], in0=t[:, 2, :], in1=t[:, 3, :], op=ALU.add), reads=[tb], writes=[b_Y[j]])
                            if j == 0:
                                S.op("dve", lambda e, ure=ure, kr=kr, cs=cs: e.tensor_tensor(out=Y[0:1, 0, cs], in0=banks[ure][0:1, :], in1=kr[0:1, 0, cs], op=ALU.mult), reads=[bkb[ure], krb, b_Y[0]], writes=[b_Y[0]])
                                S.op("dve", lambda e, uim=uim, ki=ki, cs=cs: e.tensor_tensor(out=Y[0:1, 16, cs], in0=banks[uim][0:1, :], in1=ki[0:1, 0, cs], op=ALU.mult), reads=[bkb[uim], kib, b_Y[0]], writes=[b_Y[0]])

                pipelined(8, load, comp, pf=1)
                S.barrier()

            with contextlib.ExitStack() as stk:
                S.dma("sp", lambda e: e.dma_start(out=utm[:], in_=v_tm_d.rearrange("(tc p) c -> p tc c", p=128)), writes=b_utm)
                forward(0, stk)
            with contextlib.ExitStack() as stk:
                gt = TP("gt1", [128, 32, 256], BF16, 2, stk)
                x1t = TP("x1t", [128, HW], BF16, 3, stk)

                def load_i1(tc):
                    if tc % 2 == 0:
                        g, gb = gt.get(tc // 2)
                        S.dma("sp", lambda e, g=g, tb_=tc // 2: e.dma_start(out=g[:], in_=GTb[tb_]), writes=[gb])
                    xt_, xb_ = x1t.get(tc)
                    S.dma("sp", lambda e, xt_=xt_, tc=tc: e.dma_start(out=xt_[:], in_=x1_tm_d[tc * 128:(tc + 1) * 128, :]), writes=[xb_])

                def comp_i1(tc):
                    wconv_step(1)
                    g, gb = gt.get(tc // 2)
                    tl = tc % 2
                    xt_, xb_ = x1t.get(tc)
                    for hh in range(2):
                        bi = (tc % 4) * 2 + hh
                        for fc in range(32):
                            S.op("pe", lambda e, g=g, fc=fc, tl=tl, hh=hh, bi=bi: e.matmul(banks[bi][:, :], lhsT=g[:, fc, tl * 128:(tl + 1) * 128], rhs=Y[:, fc, hh * 512:(hh + 1) * 512], start=(fc == 0), stop=(fc == 31)),
                                 reads=[gb, b_Y[fc % 16]], writes=[bkb[bi]])
                        S.op("dve", lambda e, xt_=xt_, tc=tc, hh=hh, bi=bi: e.tensor_tensor(out=utm[:, tc, hh * 512:(hh + 1) * 512], in0=banks[bi][:, :], in1=xt_[:, hh * 512:(hh + 1) * 512], op=ALU.mult),
                             reads=[bkb[bi], xb_], writes=[b_utm[tc]])

                pipelined(16, load_i1, comp_i1, pf=1)
                S.barrier()
            with contextlib.ExitStack() as stk:
                forward(1, stk)
            with contextlib.ExitStack() as stk:
                gt = TP("gt2_", [128, 32, 256], BF16, 2, stk)
                x2t = TP("x2t", [128, 256], BF16, 4, stk)
                ybo = TP("ybo", [128, 256], BF16, 3, stk)

                def load_i2(it):
                    tb_, cc = it // 8, it % 8
                    if cc == 0:
                        g, gb = gt.get(tb_)
                        S.dma("sp", lambda e, g=g, tb_=tb_: e.dma_start(out=g[:], in_=GTb[tb_]), writes=[gb])
                    xt_, xb_ = x2t.get(it)
                    S.dma("sp", lambda e, xt_=xt_, cc=cc, tb_=tb_: e.dma_start(out=xt_[:], in_=x2T_d[cc][:, tb_ * 256:(tb_ + 1) * 256]), writes=[xb_])

                def comp_i2(it):
                    if it % 4 == 0:
                        wconv_step(1)
                    tb_, cc = it // 8, it % 8
                    g, gb = gt.get(tb_)
                    xt_, xb_ = x2t.get(it)
                    bi = it % 8
                    for fc in range(32):
                        S.op("pe", lambda e, g=g, fc=fc, cc=cc, bi=bi: e.matmul(banks[bi][:, 0:256], lhsT=Y[:, fc, cc * 128:(cc + 1) * 128], rhs=g[:, fc, :], start=(fc == 0), stop=(fc == 31)),
                             reads=[gb, b_Y[fc % 16]], writes=[bkb[bi]])
                    yo, yob = ybo.get(it)
                    S.op("dve", lambda e, yo=yo, xt_=xt_, bi=bi: e.tensor_tensor(out=yo[:], in0=banks[bi][:, 0:256], in1=xt_[:], op=ALU.mult), reads=[bkb[bi], xb_], writes=[yob])
                    S.dma("sp", lambda e, yo=yo, cc=cc, tb_=tb_: e.dma_start(out=ybT_d[cc][:, tb_ * 256:(tb_ + 1) * 256], in_=yo[:]), reads=[yob])

                pipelined(64, load_i2, comp_i2, pf=3)
                wconv_step(64)
                S.barrier()
        if stop <= 10:
            return finish(nc, S)

        with contextlib.ExitStack() as stM:
            mT = sb("mT", [128, 16, S_LEN], BF16, stM)
            b_mT = [[S.buf(f"mT{fc}_{th}") for th in range(2)] for fc in range(16)]
            with contextlib.ExitStack() as stk:
                sgu = sb("sgu", [128, 8, S_LEN], BF16, stk)
                ybT = sb("ybT", [128, 8, S_LEN], BF16, stk)
                b_sg = S.buf("sgu")
                b_yb = S.buf("ybT")
                S.dma("sp", lambda e: e.dma_start(out=sgu[:], in_=sguT_d.rearrange("g p t -> p g t")), writes=[b_sg])
                S.dma("sp", lambda e: e.dma_start(out=ybT[:], in_=ybT_d.rearrange("g p t -> p g t")), writes=[b_yb])
                wa = TP("wa", [128, 8, 256], BF16, 2, stk)
                wb_ = TP("wbb", [128, 8, 256], BF16, 2, stk)
                ga = TP("ga", [128, S_LEN], BF16, 2, stk)
                gbt = TP("gbt", [128, S_LEN], BF16, 2, stk)
                mt = TP("mtmp", [128, 2, 512], F32, 3, stk)
                Wa_v = w_a.rearrange("(kc p) n -> p kc n", p=128)
                Wb_v = w_b.rearrange("(kc p) n -> p kc n", p=128)

                def load_ab(blk):
                    wat, wab = wa.get(blk)
                    wbt, wbb = wb_.get(blk)
                    S.dma("pool", lambda e, wat=wat, blk=blk: e.dma_start(out=wat[:], in_=Wa_v[:, :, blk * 256:(blk + 1) * 256]), writes=[wab])
                    S.dma("pool", lambda e, wbt=wbt, blk=blk: e.dma_start(out=wbt[:], in_=Wb_v[:, :, blk * 256:(blk + 1) * 256]), writes=[wbb])

                def load_g(fc):
                    gat, gab = ga.get(fc)
                    gbt_, gbb = gbt.get(fc)
                    S.dma("sp", lambda e, gat=gat, fc=fc: e.dma_start(out=gat[:], in_=gT_d[fc]), writes=[gab])
                    S.dma("sp", lambda e, gbt_=gbt_, fc=fc: e.dma_start(out=gbt_[:], in_=gT_d[16 + fc]), writes=[gbb])

                pidx = 0
                load_ab(0)
                load_g(0)
                for blk in range(8):
                    if blk + 1 < 8:
                        load_ab(blk + 1)
                    wat, wab = wa.get(blk)
                    wbt, wbb = wb_.get(blk)
                    for fl in range(2):
                        fc = blk * 2 + fl
                        if fc + 1 < 16:
                            load_g(fc + 1)
                        gat, gab = ga.get(fc)
                        gbt_, gbb = gbt.get(fc)
                        for th in range(2):
                            base = (pidx % 2) * 4
                            for (wt_, wtb, src, srcb, off) in ((wat, wab, sgu, b_sg, 0), (wbt, wbb, ybT, b_yb, 2)):
                                for kc in range(8):
                                    for tl in range(2):
                                        bi = base + off + tl
                                        tt = th * 2 + tl
                                        S.op("pe", lambda e, wt_=wt_, kc=kc, fl=fl, src=src, tt=tt, bi=bi: e.matmul(banks[bi][:, :], lhsT=wt_[:, kc, fl * 128:(fl + 1) * 128], rhs=src[:, kc, tt * 512:(tt + 1) * 512], start=(kc == 0), stop=(kc == 7)),
                                             reads=[wtb, srcb], writes=[bkb[bi]])
                            for tl in range(2):
                                tt = th * 2 + tl
                                t, tb = mt.get(pidx * 2 + tl)
                                cs = slice(tt * 512, (tt + 1) * 512)
                                S.op("dve", lambda e, t=t, gat=gat, cs=cs, bi=base + tl: e.tensor_tensor(out=t[:, 0, :], in0=banks[bi][:, :], in1=gat[:, cs], op=ALU.mult), reads=[bkb[base + tl], gab], writes=[tb])
                                S.op("dve", lambda e, t=t, gbt_=gbt_, cs=cs, bi=base + 2 + tl: e.tensor_tensor(out=t[:, 1, :], in0=banks[bi][:, :], in1=gbt_[:, cs], op=ALU.mult), reads=[bkb[base + 2 + tl], gbb], writes=[tb])
                                S.op("dve", lambda e, t=t, fc=fc, cs=cs: e.tensor_tensor(out=mT[:, fc, cs], in0=t[:, 0, :], in1=t[:, 1, :], op=ALU.add), reads=[tb], writes=[b_mT[fc][th]])
                            pidx += 1
                S.barrier()
            if stop <= 11:
                return finish(nc, S)
            with contextlib.ExitStack() as stk:
                cst6 = S.buf("cst6")
                g1bo = sb("g1bo", [128, D], F32, stk)
                S.dma("sp", lambda e: e.dma_start(out=g1bo[:], in_=bo_bc), writes=[cst6])
                S.op("dve", lambda e: e.tensor_tensor(out=g1bo[:], in0=g1bo[:], in1=g1_bc[:], op=ALU.mult), reads=[cst6, b_g1], writes=[cst6])
                wo = TP("wo", [128, 16, 512], BF16, 2, stk)
                xbk = TP("xbk", [128, 4, 512], F32, 3, stk)
                tbk = TP("tbk", [128, 4, 512], F32, 2, stk)
                Wo_v = w_o.rearrange("(kc p) n -> p kc n", p=128)
                xq = x.rearrange("(q j p) d -> q p j d", j=4, p=128)
                x1q = x1_d.rearrange("(q j p) d -> q p j d", j=4, p=128)

                def load_wo(cb):
                    t, b = wo.get(cb)
                    S.dma("pool", lambda e, t=t, cb=cb: e.dma_start(out=t[:], in_=Wo_v[:, :, cb * 512:(cb + 1) * 512]), writes=[b])

                def load_xb(it):
                    cb, q = it // 4, it % 4
                    xt_, xb_ = xbk.get(it)
                    S.dma("sp", lambda e, xt_=xt_, q=q, cb=cb: e.dma_start(out=xt_[:], in_=xq[q][:, :, cb * 512:(cb + 1) * 512]), writes=[xb_])

                def comp_xb(it):
                    cb, q = it // 4, it % 4
                    if q == 0 and cb + 1 < 4:
                        load_wo(cb + 1)
                    wt_, wtb = wo.get(cb)
                    cs = slice(cb * 512, (cb + 1) * 512)
                    xt_, xb_ = xbk.get(it)
                    tt2, tb2 = tbk.get(it)
                    for j in range(4):
                        tc = q * 4 + j
                        bi = (it * 4 + j) % 8
                        for kc in range(16):
                            S.op("pe", lambda e, kc=kc, tc=tc, wt_=wt_, bi=bi: e.matmul(banks[bi][:, :], lhsT=mT[:, kc, tc * 128:(tc + 1) * 128], rhs=wt_[:, kc, :], start=(kc == 0), stop=(kc == 15)),
                                 reads=[b_mT[kc][tc // 8], wtb], writes=[bkb[bi]])
                        S.op("dve", lambda e, tt2=tt2, bi=bi, cs=cs, j=j: e.tensor_tensor(out=tt2[:, j, :], in0=banks[bi][:, :], in1=g1_bc[:, cs], op=ALU.mult), reads=[bkb[bi], b_g1], writes=[tb2])
                        S.op("dve", lambda e, xt_=xt_, cs=cs, j=j: e.scalar_tensor_tensor(out=xt_[:, j, :], in0=xt_[:, j, :], scalar=ALPHA, in1=g1bo[:, cs], op0=ALU.mult, op1=ALU.add), reads=[xb_, cst6], writes=[xb_])
                    S.op("dve", lambda e, tt2=tt2, xt_=xt_: e.tensor_tensor(out=tt2[:], in0=tt2[:], in1=xt_[:], op=ALU.add), reads=[tb2, xb_], writes=[tb2])
                    S.dma("sp", lambda e, tt2=tt2, q=q, cs=cs: e.dma_start(out=x1q[q][:, :, cs], in_=tt2[:]), reads=[tb2])

                load_wo(0)
                pipelined(16, load_xb, comp_xb, pf=2)
                S.barrier()
        if stop <= 12:
            return finish(nc, S)

        with contextlib.ExitStack() as stk:
            cst6 = S.buf("cst6c")
            l1g = sb("l1g", [128, D], F32, stk)
            l1b = sb("l1b", [128, D], F32, stk)
            lpp = sb("lpp", [128, 4, 16], F32, stk)
            S.dma("sp", lambda e: e.dma_start(out=l1g[:], in_=ln1g_bc), writes=[cst6])
            S.dma("sp", lambda e: e.dma_start(out=l1b[:], in_=ln1b_bc), writes=[cst6])
            S.dma("sp", lambda e: e.dma_start(out=lpp[:, 0, :], in_=ln1g_pp), writes=[cst6])
            S.dma("sp", lambda e: e.dma_start(out=lpp[:, 1, :], in_=ln1b_pp), writes=[cst6])
            S.op("dve", lambda e: e.tensor_tensor(out=lpp[:, 2, :], in0=lpp[:, 0, :], in1=modpp[:, 2, :], op=ALU.mult), reads=[cst6, b_modpp], writes=[cst6])
            S.op("dve", lambda e: e.tensor_tensor(out=lpp[:, 3, :], in0=lpp[:, 1, :], in1=modpp[:, 2, :], op=ALU.mult), reads=[cst6, b_modpp], writes=[cst6])
            S.op("dve", lambda e: e.tensor_tensor(out=lpp[:, 3, :], in0=lpp[:, 3, :], in1=modpp[:, 3, :], op=ALU.add), reads=[cst6, b_modpp], writes=[cst6])
            rr = TP("rr", [128, D], F32, 3, stk)
            rn = TP("rn", [128, D], F32, 2, stk)
            x1o = TP("x1o", [128, D], F32, 2, stk)
            h2q = TP("h2q", [128, D], BF16, 8, stk)
            s6 = TP("s6b", [128, 4, 6], F32, 2, stk)
            sm = TP("smb", [128, 8], F32, 3, stk)
            h2o = TP("h2o", [128, 16, 512], BF16, 2, stk)

            def load_r(tc):
                r, rb = rr.get(tc)
                S.dma("sp", lambda e, r=r, tc=tc: e.dma_start(out=r[:], in_=x1_d[tc * 128:(tc + 1) * 128, :]), writes=[rb])

            def stage_a1(tc):
                r, rb = rr.get(tc)
                s6t, s6b_ = s6.get(tc)
                smt, smb = sm.get(tc)
                for cb in range(4):
                    S.op("dve", lambda e, s6t=s6t, r=r, cb=cb: e.bn_stats(out=s6t[:, cb, :], in_=r[:, cb * 512:(cb + 1) * 512]), reads=[rb], writes=[s6b_])
                S.op("dve", lambda e: e.bn_aggr(out=smt[:, 0:2], in_=s6t[:].rearrange("p a b -> p (a b)")), reads=[s6b_], writes=[smb])
                S.op("dve", lambda e: e.tensor_scalar(out=smt[:, 2:3], in0=smt[:, 1:2], scalar1=EPS, scalar2=None, op0=ALU.add), reads=[smb], writes=[smb])
                S.op("pool", lambda e: e.tensor_tensor(out=smt[:, 4:5], in0=smt[:, 2:3], in1=eps_t[:, 1:2], op=ALU.pow), reads=[smb, b_cst], writes=[smb])

            def stage_a2(tc):
                r, rb = rr.get(tc)
                smt, smb = sm.get(tc)
                rnt, rnb = rn.get(tc)
                hq, hqb = h2q.get(tc)
                S.op("dve", lambda e: e.tensor_scalar(out=smt[:, 5:6], in0=smt[:, 0:1], scalar1=-1.0, scalar2=smt[:, 4:5], op0=ALU.mult, op1=ALU.mult), reads=[smb], writes=[smb])
                S.op("act", lambda e, r=r, rnt=rnt, smt=smt: e.activation(out=rnt[:], in_=r[:], func=AF.Identity, scale=smt[:, 4:5], bias=smt[:, 5:6]), reads=[rb, smb], writes=[rnb])
                S.op("act", lambda e, r=r, hq=hq, smt=smt: e.activation(out=hq[:], in_=r[:], func=AF.Identity, scale=smt[:, 4:5], bias=smt[:, 5:6]), reads=[rb, smb], writes=[hqb])

            def stage_x(tc):
                rnt, rnb = rn.get(tc)
                xo, xob = x1o.get(tc)
                S.op("dve", lambda e, rnt=rnt, xo=xo: e.tensor_tensor(out=xo[:], in0=rnt[:], in1=l1g[:], op=ALU.mult), reads=[rnb, cst6], writes=[xob])
                S.op("dve", lambda e, xo=xo: e.tensor_tensor(out=xo[:], in0=xo[:], in1=l1b[:], op=ALU.add), reads=[xob, cst6], writes=[xob])
                S.dma("sp", lambda e, xo=xo, tc=tc: e.dma_start(out=x1_d[tc * 128:(tc + 1) * 128, :], in_=xo[:]), reads=[xob])

            def group(g):
                ho, hob = h2o.get(g)
                for j in range(4):
                    hq, hqb = h2q.get(g * 4 + j)
                    for fc in range(16):
                        bi = fc // 2
                        S.op("pe", lambda e, hq=hq, fc=fc, bi=bi, j=j: e.transpose(out=bkbf(bi)[:, (fc % 2) * 512 + j * 128:(fc % 2) * 512 + (j + 1) * 128], in_=hq[:, fc * 128:(fc + 1) * 128], identity=identb[:]),
                             reads=[hqb, b_cst], writes=[bkb[bi]])
                for fc in range(16):
                    bi = fc // 2
                    S.op("act", lambda e, ho=ho, fc=fc, bi=bi: e.activation(out=ho[:, fc, :], in_=bkbf(bi)[:, (fc % 2) * 512:(fc % 2 + 1) * 512], func=AF.Identity, scale=lpp[:, 2, fc:fc + 1], bias=lpp[:, 3, fc:fc + 1]),
                         reads=[bkb[bi], cst6], writes=[hob])
                S.dma("sp", lambda e, ho=ho, g=g: e.dma_start(out=h2T_d[:, :, g * 512:(g + 1) * 512].rearrange("f p t -> p f t"), in_=ho[:]), reads=[hob])

            load_r(0)
            load_r(1)
            stage_a1(0)
            stage_a2(0)
            for tc in range(16):
                if tc + 2 < 16:
                    load_r(tc + 2)
                if tc + 1 < 16:
                    stage_a1(tc + 1)
                stage_x(tc)
                if tc + 1 < 16:
                    stage_a2(tc + 1)
                if tc % 4 == 3:
                    group(tc // 4)
            S.barrier()
        if stop <= 13:
            return finish(nc, S)

        stG1.close()
        with contextlib.ExitStack() as stk:
            cst7 = S.buf("cst7")
            bm1 = sb("bm1", [128, 64], F32, stk)
            bm2 = sb("bm2", [128, 16], F32, stk)
            l2g = sb("l2g", [128, D], F32, stk)
            l2b = sb("l2b", [128, D], F32, stk)
            for dst, src in ((bm1, b_m1_pp), (bm2, b_m2_pp), (l2g, ln2g_bc), (l2b, ln2b_bc)):
                S.dma("sp", lambda e, dst=dst, src=src: e.dma_start(out=dst[:], in_=src), writes=[cst7])
            h2 = TP("h2", [128, 16, 512], BF16, 1, stk)
            aT = sb("aT", [128, 32, 512], BF16, stk)
            b_aT = S.bufs("aT", 32)
            ar = TP("ar", [128, 512], BF16, 3, stk)
            fTs = sb("fTs", [128, 16, 512], F32, stk)
            b_fT = S.bufs("fTs", 16)
            wm1 = TP("wm1", [128, 16, 256], BF16, 3, stk)
            fTb = sb("fTb", [128, 16, 512], BF16, stk)
            wm2 = TP("wm2", [128, 32, 128], BF16, 3, stk)
            x1in = TP("x1in", [128, D], F32, 2, stk)
            wk = TP("wk", [128, D], F32, 2, stk)
            s6 = TP("s6c", [128, 4, 6], F32, 2, stk)
            sm = TP("smc", [128, 8], F32, 2, stk)
            wl = []
            for tq in range(4):
                for hf_ in range(2):
                    for blk in range(16):
                        wl.append(("m1", hf_ * 16 + blk))
                    for fc in range(16):
                        wl.append(("m2", hf_, fc))
            cnt = {"m1": 0, "m2": 0}
            slot_of = []
            for ent in wl:
                slot_of.append(cnt[ent[0]])
                cnt[ent[0]] += 1
            issued = [0]

            def need(i, dist=2):
                while issued[0] <= min(i + dist, len(wl) - 1):
                    k = issued[0]
                    ent = wl[k]
                    if ent[0] == "m1":
                        t, b = wm1.get(slot_of[k])
                        dsc = wm1b_d[ent[1]]
                    else:
                        t, b = wm2.get(slot_of[k])
                        dsc = wm2b_d[ent[1] * 16 + ent[2]]
                    S.dma("pool", lambda e, t=t, dsc=dsc: e.dma_start(out=t[:], in_=dsc), writes=[b])
                    issued[0] += 1

            def load_x1(k):
                xt_, xb_ = x1in.get(k)
                S.dma("sp", lambda e, xt_=xt_, k=k: e.dma_start(out=xt_[:], in_=x1_d[k * 128:(k + 1) * 128, :]), writes=[xb_])

            st_ = {"wi": 0, "ai": 0}
            ht, hb_ = h2.get(0)

            def m1_half(tq, hf_):
                for blk in range(16):
                    need(st_["wi"])
                    wt_, wtb = wm1.get(slot_of[st_["wi"]])
                    st_["wi"] += 1
                    for fl in range(2):
                        fl_c = blk * 2 + fl
                        ffc = hf_ * 32 + fl_c
                        bi = fl_c % 2
                        for kc in range(16):
                            S.op("pe", lambda e, wt_=wt_, kc=kc, fl=fl, bi=bi: e.matmul(banks[bi][:, :], lhsT=wt_[:, kc, fl * 128:(fl + 1) * 128], rhs=ht[:, kc, :], start=(kc == 0), stop=(kc == 15)),
                                 reads=[wtb, hb_], writes=[bkb[bi]])
                        a, ab = ar.get(st_["ai"])
                        st_["ai"] += 1
                        S.op("act", lambda e, a=a, bi=bi, ffc=ffc: e.activation(out=a[:], in_=banks[bi][:, :], func=AF.Relu, bias=bm1[:, ffc:ffc + 1]), reads=[bkb[bi], cst7], writes=[ab])
                        S.op("dve", lambda e, a=a, fl_c=fl_c: e.tensor_tensor(out=aT[:, fl_c, :], in0=a[:], in1=a[:], op=ALU.mult), reads=[ab], writes=[b_aT[fl_c]])

            def m2_half(tq, hf_, hooks=None):
                for fc in range(16):
                    if hooks and fc % 3 == 1 and hooks:
                        hooks.pop(0)()
                    need(st_["wi"])
                    wt_, wtb = wm2.get(slot_of[st_["wi"]])
                    st_["wi"] += 1
                    bi = 2 + fc % 2
                    for kc in range(32):
                        S.op("pe", lambda e, wt_=wt_, kc=kc, bi=bi: e.matmul(banks[bi][:, :], lhsT=wt_[:, kc, :], rhs=aT[:, kc, :], start=(kc == 0), stop=(kc == 31)),
                             reads=[wtb, b_aT[kc]], writes=[bkb[bi]])
                    if hf_ == 0:
                        S.op("act", lambda e, fc=fc, bi=bi: e.activation(out=fTs[:, fc, :], in_=banks[bi][:, :], func=AF.Identity, bias=bm2[:, fc:fc + 1]), reads=[bkb[bi], cst7], writes=[b_fT[fc]])
                    else:
                        S.op("dve", lambda e, fc=fc, bi=bi: e.tensor_tensor(out=fTb[:, fc, :], in0=banks[bi][:, :], in1=fTs[:, fc, :], op=ALU.add), reads=[bkb[bi], b_fT[fc]], writes=[b_fT[fc]])

            def tail_A(tc):
                j = tc % 4
                xt_, xb_ = x1in.get(tc)
                for fc in range(16):
                    bi = 4 + fc // 4
                    S.op("pe", lambda e, fc=fc, j=j, bi=bi: e.transpose(out=bkbf(bi)[:, (fc % 4) * 128:(fc % 4 + 1) * 128], in_=fTb[:, fc, j * 128:(j + 1) * 128], identity=identb[:]),
                         reads=[b_fT[fc], b_cst], writes=[bkb[bi]])
                w, wb2 = wk.get(tc)
                for cb in range(4):
                    cs = slice(cb * 512, (cb + 1) * 512)
                    S.op("dve", lambda e, w=w, cb=cb, cs=cs: e.tensor_tensor(out=w[:, cs], in0=bkbf(4 + cb)[:, 0:512], in1=g2_bc[:, cs], op=ALU.mult), reads=[bkb[4 + cb], b_g2], writes=[wb2])
                S.op("dve", lambda e, xt_=xt_, w=w: e.scalar_tensor_tensor(out=w[:], in0=xt_[:], scalar=ALPHA, in1=w[:], op0=ALU.mult, op1=ALU.add), reads=[xb_, wb2], writes=[wb2])

            def tail_B(tc):
                w, wb2 = wk.get(tc)
                s6t, s6b_ = s6.get(tc)
                smt, smb = sm.get(tc)
                for cb in range(4):
                    S.op("dve", lambda e, s6t=s6t, w=w, cb=cb: e.bn_stats(out=s6t[:, cb, :], in_=w[:, cb * 512:(cb + 1) * 512]), reads=[wb2], writes=[s6b_])
                ln_tail(S, s6t, s6b_, smt, smb, eps_t, b_cst)
                S.op("act", lambda e, w=w, smt=smt: e.activation(out=w[:], in_=w[:], func=AF.Identity, scale=smt[:, 4:5], bias=smt[:, 5:6]), reads=[wb2, smb], writes=[wb2])
                S.op("dve", lambda e, w=w: e.tensor_tensor(out=w[:], in0=w[:], in1=l2g[:], op=ALU.mult), reads=[wb2, cst7], writes=[wb2])
                S.op("dve", lambda e, w=w: e.tensor_tensor(out=w[:], in0=w[:], in1=l2b[:], op=ALU.add), reads=[wb2, cst7], writes=[wb2])
                S.dma("sp", lambda e, w=w, tc=tc: e.dma_start(out=out[tc * 128:(tc + 1) * 128, :], in_=w[:]), reads=[wb2])

            def tail_steps(tq):
                base = tq * 4

                def s0():
                    tail_A(base)

                def mk(j):
                    def f():
                        if j + 1 < 4:
                            tail_A(base + j + 1)
                        tail_B(base + j)
                        if j + 2 < 4:
                            load_x1(base + j + 2)
                    return f

                return [s0] + [mk(j) for j in range(4)]

            for tq in range(4):
                S.dma("sp", lambda e, tq=tq: e.dma_start(out=ht[:], in_=h2T_d[:, :, tq * 512:(tq + 1) * 512].rearrange("f p t -> p f t")), writes=[hb_])
                hooks = None
                if tq > 0:
                    load_x1((tq - 1) * 4)
                    load_x1((tq - 1) * 4 + 1)
                    hooks = tail_steps(tq - 1)
                m1_half(tq, 0)
                m2_half(tq, 0, hooks)
                assert not hooks
                m1_half(tq, 1)
                m2_half(tq, 1)
            load_x1(12)
            load_x1(13)
            for f in tail_steps(3):
                f()
            S.barrier()
        return finish(nc, S)


def pipelined(n, load, compute, pf=1):
    for i in range(min(pf, n)):
        load(i)
    for i in range(n):
        if i + pf < n:
            load(i + pf)
        compute(i)


def ln_tail(S, s6, s6b, smt, smb, eps_t, b_cst):
    S.op("dve", lambda e: e.bn_aggr(out=smt[:, 0:2], in_=s6[:].rearrange("p a b -> p (a b)")), reads=[s6b], writes=[smb])
    S.op("dve", lambda e: e.tensor_scalar(out=smt[:, 2:3], in0=smt[:, 1:2], scalar1=EPS, scalar2=None, op0=ALU.add), reads=[smb], writes=[smb])
    S.op("pool", lambda e: e.tensor_tensor(out=smt[:, 4:5], in0=smt[:, 2:3], in1=eps_t[:, 1:2], op=ALU.pow), reads=[smb, b_cst], writes=[smb])
    S.op("dve", lambda e: e.tensor_scalar(out=smt[:, 5:6], in0=smt[:, 0:1], scalar1=-1.0, scalar2=smt[:, 4:5], op0=ALU.mult, op1=ALU.mult), reads=[smb], writes=[smb])


def finish(nc, S):
    S.barrier()
    with nc.Block() as block:
        S.emit(block)
    return nc


_CONST = {}


def _constants():
    if _CONST:
        return _CONST
    N, L = NFFT, S_LEN
    t = np.arange(L, dtype=np.int64)
    r = np.arange(N, dtype=np.int64)
    f = np.where(r < L, r, r - L)
    ang = 2.0 * np.pi * ((f[:, None] * t[None, :]) % N).astype(np.float64) / N
    cosm = np.cos(ang)
    sinm = np.sin(ang)
    is_re = (r < L)[:, None]
    nyq = (r == L)[:, None]
    alt = np.where(t % 2 == 0, 1.0, -1.0)[None, :]
    F = np.where(is_re, cosm, np.where(nyq, alt, -sinm))
    scale = np.where((f == 0), 1.0 / N, 2.0 / N)[:, None]
    G = F * scale
    FT = np.ascontiguousarray(F.T)

    def blkT(M):
        return np.ascontiguousarray(M.reshape(16, 128, 16, 256).transpose(2, 1, 0, 3)).astype(ml_dtypes.bfloat16)

    _CONST["FTb"] = blkT(FT)
    ftz = _CONST["FTb"][8].copy()
    ftz[:, :, 0] = 0
    _CONST["FTz"] = ftz
    ne = np.zeros((128, 16, 128), np.float32)
    ne[:, :, 0] = np.where((np.arange(16)[None, :] * 128 + np.arange(128)[:, None]) % 2 == 0, 1.0, -1.0)
    _CONST["nyqE"] = ne.astype(ml_dtypes.bfloat16)
    _CONST["GTb"] = np.ascontiguousarray(G.reshape(32, 128, 8, 256).transpose(2, 1, 0, 3)).astype(ml_dtypes.bfloat16)
    f32 = np.float32
    tt = np.linspace(0.0, 1.0, L, dtype=f32)[:, None]
    omega = (f32(2.0 * math.pi) * np.arange(L, dtype=f32)[:, None] / f32(L)).astype(f32)
    bands = np.linspace(1e-4, 15, 16, dtype=f32)[None, :]
    feats = np.concatenate([tt, np.cos(bands * omega), -np.sin(bands * omega)], axis=-1).astype(f32)
    _CONST["featsT"] = np.ascontiguousarray(feats.T)
    min_decay = math.log(1e-2) / 1.5
    max_decay = math.log(1e-2) / 0.3
    deltas = np.abs(np.linspace(min_decay, max_decay, HW, dtype=f32))
    _CONST["window"] = np.exp(-tt * deltas).astype(f32)
    _CONST["ident"] = np.eye(128, dtype=f32)
    _CONST["ones"] = np.ones((128, 128), dtype=f32)
    return _CONST


def _pp(v, n):
    return np.ascontiguousarray(np.asarray(v, np.float32).reshape(n, 128).T)


def _bc(v):
    v = np.asarray(v, np.float32)
    return np.ascontiguousarray(np.broadcast_to(v[None, :], (128, v.shape[0])))


def make_in_maps(inp):
    C = _constants()
    g = lambda k: np.asarray(inp[k], np.float32)
    b_in = g("b_in")[0]
    shared = {
        "w_ada": np.ascontiguousarray(g("w_ada")[0]),
        "w_in": np.ascontiguousarray(g("w_in")[0]),
        "b_in_pp": _pp(b_in, 72),
        "b_in_v_bc": _bc(b_in[HW:2 * HW]),
        "lng_bc": _bc(g("sgu_ln_g")[0]),
        "lnb_bc": _bc(g("sgu_ln_b")[0]),
        "wsT": np.ascontiguousarray(g("sgu_w")[0].transpose(2, 0, 1)),
        "bs_bc": _bc(g("sgu_b")[0].reshape(-1)),
        "cw_pp": np.ascontiguousarray(g("hy_conv_w")[0].reshape(3, 24, 128).transpose(2, 0, 1)),
        "cb_pp": _pp(g("hy_conv_b")[0], 24),
        "featsT": C["featsT"],
        "hw1": np.ascontiguousarray(g("hy_w1")[0]),
        "hb1_pp": np.ascontiguousarray(g("hy_b1")[0].reshape(64, 1)),
        "hfr_pp": np.ascontiguousarray(g("hy_freq")[0].reshape(64, 1)),
        "hw2": np.ascontiguousarray(g("hy_w2")[0]),
        "hb2_pp": np.ascontiguousarray(g("hy_b2")[0].reshape(64, 1)),
        "hw3": np.ascontiguousarray(g("hy_w3")[0]),
        "window": C["window"],
        "skip_bc": np.ascontiguousarray(np.broadcast_to(g("hy_skip")[0][None], (128, 2, HW))),
        "FTb": C["FTb"], "GTb": C["GTb"], "FTz": C["FTz"], "nyqE": C["nyqE"],
        "w_a": np.ascontiguousarray(g("w_branch_a")[0]),
        "w_b": np.ascontiguousarray(g("w_branch_b")[0]),
        "w_o": np.ascontiguousarray(g("w_o")[0]),
        "bo_bc": _bc(g("b_o")[0]),
        "ln1g_bc": _bc(g("ln1_g")[0]), "ln1b_bc": _bc(g("ln1_b")[0]),
        "ln1g_pp": _pp(g("ln1_g")[0], 16), "ln1b_pp": _pp(g("ln1_b")[0], 16),
        "w_m1": np.ascontiguousarray(g("w_m1")[0]),
        "b_m1_pp": _pp(g("b_m1")[0], 64),
        "w_m2": np.ascontiguousarray(g("w_m2")[0]),
        "b_m2_pp": _pp(g("b_m2")[0], 16),
        "ln2g_bc": _bc(g("ln2_g")[0]), "ln2b_bc": _bc(g("ln2_b")[0]),
        "ident": C["ident"], "ones": C["ones"],
    }
    acc0 = np.zeros((128, 6 * D), np.float32)
    acc0[0, :] = g("b_ada")[0]
    shared["acc0"] = acc0
    xs = g("x")
    cs = g("c")
    maps = []
    for b in range(NB):
        m = dict(shared)
        m["x"] = np.ascontiguousarray(xs[b])
        m["c_pp"] = _pp(cs[b], 16)
        maps.append(m)
    return maps


_PROG = {}


def kernel(**inputs):
    if "nc" not in _PROG:
        _PROG["nc"] = build_program()
    nc = _PROG["nc"]
    in_maps = make_in_maps(inputs)
    res = run_bass_kernel_spmd(nc, in_maps, core_ids=list(range(NB)))
    return np.stack([np.asarray(r["out"], np.float32) for r in res.results], axis=0)
```

```python
import contextlib
import math
import numpy as np
import ml_dtypes
import concourse.bass as bass
import concourse.mybir as mybir
from concourse.bass_utils import run_bass_kernel_spmd

F32 = mybir.dt.float32
BF16 = mybir.dt.bfloat16
AF = mybir.ActivationFunctionType
ALU = mybir.AluOpType
AX = mybir.AxisListType

D = 2048
S_LEN = 2048
NB = 8
HW = 1024
DFF = 8192
INW = 9216
NFFT = 4096
ALPHA = 2.0 ** 0.25
EPS = 1e-5
ENGS = ("pe", "act", "dve", "pool", "sp")


class Buf:
    __slots__ = ("name", "last_w", "readers")

    def __init__(self, name):
        self.name = name
        self.last_w = None
        self.readers = []


class Op:
    __slots__ = ("eng", "fn", "deps", "signal", "count", "is_dma", "sem_key", "dma_count", "waits", "wkey")

    def __init__(self, eng, fn, is_dma=False):
        self.eng = eng
        self.fn = fn
        self.deps = []
        self.signal = False
        self.count = 0
        self.is_dma = is_dma
        self.sem_key = None
        self.dma_count = 0
        self.waits = None
        self.wkey = None


class Sched:
    def __init__(self, nc, stack):
        self.nc = nc
        self.stack = stack
        self.ops = {e: [] for e in ENGS}
        self.esem = {e: stack.enter_context(nc.semaphore("s_" + e)) for e in ("pe", "act", "dve", "pool")}
        self.dsem = {}
        self.free_sems = {}
        self.nsem = 0
        self.all_bufs = []

    def buf(self, name):
        b = Buf(name)
        self.all_bufs.append(b)
        return b

    def bufs(self, name, n):
        return [self.buf(f"{name}{i}") for i in range(n)]

    def _track(self, op, reads, writes):
        deps = []
        for b in reads:
            if b.last_w is not None:
                deps.append(b.last_w)
        for b in writes:
            if b.last_w is not None:
                deps.append(b.last_w)
            deps.extend(b.readers)
        for b in reads:
            if not op.is_dma:
                b.readers = [r for r in b.readers if r.is_dma or r.eng != op.eng]
            b.readers.append(op)
        for b in writes:
            b.last_w = op
            b.readers = []
        seen = set()
        for d in deps:
            if id(d) in seen or d is op:
                continue
            seen.add(id(d))
            if (not d.is_dma) and (not op.is_dma) and d.eng == "pe" and op.eng == "pe":
                continue
            op.deps.append(d)
            if not d.is_dma:
                d.signal = True

    def op(self, eng, fn, reads=(), writes=()):
        o = Op(eng, fn)
        self._track(o, reads, writes)
        self.ops[eng].append(o)
        return o

    def dma(self, queue, fn, reads=(), writes=(), key=None, par=False):
        o = Op(queue, fn, is_dma=True)
        self._track(o, reads, writes)
        if par:
            wset = set(id(b) for b in writes)
            o.deps = [d for d in o.deps if not (d.is_dma and getattr(d, "wkey", None) == tuple(sorted(wset)))]
        o.wkey = tuple(sorted(id(b) for b in writes))
        if key is None:
            key = (list(writes) + list(reads))[0]
        kid = id(key)
        if kid not in self.dsem:
            fl = self.free_sems.setdefault(queue, [])
            if fl:
                sem, cnt = fl.pop()
            else:
                sem, cnt = self.stack.enter_context(self.nc.semaphore(f"d{self.nsem}")), 0
                self.nsem += 1
            self.dsem[kid] = [sem, cnt, queue]
        ent = self.dsem[kid]
        assert ent[2] == queue, "a DMA semaphore key must stay on one queue"
        ent[1] += 16
        o.sem_key = ent[0]
        o.dma_count = ent[1]
        self.ops[queue].append(o)
        return o

    def barrier(self):
        lasts = []
        for e in ("pe", "act", "dve", "pool"):
            for o in reversed(self.ops[e]):
                if not o.is_dma and o.fn is not None:
                    o.signal = True
                    lasts.append(o)
                    break
        dmas = [(ent[0], ent[1]) for ent in self.dsem.values() if ent[1] > 0]
        dmas += [(sem, cnt) for fl in self.free_sems.values() for sem, cnt in fl if cnt > 0]
        for e in ENGS:
            o = Op(e, None)
            o.deps = list(lasts)
            o.waits = dmas
            self.ops[e].append(o)
        for b in self.all_bufs:
            b.last_w = None
            b.readers = []
        for ent in self.dsem.values():
            self.free_sems.setdefault(ent[2], []).append((ent[0], ent[1]))
        self.dsem = {}

    def emit(self, block):
        for e in ("pe", "act", "dve", "pool"):
            c = 0
            for o in self.ops[e]:
                if o.is_dma or o.fn is None:
                    continue
                if o.signal:
                    c += 1
                    o.count = c
        handles = {"pe": block.tensor, "act": block.scalar, "dve": block.vector, "pool": block.gpsimd, "sp": block.sync}
        for e in ENGS:
            ops = self.ops[e]
            if not ops:
                continue

            def body(eng, ops=ops):
                waited = {}
                for o in ops:
                    need = {}
                    for d in o.deps:
                        if d.is_dma:
                            sem, val = d.sem_key, d.dma_count
                        else:
                            sem, val = self.esem[d.eng], d.count
                        k = id(sem)
                        if val > need.get(k, (None, 0))[1]:
                            need[k] = (sem, val)
                    if o.waits:
                        for sem, val in o.waits:
                            k = id(sem)
                            if val > need.get(k, (None, 0))[1]:
                                need[k] = (sem, val)
                    for k, (sem, val) in need.items():
                        if waited.get(k, 0) >= val:
                            continue
                        waited[k] = val
                        eng.wait_ge(sem, val)
                    if o.fn is None:
                        continue
                    ins = o.fn(eng)
                    if o.is_dma:
                        ins.then_inc(o.sem_key, 16)
                    elif o.signal:
                        ins.then_inc(self.esem[o.eng], 1)

            handles[e](body)


def build_program(stop=99, debug=False):
    nc = bass.Bass("TRN2", target_bir_lowering=False)

    def din(name, shape, dt=F32):
        return nc.dram_tensor(name, list(shape), dt, kind="ExternalInput").ap()

    def dscr(name, shape, dt):
        return nc.dram_tensor(name, list(shape), dt, kind="ExternalOutput" if debug else "Internal").ap()

    x = din("x", [S_LEN, D])
    c_pp = din("c_pp", [128, 16])
    w_ada = din("w_ada", [D, 6 * D])
    acc0 = din("acc0", [128, 6 * D])
    w_in = din("w_in", [D, INW])
    b_in_pp = din("b_in_pp", [128, 72])
    b_in_v_bc = din("b_in_v_bc", [128, HW])
    lng_bc = din("lng_bc", [128, HW])
    lnb_bc = din("lnb_bc", [128, HW])
    wsT = din("wsT", [128, 8, 128])
    bs_bc = din("bs_bc", [128, HW])
    cw_pp = din("cw_pp", [128, 3, 24])
    cb_pp = din("cb_pp", [128, 24])
    featsT = din("featsT", [33, S_LEN])
    hw1 = din("hw1", [33, 64])
    hb1_pp = din("hb1_pp", [64, 1])
    hfr_pp = din("hfr_pp", [64, 1])
    hw2 = din("hw2", [64, 64])
    hb2_pp = din("hb2_pp", [64, 1])
    hw3 = din("hw3", [64, 4096])
    window = din("window", [S_LEN, HW])
    skip_bc = din("skip_bc", [128, 2, HW])
    FTb = din("FTb", [16, 128, 16, 256], BF16)
    GTb = din("GTb", [8, 128, 32, 256], BF16)
    FTz = din("FTz", [128, 16, 256], BF16)
    nyqE = din("nyqE", [128, 16, 128], BF16)
    w_a = din("w_a", [HW, D])
    w_b = din("w_b", [HW, D])
    w_o = din("w_o", [D, D])
    bo_bc = din("bo_bc", [128, D])
    ln1g_bc = din("ln1g_bc", [128, D])
    ln1b_bc = din("ln1b_bc", [128, D])
    ln1g_pp = din("ln1g_pp", [128, 16])
    ln1b_pp = din("ln1b_pp", [128, 16])
    w_m1 = din("w_m1", [D, DFF])
    b_m1_pp = din("b_m1_pp", [128, 64])
    w_m2 = din("w_m2", [DFF, D])
    b_m2_pp = din("b_m2_pp", [128, 16])
    ln2g_bc = din("ln2g_bc", [128, D])
    ln2b_bc = din("ln2b_bc", [128, D])
    ident_d = din("ident", [128, 128])
    ones_d = din("ones", [128, 128])
    out = nc.dram_tensor("out", [S_LEN, D], F32, kind="ExternalOutput").ap()

    uT_d = dscr("uT_d", [8, 128, S_LEN], BF16)
    sguT_d = dscr("sguT_d", [8, 128, S_LEN], BF16)
    v_tm_d = dscr("v_tm_d", [S_LEN, HW], BF16)
    x1_tm_d = dscr("x1_tm_d", [S_LEN, HW], BF16)
    x2T_d = dscr("x2T_d", [8, 128, S_LEN], BF16)
    gT_d = dscr("gT_d", [32, 128, S_LEN], BF16)
    filt_d = dscr("filt_d", [S_LEN, 4096], BF16)
    K_d = dscr("K_d", [2, 32, 128, HW], F32)
    ybT_d = dscr("ybT_d", [8, 128, S_LEN], BF16)
    x1_d = dscr("x1_d", [S_LEN, D], F32)
    h2T_d = dscr("h2T_d", [16, 128, S_LEN], BF16)
    wm1b_d = nc.dram_tensor("wm1b_d", [32, 128, 16, 256], BF16).ap()
    wm2b_d = nc.dram_tensor("wm2b_d", [32, 128, 32, 128], BF16).ap()
    dbg_d = dscr("dbg_d", [128, 4 * 16 + 2 * D], F32) if debug else None
    hT_dbg = dscr("hT_dbg", [16, 128, S_LEN], BF16) if debug else None

    with contextlib.ExitStack() as st:
        S = Sched(nc, st)

        def sb(name, shape, dt, stack=st):
            return stack.enter_context(nc.sbuf_tensor("sb_" + name, list(shape), dt))

        banks = [st.enter_context(nc.psum_tensor(f"bank{i}", [128, 512], F32)) for i in range(8)]
        bkb = S.bufs("bank", 8)

        class TP:
            def __init__(self, name, shape, dt, n, stack):
                self.t = [sb(f"{name}_{i}", shape, dt, stack) for i in range(n)]
                self.b = S.bufs(name, n)
                self.n = n

            def get(self, i):
                return self.t[i % self.n], self.b[i % self.n]

        ident = sb("ident", [128, 128], F32)
        ones = sb("ones", [128, 128], F32)
        b_cst = S.buf("cst")
        S.dma("sp", lambda e: e.dma_start(out=ident[:], in_=ident_d), writes=[b_cst])
        S.dma("sp", lambda e: e.dma_start(out=ones[:], in_=ones_d), writes=[b_cst])
        binpp = sb("binpp", [128, 72], F32)
        S.dma("sp", lambda e: e.dma_start(out=binpp[:], in_=b_in_pp), writes=[b_cst])
        modpp = sb("modpp", [128, 4, 16], F32)
        b_modpp = S.buf("modpp")
        g2_bc = sb("g2_bc", [128, D], F32)
        eps_t = sb("eps_t", [128, 2], F32)
        identb = sb("identb", [128, 128], BF16)
        stG1 = contextlib.ExitStack()
        g1_bc = sb("g1_bc", [128, D], F32, stG1)
        b_g1 = S.buf("g1bc")
        b_g2 = S.buf("g2bc")
        S.op("dve", lambda e: e.memset(eps_t[:, 0:1], EPS), writes=[b_cst])
        S.op("dve", lambda e: e.memset(eps_t[:, 1:2], -0.5), writes=[b_cst])
        S.op("act", lambda e: e.activation(out=identb[:], in_=ident[:], func=AF.Identity), reads=[b_cst], writes=[b_cst])

        def bkbf(bi):
            return banks[bi][:, :].bitcast(BF16)

        W_in_v = w_in.rearrange("(kc p) n -> p kc n", p=128)

        stMod = contextlib.ExitStack()
        def mod_steps():
            acc = sb("acc", [128, 3 * D], F32, stMod)
            b_acc = S.buf("acc")
            cond = sb("cond", [128, 16], F32, stMod)
            cl = sb("cl", [128, 16], F32, stMod)
            b_cond = S.buf("cond")
            tbc = sb("tbc", [128, D], F32, stMod)
            b_tbc = S.buf("tbc")
            dg = sb("dg", [128, 16, 128], F32, stMod)
            b_dg = S.buf("dg")
            wt = TP("wada", [128, D], F32, 3, stMod)
            yield
            S.dma("pool", lambda e: e.dma_start(out=cl[:], in_=c_pp), writes=[b_cond])
            S.op("act", lambda e: e.activation(out=cond[:], in_=cl[:], func=AF.Silu), reads=[b_cond], writes=[b_cond])
            names = ["sh1", "sc1", "g1", "sh2", "sc2", "g2"]
            ppidx = {"sc1": 0, "sh1": 1, "sc2": 2, "sh2": 3}
            it = [0]
            for half in range(2):
                lo = half * 3 * D
                S.dma("pool", lambda e, lo=lo: e.dma_start(out=acc[:], in_=acc0[:, lo:lo + 3 * D]), writes=[b_acc])
                items = [(cb, kc) for cb in range(3) for kc in range(16)]

                def load(i, lo=lo, items=items):
                    cb, kc = items[i]
                    t, b = wt.get(it[0] + i)
                    S.dma("pool", lambda e, t=t, kc=kc, cb=cb: e.dma_start(out=t[:], in_=w_ada[kc * 128:(kc + 1) * 128, lo + cb * D: lo + (cb + 1) * D]), writes=[b])

                load(0)
                load(1)
                for i, (cb, kc) in enumerate(items):
                    if i + 2 < len(items):
                        load(i + 2)
                    t, b = wt.get(it[0] + i)
                    S.op("dve", lambda e, t=t, kc=kc, cb=cb: e.scalar_tensor_tensor(
                        out=acc[:, cb * D:(cb + 1) * D], in0=t[:], scalar=cond[:, kc:kc + 1], in1=acc[:, cb * D:(cb + 1) * D],
                        op0=ALU.mult, op1=ALU.add), reads=[b, b_cond, b_acc], writes=[b_acc])
                    yield
                it[0] += len(items)
                for _ in range(10):
                    yield
                for cb in range(3):
                    nm = names[half * 3 + cb]
                    dst, bd = {"g1": (g1_bc, b_g1), "g2": (g2_bc, b_g2)}.get(nm, (tbc, b_tbc))
                    for j in range(4):
                        bi = 6 + (j % 2)
                        S.op("pe", lambda e, bi=bi, cb=cb, j=j: e.matmul(banks[bi][:, :], lhsT=ones[:], rhs=acc[:, cb * D + j * 512: cb * D + (j + 1) * 512], start=True, stop=True),
                             reads=[b_acc, b_cst], writes=[bkb[bi]])
                        S.op("act", lambda e, dst=dst, j=j, bi=bi: e.activation(out=dst[:, j * 512:(j + 1) * 512], in_=banks[bi][:, :], func=AF.Identity),
                             reads=[bkb[bi]], writes=[bd])
                    if nm in ppidx:
                        idx = ppidx[nm]
                        for kc in range(16):
                            S.op("dve", lambda e, kc=kc: e.tensor_tensor(out=dg[:, kc, :], in0=tbc[:, kc * 128:(kc + 1) * 128], in1=ident[:], op=ALU.mult),
                                 reads=[b_tbc, b_cst], writes=[b_dg])
                        S.op("dve", lambda e, idx=idx: e.tensor_reduce(out=modpp[:, idx, :], in_=dg[:], axis=AX.X, op=ALU.add), reads=[b_dg], writes=[b_modpp])
                        if nm.startswith("sc"):
                            S.op("dve", lambda e, idx=idx: e.tensor_scalar(out=modpp[:, idx, :], in0=modpp[:, idx, :], scalar1=1.0, scalar2=None, op0=ALU.add),
                                 reads=[b_modpp], writes=[b_modpp])
                    yield

        mgen = mod_steps()

        def advance(k):
            for _ in range(k):
                try:
                    next(mgen)
                except StopIteration:
                    return False
            return True

        advance(1)
        with contextlib.ExitStack() as stk:
            cst4 = S.buf("cst4")
            fT = sb("fT", [33, S_LEN], F32, stk)
            w1t = sb("w1t", [33, 64], F32, stk)
            w2t = sb("w2t", [64, 64], F32, stk)
            w3f = sb("w3f", [64, 4096], F32, stk)
            w3b = sb("w3b", [64, 4096], BF16, stk)
            hv = sb("hv", [64, 8], F32, stk)
            for dst, src in ((fT[:], featsT), (w1t[:], hw1), (w2t[:], hw2), (w3f[:], hw3), (hv[:, 0:1], hb1_pp), (hv[:, 1:2], hfr_pp), (hv[:, 2:3], hb2_pp)):
                S.dma("sp", lambda e, dst=dst, src=src: e.dma_start(out=dst, in_=src), writes=[cst4])
            S.op("dve", lambda e: e.tensor_tensor(out=hv[:, 3:4], in0=hv[:, 0:1], in1=hv[:, 1:2], op=ALU.mult), reads=[cst4], writes=[cst4])
            S.op("dve", lambda e: e.tensor_tensor(out=hv[:, 4:5], in0=hv[:, 2:3], in1=hv[:, 1:2], op=ALU.mult), reads=[cst4], writes=[cst4])
            S.op("dve", lambda e: e.tensor_tensor(out=w3b[:, 0:2048], in0=w3f[:, 0:2048], in1=w3f[:, 2048:4096], op=ALU.add), reads=[cst4], writes=[cst4])
            S.op("dve", lambda e: e.tensor_tensor(out=w3b[:, 2048:4096], in0=w3f[:, 0:2048], in1=w3f[:, 2048:4096], op=ALU.subtract), reads=[cst4], writes=[cst4])
            arg = sb("arg", [64, S_LEN], F32, stk)
            wr1 = sb("wr1", [64, S_LEN], F32, stk)
            wr2 = sb("wr2", [64, S_LEN], F32, stk)
            h1 = sb("h1", [64, S_LEN], F32, stk)
            h2b = sb("h2b", [64, S_LEN], BF16, stk)
            b_arg = S.buf("arg")
            b_h1 = S.buf("h1")
            b_h2 = S.buf("h2")
            TWO_PI = 2.0 * math.pi

            def sin_layer(lhs, kdim, rhs_t, rhs_b, bias_col, dst, dst_b):
                for tt in range(4):
                    S.op("pe", lambda e, tt=tt: e.matmul(banks[tt][0:64, :], lhsT=lhs[0:kdim, :], rhs=rhs_t[0:kdim, tt * 512:(tt + 1) * 512], start=True, stop=True),
                         reads=[cst4, rhs_b], writes=[bkb[tt]])
                    S.op("act", lambda e, tt=tt: e.activation(out=arg[:, tt * 512:(tt + 1) * 512], in_=banks[tt][0:64, :], func=AF.Identity, scale=hv[:, 1:2], bias=hv[:, bias_col:bias_col + 1]),
                         reads=[bkb[tt], cst4], writes=[b_arg])
                S.op("dve", lambda e: e.tensor_scalar(out=wr1[:], in0=arg[:], scalar1=math.pi, scalar2=-TWO_PI, op0=ALU.is_gt, op1=ALU.mult), reads=[b_arg], writes=[b_arg])
                S.op("dve", lambda e: e.tensor_scalar(out=wr2[:], in0=arg[:], scalar1=-math.pi, scalar2=TWO_PI, op0=ALU.is_lt, op1=ALU.mult), reads=[b_arg], writes=[b_arg])
                S.op("dve", lambda e: e.tensor_tensor(out=wr1[:], in0=wr1[:], in1=wr2[:], op=ALU.add), reads=[b_arg], writes=[b_arg])
                S.op("dve", lambda e: e.tensor_tensor(out=arg[:], in0=arg[:], in1=wr1[:], op=ALU.add), reads=[b_arg], writes=[b_arg])
                S.op("act", lambda e: e.activation(out=dst[:], in_=arg[:], func=AF.Sin), reads=[b_arg], writes=[dst_b])

            sin_layer(w1t, 33, fT, cst4, 3, h1, b_h1)
            sin_layer(w2t, 64, h1, b_h1, 4, h2b, b_h2)
            wint = TP("wint", [128, HW], F32, 2, stk)
            fo = TP("fo", [128, 4096], BF16, 2, stk)

            def load_win(tc):
                wt_, wb_ = wint.get(tc)
                S.dma("sp", lambda e, wt_=wt_, tc=tc: e.dma_start(out=wt_[:], in_=window[tc * 128:(tc + 1) * 128, :]), writes=[wb_])

            def comp_filt(tc):
                wt_, wb_ = wint.get(tc)
                f, fb = fo.get(tc)
                for blk in range(8):
                    bi = blk
                    S.op("pe", lambda e, tc=tc, blk=blk, bi=bi: e.matmul(banks[bi][:, :], lhsT=h2b[:, tc * 128:(tc + 1) * 128], rhs=w3b[:, blk * 512:(blk + 1) * 512], start=True, stop=True),
                         reads=[b_h2, cst4], writes=[bkb[bi]])
                    S.op("dve", lambda e, f=f, wt_=wt_, blk=blk, bi=bi: e.tensor_tensor(out=f[:, blk * 512:(blk + 1) * 512], in0=banks[bi][:, :], in1=wt_[:, (blk % 2) * 512:(blk % 2 + 1) * 512], op=ALU.mult),
                         reads=[bkb[bi], wb_], writes=[fb])
                S.dma("sp", lambda e, f=f, tc=tc: e.dma_start(out=filt_d[tc * 128:(tc + 1) * 128, :], in_=f[:]), reads=[fb])

            pipelined(16, load_win, comp_filt, pf=1)
            S.barrier()

        with contextlib.ExitStack() as stk:
            skb = sb("skb", [128, 2, HW], F32, stk)
            cst5 = S.buf("cst5")
            S.dma("sp", lambda e: e.dma_start(out=skb[:], in_=skip_bc), writes=[cst5])
            nyq = sb("nyq", [128, 16, 128], BF16, stk)
            S.dma("sp", lambda e: e.dma_start(out=nyq[:], in_=nyqE), writes=[cst5])
            hs = sb("hs", [128, 16, HW], BF16, stk)
            hd = sb("hd", [128, 16, HW], BF16, stk)
            b_hs = S.buf("hs")
            b_hd = S.buf("hd")
            ftA = TP("ftA", [128, 16, 256], BF16, 2, stk)
            ko = TP("ko", [128, HW], F32, 3, stk)
            fv = filt_d.rearrange("(tc p) c -> p tc c", p=128)
            items = [(n, j2) for n in range(2) for j2 in range(16)]

            def load_ft(i):
                n, j2 = items[i]
                a, ab = ftA.get(i)
                src = FTz if j2 == 8 else FTb[j2]
                S.dma("sp", lambda e, a=a, src=src: e.dma_start(out=a[:], in_=src), writes=[ab])

            def comp_k(i):
                n, j2 = items[i]
                if n == 0 and j2 == 0:
                    S.dma("sp", lambda e: e.dma_start(out=hs[:], in_=fv[:, :, 0:HW]), writes=[b_hs])
                    S.dma("sp", lambda e: e.dma_start(out=hd[:], in_=fv[:, :, 2048:2048 + HW]), writes=[b_hd])
                if n == 0 and j2 == 9:
                    S.dma("sp", lambda e: e.dma_start(out=hs[:], in_=fv[:, :, HW:2 * HW]), writes=[b_hs])
                if n == 1 and j2 == 0:
                    S.dma("sp", lambda e: e.dma_start(out=hd[:], in_=fv[:, :, 2048 + HW:2048 + 2 * HW]), writes=[b_hd])
                a, ab = ftA.get(i)
                for fl in range(2):
                    fch = j2 * 2 + fl
                    src = hs if fch < 16 else hd
                    pr = (i * 2 + fl) % 3
                    bset = [pr * 2, pr * 2 + 1]
                    for hh in range(2):
                        bi = bset[hh]
                        for tc in range(16):
                            S.op("pe", lambda e, a=a, tc=tc, fl=fl, hh=hh, bi=bi, src=src, fch=fch: e.matmul(banks[bi][:, :], lhsT=a[:, tc, fl * 128:(fl + 1) * 128], rhs=src[:, tc, hh * 512:(hh + 1) * 512], start=(tc == 0), stop=(tc == 15 and fch != 16)),
                                 reads=[ab, b_hs if fch < 16 else b_hd], writes=[bkb[bi]])
                        if fch == 16:
                            for tc in range(16):
                                S.op("pe", lambda e, tc=tc, hh=hh, bi=bi: e.matmul(banks[bi][:, :], lhsT=nyq[:, tc, :], rhs=hs[:, tc, hh * 512:(hh + 1) * 512], start=False, stop=(tc == 15)),
                                     reads=[cst5, b_hs], writes=[bkb[bi]])
                    k, kb = ko.get(i * 2 + fl)
                    for hh in range(2):
                        bi = bset[hh]
                        if fch < 16:
                            S.op("dve", lambda e, k=k, hh=hh, bi=bi, n=n: e.tensor_tensor(out=k[:, hh * 512:(hh + 1) * 512], in0=banks[bi][:, :], in1=skb[:, n, hh * 512:(hh + 1) * 512], op=ALU.add),
                                 reads=[bkb[bi], cst5], writes=[kb])
                        else:
                            S.op("act", lambda e, k=k, hh=hh, bi=bi: e.activation(out=k[:, hh * 512:(hh + 1) * 512], in_=banks[bi][:, :], func=AF.Identity),
                                 reads=[bkb[bi]], writes=[kb])
                    if fch == 16:
                        S.op("dve", lambda e, k=k, n=n: e.tensor_tensor(out=k[0:1, :], in0=k[0:1, :], in1=skb[0:1, n, :], op=ALU.add), reads=[kb, cst5], writes=[kb])
                    S.dma("sp", lambda e, k=k, n=n, fch=fch: e.dma_start(out=K_d[n, fch], in_=k[:]), reads=[kb])
                advance(4)

            pipelined(len(items), load_ft, comp_k, pf=1)
            while advance(8):
                pass
            S.barrier()
        stMod.close()
        if stop <= 0:
            return finish(nc, S)

        with contextlib.ExitStack() as stA:
            hT = sb("hT", [128, 16, S_LEN], BF16, stA)
            b_hT = [[S.buf(f"hT{fc}_{tt}") for tt in range(4)] for fc in range(16)]
            with contextlib.ExitStack() as stk:
                xt = TP("xt", [128, 4, D], BF16, 2, stk)
                xv = x.rearrange("(g j p) d -> g p j d", j=4, p=128)

                def load_x(tg):
                    t, b = xt.get(tg)
                    for j in range(4):
                        S.dma("pool", lambda e, t=t, tg=tg, j=j: e.dma_start(out=t[:, j, :], in_=xv[tg][:, j, :]), writes=[b], par=(j > 0))

                def comp_x(tg):
                    t, b = xt.get(tg)
                    for fc in range(16):
                        bi = fc % 8
                        for j in range(4):
                            S.op("pe", lambda e, t=t, j=j, fc=fc, bi=bi: e.transpose(
                                out=bkbf(bi)[:, j * 128:(j + 1) * 128], in_=t[:, j, fc * 128:(fc + 1) * 128], identity=identb[:]),
                                reads=[b, b_cst], writes=[bkb[bi]])
                        S.op("act", lambda e, fc=fc, tg=tg, bi=bi: e.activation(
                            out=hT[:, fc, tg * 512:(tg + 1) * 512], in_=bkbf(bi)[:, 0:512], func=AF.Identity,
                            scale=modpp[:, 0, fc:fc + 1], bias=modpp[:, 1, fc:fc + 1]),
                            reads=[bkb[bi], b_modpp], writes=[b_hT[fc][tg]])

                pipelined(4, load_x, comp_x, pf=1)
                S.barrier()
            if debug:
                bdbg = S.buf("dbg")
                S.dma("sp", lambda e: e.dma_start(out=dbg_d[:, 0:64], in_=modpp[:].rearrange("p a b -> p (a b)")), reads=[b_modpp], key=bdbg)
                S.dma("sp", lambda e: e.dma_start(out=dbg_d[:, 64:64 + D], in_=g1_bc[:]), reads=[b_g1], key=bdbg)
                S.dma("sp", lambda e: e.dma_start(out=dbg_d[:, 64 + D:64 + 2 * D], in_=g2_bc[:]), reads=[b_g2], key=bdbg)
                S.dma("sp", lambda e: e.dma_start(out=hT_dbg.rearrange("f p t -> p f t"), in_=hT[:]), reads=[b for row in b_hT for b in row], key=bdbg)
                S.barrier()
            if stop <= 1:
                return finish(nc, S)

            def gemm_ws(Wv, cols, n_kc, rhs, n_tt, N, epilogue, stk, nm, blk=512, nslots=3):
                wt = TP("w" + nm, [128, n_kc, blk], BF16, nslots, stk)
                blocks = []
                for c in cols:
                    if blocks and blocks[-1][0] + blk > c >= blocks[-1][0] and c == blocks[-1][1][-1] + 128:
                        blocks[-1][1].append(c)
                    else:
                        blocks.append((c, [c]))

                def load(i):
                    c0, cl = blocks[i]
                    t, b = wt.get(i)
                    w = len(cl) * 128
                    S.dma("pool", lambda e, t=t, c0=c0, w=w: e.dma_start(out=t[:, :, 0:w], in_=Wv[:, :, c0:c0 + w]), writes=[b])

                load(0)
                if len(blocks) > 1:
                    load(1)
                ci = 0
                pending = [None]
                for i, (c0, cl) in enumerate(blocks):
                    if i + 2 < len(blocks):
                        load(i + 2)
                    t, b = wt.get(i)
                    for c in cl:
                        bset = [(ci % (8 // n_tt)) * n_tt + tt for tt in range(n_tt)]
                        for kc in range(n_kc):
                            for tt in range(n_tt):
                                r_ap, r_b = rhs(kc, tt)
                                S.op("pe", lambda e, t=t, kc=kc, o=c - c0, bi=bset[tt], r_ap=r_ap: e.matmul(
                                    banks[bi][:, 0:N], lhsT=t[:, kc, o:o + 128], rhs=r_ap, start=(kc == 0), stop=(kc == n_kc - 1)),
                                    reads=[b, r_b], writes=[bkb[bset[tt]]])
                        if pending[0] is not None:
                            pending[0]()
                        pending[0] = epilogue(ci, c, bset)
                        ci += 1
                if pending[0] is not None:
                    pending[0]()

            def rhs_hT(kc, tt):
                return hT[:, kc, tt * 512:(tt + 1) * 512], b_hT[kc][tt]

            with contextlib.ExitStack() as stk:
                ut = TP("ut", [128, S_LEN], BF16, 2, stk)

                def ep_u(ci, c, bset):
                    t, b = ut.get(ci)
                    for tt in range(4):
                        S.op("act", lambda e, t=t, tt=tt, bi=bset[tt], ch=c // 128: e.activation(
                            out=t[:, tt * 512:(tt + 1) * 512], in_=banks[bi][:, :], func=AF.Gelu, bias=binpp[:, ch:ch + 1]),
                            reads=[bkb[bset[tt]], b_cst], writes=[b])
                    S.dma("sp", lambda e, t=t, ci=ci: e.dma_start(out=uT_d[ci], in_=t[:]), reads=[b])

                gemm_ws(W_in_v, [i * 128 for i in range(8)], 16, rhs_hT, 4, 512, ep_u, stk, "u")
                S.barrier()
            if stop <= 2:
                return finish(nc, S)

            with contextlib.ExitStack() as stk:
                Wv_t = sb("Wv_t", [128, 16, HW], BF16, stk)
                b_Wv = S.buf("Wv")
                for hh in range(2):
                    S.dma("pool", lambda e, hh=hh: e.dma_start(out=Wv_t[:, :, hh * 512:(hh + 1) * 512], in_=W_in_v[:, :, HW + hh * 512: HW + (hh + 1) * 512]),
                          writes=[b_Wv], par=(hh > 0))
                cst2 = S.buf("cst2")
                binv = sb("binv", [128, HW], F32, stk)
                lng = sb("lng", [128, HW], F32, stk)
                lnb = sb("lnb", [128, HW], F32, stk)
                bsb = sb("bsb", [128, HW], F32, stk)
                wsf = sb("wsf", [128, 8, 128], F32, stk)
                wsb = sb("wsb", [128, 8, 128], BF16, stk)
                for dst, src in ((binv, b_in_v_bc), (lng, lng_bc), (lnb, lnb_bc), (bsb, bs_bc), (wsf, wsT)):
                    S.dma("sp", lambda e, dst=dst, src=src: e.dma_start(out=dst[:], in_=src), writes=[cst2])
                S.op("act", lambda e: e.activation(out=wsb[:], in_=wsf[:], func=AF.Identity), reads=[cst2], writes=[cst2])
                zv = TP("zv", [128, HW], F32, 3, stk)
                gv = TP("gv", [128, HW], F32, 2, stk)
                st6 = TP("st6", [128, 2, 6], F32, 2, stk)
                sm = TP("sm", [128, 8], F32, 2, stk)
                vn0 = TP("vn0", [128, HW], F32, 2, stk)
                vn = TP("vn", [128, HW], BF16, 2, stk)
                utile = TP("utile", [128, 8, 128], BF16, 5, stk)
                sgt = TP("sgt", [128, 8, 128], F32, 2, stk)
                sgo = TP("sgo", [128, 8, 128], BF16, 2, stk)
                def load_u(t_):
                    ut_, ub_ = utile.get(t_)
                    S.dma("sp", lambda e, ut_=ut_, t_=t_: e.dma_start(out=ut_[:], in_=uT_d[:, :, t_ * 128:(t_ + 1) * 128].rearrange("g p t -> p g t")), writes=[ub_])

                def zmm(t_):
                    bz = [(t_ % 3) * 2, (t_ % 3) * 2 + 1]
                    for hh in range(2):
                        for kc in range(16):
                            S.op("pe", lambda e, hh=hh, kc=kc, bi=bz[hh], t_=t_: e.matmul(
                                banks[bi][:, :], lhsT=hT[:, kc, t_ * 128:(t_ + 1) * 128], rhs=Wv_t[:, kc, hh * 512:(hh + 1) * 512],
                                start=(kc == 0), stop=(kc == 15)), reads=[b_hT[kc][t_ // 4], b_Wv], writes=[bkb[bz[hh]]])

                def chain(t_):
                    bz = [(t_ % 3) * 2, (t_ % 3) * 2 + 1]
                    zt, zb = zv.get(t_)
                    gt_, gb_ = gv.get(t_)
                    s6, s6b = st6.get(t_)
                    smt, smb = sm.get(t_)
                    v0, v0b = vn0.get(t_)
                    vt, vb = vn.get(t_)
                    for hh in range(2):
                        S.op("dve", lambda e, hh=hh, zt=zt, bi=bz[hh]: e.tensor_tensor(out=zt[:, hh * 512:(hh + 1) * 512], in0=banks[bi][:, :], in1=binv[:, hh * 512:(hh + 1) * 512], op=ALU.add),
                             reads=[bkb[bz[hh]], cst2], writes=[zb])
                    S.op("act", lambda e, zt=zt, gt_=gt_: e.activation(out=gt_[:], in_=zt[:], func=AF.Gelu), reads=[zb], writes=[gb_])
                    yield
                    for hh in range(2):
                        S.op("dve", lambda e, hh=hh, s6=s6, gt_=gt_: e.bn_stats(out=s6[:, hh, :], in_=gt_[:, hh * 512:(hh + 1) * 512]), reads=[gb_], writes=[s6b])
                    S.op("dve", lambda e: e.bn_aggr(out=smt[:, 0:2], in_=s6[:].rearrange("p a b -> p (a b)")), reads=[s6b], writes=[smb])
                    S.op("dve", lambda e: e.tensor_scalar(out=smt[:, 2:3], in0=smt[:, 1:2], scalar1=EPS, scalar2=None, op0=ALU.add), reads=[smb], writes=[smb])
                    S.op("pool", lambda e: e.tensor_tensor(out=smt[:, 4:5], in0=smt[:, 2:3], in1=eps_t[:, 1:2], op=ALU.pow), reads=[smb, b_cst], writes=[smb])
                    yield
                    S.op("dve", lambda e: e.tensor_scalar(out=smt[:, 5:6], in0=smt[:, 0:1], scalar1=-1.0, scalar2=smt[:, 4:5], op0=ALU.mult, op1=ALU.mult), reads=[smb], writes=[smb])
                    S.op("act", lambda e, gt_=gt_, v0=v0, smt=smt: e.activation(out=v0[:], in_=gt_[:], func=AF.Identity, scale=smt[:, 4:5], bias=smt[:, 5:6]),
                         reads=[gb_, smb], writes=[v0b])
                    yield
                    S.op("dve", lambda e, v0=v0: e.tensor_tensor(out=v0[:], in0=v0[:], in1=lng[:], op=ALU.mult), reads=[v0b, cst2], writes=[v0b])
                    S.op("pool", lambda e, v0=v0, vt=vt: e.tensor_tensor(out=vt[:], in0=v0[:], in1=lnb[:], op=ALU.add), reads=[v0b, cst2], writes=[vb])
                    yield

                def sgu_tail(t_):
                    ut_, ub_ = utile.get(t_)
                    vt, vb = vn.get(t_)
                    bs_ = [6, 7]
                    for g in range(8):
                        bi = bs_[g // 4]
                        S.op("pe", lambda e, g=g, bi=bi, vt=vt: e.matmul(banks[bi][:, (g % 4) * 128:(g % 4 + 1) * 128], lhsT=vt[:, g * 128:(g + 1) * 128], rhs=wsb[:, g, :], start=True, stop=True),
                             reads=[vb, cst2], writes=[bkb[bi]])
                    sg, sgb = sgt.get(t_)
                    so, sob = sgo.get(t_)
                    for hh in range(2):
                        S.op("dve", lambda e, hh=hh, sg=sg, bi=bs_[hh]: e.tensor_tensor(
                            out=sg[:, hh * 4:(hh + 1) * 4, :].rearrange("p a b -> p (a b)"), in0=banks[bi][:, :], in1=bsb[:, hh * 512:(hh + 1) * 512], op=ALU.add),
                            reads=[bkb[bs_[hh]], cst2], writes=[sgb])
                    S.op("dve", lambda e, sg=sg, so=so, ut_=ut_: e.tensor_tensor(out=so[:], in0=sg[:], in1=ut_[:], op=ALU.mult), reads=[sgb, ub_], writes=[sob])
                    S.dma("sp", lambda e, so=so, t_=t_: e.dma_start(out=sguT_d[:, :, t_ * 128:(t_ + 1) * 128].rearrange("g p t -> p g t"), in_=so[:]), reads=[sob])

                load_u(0)
                load_u(1)
                zmm(0)
                zmm(1)
                for p in range(8):
                    t0_ = 2 * p
                    if t0_ + 2 < 16:
                        load_u(t0_ + 2)
                        load_u(t0_ + 3)
                    for _ in zip(chain(t0_), chain(t0_ + 1)):
                        pass
                    if t0_ + 2 < 16:
                        zmm(t0_ + 2)
                        zmm(t0_ + 3)
                    sgu_tail(t0_)
                    sgu_tail(t0_ + 1)
                S.barrier()
            if stop <= 3:
                return finish(nc, S)

            with contextlib.ExitStack() as stk:
                cwt = sb("cwt", [128, 3, 24], F32, stk)
                cbt = sb("cbt", [128, 24], F32, stk)
                cst3 = S.buf("cst3")
                S.dma("sp", lambda e: e.dma_start(out=cwt[:], in_=cw_pp), writes=[cst3])
                S.dma("sp", lambda e: e.dma_start(out=cbt[:], in_=cb_pp), writes=[cst3])
                zp = TP("zp", [128, S_LEN + 2], F32, 2, stk)
                for i in range(2):
                    S.op("dve", lambda e, i=i: e.memset(zp.t[i][:], 0.0), writes=[zp.b[i]])
                cvA = TP("cvA", [128, S_LEN], F32, 1, stk)
                cvB = TP("cvB", [128, S_LEN], F32, 1, stk)
                cvC = TP("cvC", [128, S_LEN], BF16, 2, stk)
                cvO = TP("cvO", [128, S_LEN], BF16, 1, stk)
                tmo = TP("tmo", [128, 16, 128], BF16, 2, stk)

                def ep_h(ci, c, bset):
                    ch = c // 128
                    hc = ch - 16
                    sel = hc // 8
                    cc = hc % 8
                    z, zb = zp.get(ci)
                    for tt in range(4):
                        S.op("act", lambda e, z=z, tt=tt, bi=bset[tt], ch=ch: e.activation(
                            out=z[:, 1 + tt * 512: 1 + (tt + 1) * 512], in_=banks[bi][:, :], func=AF.Identity, bias=binpp[:, ch:ch + 1]),
                            reads=[bkb[bset[tt]], b_cst], writes=[zb])
                    a, ab = cvA.get(ci)
                    bb_, bbb = cvB.get(ci)
                    S.op("act", lambda e, z=z, a=a, hc=hc: e.activation(out=a[:], in_=z[:, 1:S_LEN + 1], func=AF.Identity, scale=cwt[:, 1, hc:hc + 1], bias=cbt[:, hc:hc + 1]),
                         reads=[zb, cst3], writes=[ab])
                    S.op("dve", lambda e, z=z, a=a, bb_=bb_, hc=hc: e.scalar_tensor_tensor(out=bb_[:], in0=z[:, 0:S_LEN], scalar=cwt[:, 0, hc:hc + 1], in1=a[:], op0=ALU.mult, op1=ALU.add),
                         reads=[zb, ab, cst3], writes=[bbb])
                    if sel == 2:
                        o, ob = cvO.get(ci)
                        S.op("dve", lambda e, z=z, bb_=bb_, o=o, hc=hc: e.scalar_tensor_tensor(out=o[:], in0=z[:, 2:S_LEN + 2], scalar=cwt[:, 2, hc:hc + 1], in1=bb_[:], op0=ALU.mult, op1=ALU.add),
                             reads=[zb, bbb, cst3], writes=[ob])
                        S.dma("sp", lambda e, o=o, cc=cc: e.dma_start(out=x2T_d[cc], in_=o[:]), reads=[ob])
                    else:
                        cv, cvb = cvC.get(ci)
                        S.op("dve", lambda e, z=z, bb_=bb_, cv=cv, hc=hc: e.scalar_tensor_tensor(out=cv[:], in0=z[:, 2:S_LEN + 2], scalar=cwt[:, 2, hc:hc + 1], in1=bb_[:], op0=ALU.mult, op1=ALU.add),
                             reads=[zb, bbb, cst3], writes=[cvb])
                        tm, tmb = tmo.get(ci)

                        def later(cv=cv, cvb=cvb, tm=tm, tmb=tmb, bset=bset, sel=sel, cc=cc):
                            for j4 in range(4):
                                bi = bset[j4]
                                for jj in range(4):
                                    j = j4 * 4 + jj
                                    S.op("pe", lambda e, cv=cv, j=j, jj=jj, bi=bi: e.transpose(out=bkbf(bi)[:, jj * 128:(jj + 1) * 128], in_=cv[:, j * 128:(j + 1) * 128], identity=identb[:]),
                                         reads=[cvb, b_cst], writes=[bkb[bi]])
                                S.op("dve", lambda e, tm=tm, j4=j4, bi=bi: e.tensor_copy(out=tm[:, j4 * 4:(j4 + 1) * 4, :].rearrange("p a b -> p (a b)"), in_=bkbf(bi)[:, 0:512]),
                                     reads=[bkb[bi]], writes=[tmb])
                            dst = v_tm_d if sel == 0 else x1_tm_d
                            S.dma("sp", lambda e, tm=tm, dst=dst, cc=cc: e.dma_start(out=dst.rearrange("(j p) c -> p j c", p=128)[:, :, cc * 128:(cc + 1) * 128], in_=tm[:]), reads=[tmb])

                        return later
                        for j4 in range(4):
                            bi = bset[j4]
                            for jj in range(4):
                                j = j4 * 4 + jj
                                S.op("pe", lambda e, cv=cv, j=j, jj=jj, bi=bi: e.transpose(out=bkbf(bi)[:, jj * 128:(jj + 1) * 128], in_=cv[:, j * 128:(j + 1) * 128], identity=identb[:]),
                                     reads=[cvb, b_cst], writes=[bkb[bi]])
                            S.op("dve", lambda e, tm=tm, j4=j4, bi=bi: e.tensor_copy(out=tm[:, j4 * 4:(j4 + 1) * 4, :].rearrange("p a b -> p (a b)"), in_=bkbf(bi)[:, 0:512]),
                                 reads=[bkb[bi]], writes=[tmb])
                        dst = v_tm_d if sel == 0 else x1_tm_d
                        S.dma("sp", lambda e, tm=tm, dst=dst, cc=cc: e.dma_start(out=dst.rearrange("(j p) c -> p j c", p=128)[:, :, cc * 128:(cc + 1) * 128], in_=tm[:]), reads=[tmb])

                gemm_ws(W_in_v, [2048 + i * 128 for i in range(24)], 16, rhs_hT, 4, 512, ep_h, stk, "h")
                S.barrier()
            if stop <= 4:
                return finish(nc, S)

            with contextlib.ExitStack() as stk:
                gt2 = TP("gt2", [128, S_LEN], BF16, 2, stk)

                def ep_g(ci, c, bset):
                    t, b = gt2.get(ci)
                    for tt in range(4):
                        S.op("act", lambda e, t=t, tt=tt, bi=bset[tt], ch=c // 128: e.activation(
                            out=t[:, tt * 512:(tt + 1) * 512], in_=banks[bi][:, :], func=AF.Sigmoid, bias=binpp[:, ch:ch + 1]),
                            reads=[bkb[bset[tt]], b_cst], writes=[b])
                    S.dma("sp", lambda e, t=t, ci=ci: e.dma_start(out=gT_d[ci], in_=t[:]), reads=[b])

                gemm_ws(W_in_v, [5120 + i * 128 for i in range(32)], 16, rhs_hT, 4, 512, ep_g, stk, "g")
                S.barrier()
        if stop <= 5:
            return finish(nc, S)

        Wm1_v = w_m1.rearrange("(kc p) n -> p kc n", p=128)
        Wm2_v = w_m2.rearrange("(kc p) n -> p kc n", p=128)
        b_wconv = S.bufs("wconv", 2)

        wc_next = [0]

        def wconv_step(n):
            lo = wc_next[0]
            hi = min(64, lo + n)
            wc_next[0] = hi
            for k in range(lo, hi):
                if k < 32:
                    S.dma("pool", lambda e, k=k: e.dma_start(out=wm1b_d[k], in_=Wm1_v[:, :, k * 256:(k + 1) * 256]), writes=[b_wconv[k % 2]])
                else:
                    j = k - 32
                    S.dma("pool", lambda e, j=j: e.dma_start(out=wm2b_d[j], in_=Wm2_v[:, (j // 16) * 32:(j // 16 + 1) * 32, (j % 16) * 128:(j % 16 + 1) * 128]), writes=[b_wconv[k % 2]])

        with contextlib.ExitStack() as stC:
            Y = sb("Y", [128, 32, HW], BF16, stC)
            b_Y = [S.buf(f"Y{j}") for j in range(16)]
            utm = sb("utm", [128, 16, HW], BF16, stC)
            b_utm = [S.buf(f"utm{j}") for j in range(16)]

            def forward(n, stk):
                ftA = TP(f"cA{n}", [128, 16, 256], BF16, 2, stk)
                ftB = TP(f"cB{n}", [128, 16, 256], BF16, 2, stk)
                kre = TP(f"kre{n}", [128, 2, HW], F32, 2, stk)
                kim = TP(f"kim{n}", [128, 2, HW], F32, 2, stk)
                tmp = TP(f"ctmp{n}", [128, 4, 512], F32, 2, stk)

                def load(j2):
                    a, ab = ftA.get(j2)
                    b2, bb2 = ftB.get(j2)
                    kr, krb = kre.get(j2)
                    ki, kib = kim.get(j2)
                    S.dma("sp", lambda e, a=a, j2=j2: e.dma_start(out=a[:], in_=FTb[j2]), writes=[ab])
                    S.dma("sp", lambda e, b2=b2, j2=j2: e.dma_start(out=b2[:], in_=FTb[8 + j2]), writes=[bb2])
                    S.dma("sp", lambda e, kr=kr, j2=j2, n=n: e.dma_start(out=kr[:], in_=K_d[n, 2 * j2:2 * j2 + 2].rearrange("a p c -> p a c")), writes=[krb])
                    S.dma("sp", lambda e, ki=ki, j2=j2, n=n: e.dma_start(out=ki[:], in_=K_d[n, 16 + 2 * j2:16 + 2 * j2 + 2].rearrange("a p c -> p a c")), writes=[kib])

                def comp(j2):
                    wconv_step(2)
                    a, ab = ftA.get(j2)
                    b2, bb2 = ftB.get(j2)
                    kr, krb = kre.get(j2)
                    ki, kib = kim.get(j2)
                    for fl in range(2):
                        j = j2 * 2 + fl
                        base = (j % 2) * 4
                        for ri, (ft, ftb) in enumerate(((a, ab), (b2, bb2))):
                            for hh in range(2):
                                bi = base + ri * 2 + hh
                                for tc in range(16):
                                    S.op("pe", lambda e, ft=ft, tc=tc, fl=fl, hh=hh, bi=bi: e.matmul(banks[bi][:, :], lhsT=ft[:, tc, fl * 128:(fl + 1) * 128], rhs=utm[:, tc, hh * 512:(hh + 1) * 512], start=(tc == 0), stop=(tc == 15)),
                                         reads=[ftb, b_utm[tc]], writes=[bkb[bi]])
                        for hh in range(2):
                            t, tb = tmp.get(j * 2 + hh)
                            ure, uim = base + hh, base + 2 + hh
                            cs = slice(hh * 512, (hh + 1) * 512)
                            S.op("dve", lambda e, t=t, ure=ure, kr=kr, fl=fl, cs=cs: e.tensor_tensor(out=t[:, 0, :], in0=banks[ure][:, :], in1=kr[:, fl, cs], op=ALU.mult), reads=[bkb[ure], krb], writes=[tb])
                            S.op("dve", lambda e, t=t, uim=uim, ki=ki, fl=fl, cs=cs: e.tensor_tensor(out=t[:, 1, :], in0=banks[uim][:, :], in1=ki[:, fl, cs], op=ALU.mult), reads=[bkb[uim], kib], writes=[tb])
                            S.op("dve", lambda e, t=t, ure=ure, ki=ki, fl=fl, cs=cs: e.tensor_tensor(out=t[:, 2, :], in0=banks[ure][:, :], in1=ki[:, fl, cs], op=ALU.mult), reads=[bkb[ure], kib], writes=[tb])
                            S.op("dve", lambda e, t=t, uim=uim, kr=kr, fl=fl, cs=cs: e.tensor_tensor(out=t[:, 3, :], in0=banks[uim][:, :], in1=kr[:, fl, cs], op=ALU.mult), reads=[bkb[uim], krb], writes=[tb])
                            S.op("dve", lambda e, t=t, j=j, cs=cs: e.tensor_tensor(out=Y[:, j, cs], in0=t[:, 0, :], in1=t[:, 1, :], op=ALU.subtract), reads=[tb], writes=[b_Y[j]])
                            S.op("dve", lambda e, t=t, j=j, cs=cs: e.tensor_tensor(out=Y[:, 16 + j, cs], in0=t[:, 2, :], in1=t[:, 3, :], op=ALU.add), reads=[tb], writes=[b_Y[j]])
                            if j == 0:
                                S.op("dve", lambda e, ure=ure, kr=kr, cs=cs: e.tensor_tensor(out=Y[0:1, 0, cs], in0=banks[ure][0:1, :], in1=kr[0:1, 0, cs], op=ALU.mult), reads=[bkb[ure], krb, b_Y[0]], writes=[b_Y[0]])
                                S.op("dve", lambda e, uim=uim, ki=ki, cs=cs: e.tensor_tensor(out=Y[0:1, 16, cs], in0=banks[uim][0:1, :], in1=ki[0:1, 0, cs], op=ALU.mult), reads=[bkb[uim], kib, b_Y[0]], writes=[b_Y[0]])

                pipelined(8, load, comp, pf=1)
                S.barrier()

            with contextlib.ExitStack() as stk:
                S.dma("sp", lambda e: e.dma_start(out=utm[:], in_=v_tm_d.rearrange("(tc p) c -> p tc c", p=128)), writes=b_utm)
                forward(0, stk)
            with contextlib.ExitStack() as stk:
                gt = TP("gt1", [128, 32, 256], BF16, 2, stk)
                x1t = TP("x1t", [128, HW], BF16, 3, stk)

                def load_i1(tc):
                    if tc % 2 == 0:
                        g, gb = gt.get(tc // 2)
                        S.dma("sp", lambda e, g=g, tb_=tc // 2: e.dma_start(out=g[:], in_=GTb[tb_]), writes=[gb])
                    xt_, xb_ = x1t.get(tc)
                    S.dma("sp", lambda e, xt_=xt_, tc=tc: e.dma_start(out=xt_[:], in_=x1_tm_d[tc * 128:(tc + 1) * 128, :]), writes=[xb_])

                def comp_i1(tc):
                    wconv_step(1)
                    g, gb = gt.get(tc // 2)
                    tl = tc % 2
                    xt_, xb_ = x1t.get(tc)
                    for hh in range(2):
                        bi = (tc % 4) * 2 + hh
                        for fc in range(32):
                            S.op("pe", lambda e, g=g, fc=fc, tl=tl, hh=hh, bi=bi: e.matmul(banks[bi][:, :], lhsT=g[:, fc, tl * 128:(tl + 1) * 128], rhs=Y[:, fc, hh * 512:(hh + 1) * 512], start=(fc == 0), stop=(fc == 31)),
                                 reads=[gb, b_Y[fc % 16]], writes=[bkb[bi]])
                        S.op("dve", lambda e, xt_=xt_, tc=tc, hh=hh, bi=bi: e.tensor_tensor(out=utm[:, tc, hh * 512:(hh + 1) * 512], in0=banks[bi][:, :], in1=xt_[:, hh * 512:(hh + 1) * 512], op=ALU.mult),
                             reads=[bkb[bi], xb_], writes=[b_utm[tc]])

                pipelined(16, load_i1, comp_i1, pf=1)
                S.barrier()
            with contextlib.ExitStack() as stk:
                forward(1, stk)
            with contextlib.ExitStack() as stk:
                gt = TP("gt2_", [128, 32, 256], BF16, 2, stk)
                x2t = TP("x2t", [128, 256], BF16, 4, stk)
                ybo = TP("ybo", [128, 256], BF16, 3, stk)

                def load_i2(it):
                    tb_, cc = it // 8, it % 8
                    if cc == 0:
                        g, gb = gt.get(tb_)
                        S.dma("sp", lambda e, g=g, tb_=tb_: e.dma_start(out=g[:], in_=GTb[tb_]), writes=[gb])
                    xt_, xb_ = x2t.get(it)
                    S.dma("sp", lambda e, xt_=xt_, cc=cc, tb_=tb_: e.dma_start(out=xt_[:], in_=x2T_d[cc][:, tb_ * 256:(tb_ + 1) * 256]), writes=[xb_])

                def comp_i2(it):
                    if it % 4 == 0:
                        wconv_step(1)
                    tb_, cc = it // 8, it % 8
                    g, gb = gt.get(tb_)
                    xt_, xb_ = x2t.get(it)
                    bi = it % 8
                    for fc in range(32):
                        S.op("pe", lambda e, g=g, fc=fc, cc=cc, bi=bi: e.matmul(banks[bi][:, 0:256], lhsT=Y[:, fc, cc * 128:(cc + 1) * 128], rhs=g[:, fc, :], start=(fc == 0), stop=(fc == 31)),
                             reads=[gb, b_Y[fc % 16]], writes=[bkb[bi]])
                    yo, yob = ybo.get(it)
                    S.op("dve", lambda e, yo=yo, xt_=xt_, bi=bi: e.tensor_tensor(out=yo[:], in0=banks[bi][:, 0:256], in1=xt_[:], op=ALU.mult), reads=[bkb[bi], xb_], writes=[yob])
                    S.dma("sp", lambda e, yo=yo, cc=cc, tb_=tb_: e.dma_start(out=ybT_d[cc][:, tb_ * 256:(tb_ + 1) * 256], in_=yo[:]), reads=[yob])

                pipelined(64, load_i2, comp_i2, pf=3)
                wconv_step(64)
                S.barrier()
        if stop <= 10:
            return finish(nc, S)

        with contextlib.ExitStack() as stM:
            mT = sb("mT", [128, 16, S_LEN], BF16, stM)
            b_mT = [[S.buf(f"mT{fc}_{th}") for th in range(2)] for fc in range(16)]
            with contextlib.ExitStack() as stk:
                sgu = sb("sgu", [128, 8, S_LEN], BF16, stk)
                ybT = sb("ybT", [128, 8, S_LEN], BF16, stk)
                b_sg = S.buf("sgu")
                b_yb = S.buf("ybT")
                S.dma("sp", lambda e: e.dma_start(out=sgu[:], in_=sguT_d.rearrange("g p t -> p g t")), writes=[b_sg])
                S.dma("sp", lambda e: e.dma_start(out=ybT[:], in_=ybT_d.rearrange("g p t -> p g t")), writes=[b_yb])
                wa = TP("wa", [128, 8, 256], BF16, 2, stk)
                wb_ = TP("wbb", [128, 8, 256], BF16, 2, stk)
                ga = TP("ga", [128, S_LEN], BF16, 2, stk)
                gbt = TP("gbt", [128, S_LEN], BF16, 2, stk)
                mt = TP("mtmp", [128, 2, 512], F32, 3, stk)
                Wa_v = w_a.rearrange("(kc p) n -> p kc n", p=128)
                Wb_v = w_b.rearrange("(kc p) n -> p kc n", p=128)

                def load_ab(blk):
                    wat, wab = wa.get(blk)
                    wbt, wbb = wb_.get(blk)
                    S.dma("pool", lambda e, wat=wat, blk=blk: e.dma_start(out=wat[:], in_=Wa_v[:, :, blk * 256:(blk + 1) * 256]), writes=[wab])
                    S.dma("pool", lambda e, wbt=wbt, blk=blk: e.dma_start(out=wbt[:], in_=Wb_v[:, :, blk * 256:(blk + 1) * 256]), writes=[wbb])

                def load_g(fc):
                    gat, gab = ga.get(fc)
                    gbt_, gbb = gbt.get(fc)
                    S.dma("sp", lambda e, gat=gat, fc=fc: e.dma_start(out=gat[:], in_=gT_d[fc]), writes=[gab])
                    S.dma("sp", lambda e, gbt_=gbt_, fc=fc: e.dma_start(out=gbt_[:], in_=gT_d[16 + fc]), writes=[gbb])

                pidx = 0
                load_ab(0)
                load_g(0)
                for blk in range(8):
                    if blk + 1 < 8:
                        load_ab(blk + 1)
                    wat, wab = wa.get(blk)
                    wbt, wbb = wb_.get(blk)
                    for fl in range(2):
                        fc = blk * 2 + fl
                        if fc + 1 < 16:
                            load_g(fc + 1)
                        gat, gab = ga.get(fc)
                        gbt_, gbb = gbt.get(fc)
                        for th in range(2):
                            base = (pidx % 2) * 4
                            for (wt_, wtb, src, srcb, off) in ((wat, wab, sgu, b_sg, 0), (wbt, wbb, ybT, b_yb, 2)):
                                for kc in range(8):
                                    for tl in range(2):
                                        bi = base + off + tl
                                        tt = th * 2 + tl
                                        S.op("pe", lambda e, wt_=wt_, kc=kc, fl=fl, src=src, tt=tt, bi=bi: e.matmul(banks[bi][:, :], lhsT=wt_[:, kc, fl * 128:(fl + 1) * 128], rhs=src[:, kc, tt * 512:(tt + 1) * 512], start=(kc == 0), stop=(kc == 7)),
                                             reads=[wtb, srcb], writes=[bkb[bi]])
                            for tl in range(2):
                                tt = th * 2 + tl
                                t, tb = mt.get(pidx * 2 + tl)
                                cs = slice(tt * 512, (tt + 1) * 512)
                                S.op("dve", lambda e, t=t, gat=gat, cs=cs, bi=base + tl: e.tensor_tensor(out=t[:, 0, :], in0=banks[bi][:, :], in1=gat[:, cs], op=ALU.mult), reads=[bkb[base + tl], gab], writes=[tb])
                                S.op("dve", lambda e, t=t, gbt_=gbt_, cs=cs, bi=base + 2 + tl: e.tensor_tensor(out=t[:, 1, :], in0=banks[bi][:, :], in1=gbt_[:, cs], op=ALU.mult), reads=[bkb[base + 2 + tl], gbb], writes=[tb])
                                S.op("dve", lambda e, t=t, fc=fc, cs=cs: e.tensor_tensor(out=mT[:, fc, cs], in0=t[:, 0, :], in1=t[:, 1, :], op=ALU.add), reads=[tb], writes=[b_mT[fc][th]])
                            pidx += 1
                S.barrier()
            if stop <= 11:
                return finish(nc, S)
            with contextlib.ExitStack() as stk:
                cst6 = S.buf("cst6")
                g1bo = sb("g1bo", [128, D], F32, stk)
                S.dma("sp", lambda e: e.dma_start(out=g1bo[:], in_=bo_bc), writes=[cst6])
                S.op("dve", lambda e: e.tensor_tensor(out=g1bo[:], in0=g1bo[:], in1=g1_bc[:], op=ALU.mult), reads=[cst6, b_g1], writes=[cst6])
                wo = TP("wo", [128, 16, 512], BF16, 2, stk)
                xbk = TP("xbk", [128, 4, 512], F32, 3, stk)
                tbk = TP("tbk", [128, 4, 512], F32, 2, stk)
                Wo_v = w_o.rearrange("(kc p) n -> p kc n", p=128)
                xq = x.rearrange("(q j p) d -> q p j d", j=4, p=128)
                x1q = x1_d.rearrange("(q j p) d -> q p j d", j=4, p=128)

                def load_wo(cb):
                    t, b = wo.get(cb)
                    S.dma("pool", lambda e, t=t, cb=cb: e.dma_start(out=t[:], in_=Wo_v[:, :, cb * 512:(cb + 1) * 512]), writes=[b])

                def load_xb(it):
                    cb, q = it // 4, it % 4
                    xt_, xb_ = xbk.get(it)
                    S.dma("sp", lambda e, xt_=xt_, q=q, cb=cb: e.dma_start(out=xt_[:], in_=xq[q][:, :, cb * 512:(cb + 1) * 512]), writes=[xb_])

                def comp_xb(it):
                    cb, q = it // 4, it % 4
                    if q == 0 and cb + 1 < 4:
                        load_wo(cb + 1)
                    wt_, wtb = wo.get(cb)
                    cs = slice(cb * 512, (cb + 1) * 512)
                    xt_, xb_ = xbk.get(it)
                    tt2, tb2 = tbk.get(it)
                    for j in range(4):
                        tc = q * 4 + j
                        bi = (it * 4 + j) % 8
                        for kc in range(16):
                            S.op("pe", lambda e, kc=kc, tc=tc, wt_=wt_, bi=bi: e.matmul(banks[bi][:, :], lhsT=mT[:, kc, tc * 128:(tc + 1) * 128], rhs=wt_[:, kc, :], start=(kc == 0), stop=(kc == 15)),
                                 reads=[b_mT[kc][tc // 8], wtb], writes=[bkb[bi]])
                        S.op("dve", lambda e, tt2=tt2, bi=bi, cs=cs, j=j: e.tensor_tensor(out=tt2[:, j, :], in0=banks[bi][:, :], in1=g1_bc[:, cs], op=ALU.mult), reads=[bkb[bi], b_g1], writes=[tb2])
                        S.op("dve", lambda e, xt_=xt_, cs=cs, j=j: e.scalar_tensor_tensor(out=xt_[:, j, :], in0=xt_[:, j, :], scalar=ALPHA, in1=g1bo[:, cs], op0=ALU.mult, op1=ALU.add), reads=[xb_, cst6], writes=[xb_])
                    S.op("dve", lambda e, tt2=tt2, xt_=xt_: e.tensor_tensor(out=tt2[:], in0=tt2[:], in1=xt_[:], op=ALU.add), reads=[tb2, xb_], writes=[tb2])
                    S.dma("sp", lambda e, tt2=tt2, q=q, cs=cs: e.dma_start(out=x1q[q][:, :, cs], in_=tt2[:]), reads=[tb2])

                load_wo(0)
                pipelined(16, load_xb, comp_xb, pf=2)
                S.barrier()
        if stop <= 12:
            return finish(nc, S)

        with contextlib.ExitStack() as stk:
            cst6 = S.buf("cst6c")
            l1g = sb("l1g", [128, D], F32, stk)
            l1b = sb("l1b", [128, D], F32, stk)
            lpp = sb("lpp", [128, 4, 16], F32, stk)
            S.dma("sp", lambda e: e.dma_start(out=l1g[:], in_=ln1g_bc), writes=[cst6])
            S.dma("sp", lambda e: e.dma_start(out=l1b[:], in_=ln1b_bc), writes=[cst6])
            S.dma("sp", lambda e: e.dma_start(out=lpp[:, 0, :], in_=ln1g_pp), writes=[cst6])
            S.dma("sp", lambda e: e.dma_start(out=lpp[:, 1, :], in_=ln1b_pp), writes=[cst6])
            S.op("dve", lambda e: e.tensor_tensor(out=lpp[:, 2, :], in0=lpp[:, 0, :], in1=modpp[:, 2, :], op=ALU.mult), reads=[cst6, b_modpp], writes=[cst6])
            S.op("dve", lambda e: e.tensor_tensor(out=lpp[:, 3, :], in0=lpp[:, 1, :], in1=modpp[:, 2, :], op=ALU.mult), reads=[cst6, b_modpp], writes=[cst6])
            S.op("dve", lambda e: e.tensor_tensor(out=lpp[:, 3, :], in0=lpp[:, 3, :], in1=modpp[:, 3, :], op=ALU.add), reads=[cst6, b_modpp], writes=[cst6])
            rr = TP("rr", [128, D], F32, 3, stk)
            rn = TP("rn", [128, D], F32, 2, stk)
            x1o = TP("x1o", [128, D], F32, 2, stk)
            h2q = TP("h2q", [128, D], BF16, 8, stk)
            s6 = TP("s6b", [128, 4, 6], F32, 2, stk)
            sm = TP("smb", [128, 8], F32, 3, stk)
            h2o = TP("h2o", [128, 16, 512], BF16, 2, stk)

            def load_r(tc):
                r, rb = rr.get(tc)
                S.dma("sp", lambda e, r=r, tc=tc: e.dma_start(out=r[:], in_=x1_d[tc * 128:(tc + 1) * 128, :]), writes=[rb])

            def stage_a1(tc):
                r, rb = rr.get(tc)
                s6t, s6b_ = s6.get(tc)
                smt, smb = sm.get(tc)
                for cb in range(4):
                    S.op("dve", lambda e, s6t=s6t, r=r, cb=cb: e.bn_stats(out=s6t[:, cb, :], in_=r[:, cb * 512:(cb + 1) * 512]), reads=[rb], writes=[s6b_])
                S.op("dve", lambda e: e.bn_aggr(out=smt[:, 0:2], in_=s6t[:].rearrange("p a b -> p (a b)")), reads=[s6b_], writes=[smb])
                S.op("dve", lambda e: e.tensor_scalar(out=smt[:, 2:3], in0=smt[:, 1:2], scalar1=EPS, scalar2=None, op0=ALU.add), reads=[smb], writes=[smb])
                S.op("pool", lambda e: e.tensor_tensor(out=smt[:, 4:5], in0=smt[:, 2:3], in1=eps_t[:, 1:2], op=ALU.pow), reads=[smb, b_cst], writes=[smb])

            def stage_a2(tc):
                r, rb = rr.get(tc)
                smt, smb = sm.get(tc)
                rnt, rnb = rn.get(tc)
                hq, hqb = h2q.get(tc)
                S.op("dve", lambda e: e.tensor_scalar(out=smt[:, 5:6], in0=smt[:, 0:1], scalar1=-1.0, scalar2=smt[:, 4:5], op0=ALU.mult, op1=ALU.mult), reads=[smb], writes=[smb])
                S.op("act", lambda e, r=r, rnt=rnt, smt=smt: e.activation(out=rnt[:], in_=r[:], func=AF.Identity, scale=smt[:, 4:5], bias=smt[:, 5:6]), reads=[rb, smb], writes=[rnb])
                S.op("act", lambda e, r=r, hq=hq, smt=smt: e.activation(out=hq[:], in_=r[:], func=AF.Identity, scale=smt[:, 4:5], bias=smt[:, 5:6]), reads=[rb, smb], writes=[hqb])

            def stage_x(tc):
                rnt, rnb = rn.get(tc)
                xo, xob = x1o.get(tc)
                S.op("dve", lambda e, rnt=rnt, xo=xo: e.tensor_tensor(out=xo[:], in0=rnt[:], in1=l1g[:], op=ALU.mult), reads=[rnb, cst6], writes=[xob])
                S.op("dve", lambda e, xo=xo: e.tensor_tensor(out=xo[:], in0=xo[:], in1=l1b[:], op=ALU.add), reads=[xob, cst6], writes=[xob])
                S.dma("sp", lambda e, xo=xo, tc=tc: e.dma_start(out=x1_d[tc * 128:(tc + 1) * 128, :], in_=xo[:]), reads=[xob])

            def group(g):
                ho, hob = h2o.get(g)
                for j in range(4):
                    hq, hqb = h2q.get(g * 4 + j)
                    for fc in range(16):
                        bi = fc // 2
                        S.op("pe", lambda e, hq=hq, fc=fc, bi=bi, j=j: e.transpose(out=bkbf(bi)[:, (fc % 2) * 512 + j * 128:(fc % 2) * 512 + (j + 1) * 128], in_=hq[:, fc * 128:(fc + 1) * 128], identity=identb[:]),
                             reads=[hqb, b_cst], writes=[bkb[bi]])
                for fc in range(16):
                    bi = fc // 2
                    S.op("act", lambda e, ho=ho, fc=fc, bi=bi: e.activation(out=ho[:, fc, :], in_=bkbf(bi)[:, (fc % 2) * 512:(fc % 2 + 1) * 512], func=AF.Identity, scale=lpp[:, 2, fc:fc + 1], bias=lpp[:, 3, fc:fc + 1]),
                         reads=[bkb[bi], cst6], writes=[hob])
                S.dma("sp", lambda e, ho=ho, g=g: e.dma_start(out=h2T_d[:, :, g * 512:(g + 1) * 512].rearrange("f p t -> p f t"), in_=ho[:]), reads=[hob])

            load_r(0)
            load_r(1)
            stage_a1(0)
            stage_a2(0)
            for tc in range(16):
                if tc + 2 < 16:
                    load_r(tc + 2)
                if tc + 1 < 16:
                    stage_a1(tc + 1)
                stage_x(tc)
                if tc + 1 < 16:
                    stage_a2(tc + 1)
                if tc % 4 == 3:
                    group(tc // 4)
            S.barrier()
        if stop <= 13:
            return finish(nc, S)

        stG1.close()
        with contextlib.ExitStack() as stk:
            cst7 = S.buf("cst7")
            bm1 = sb("bm1", [128, 64], F32, stk)
            bm2 = sb("bm2", [128, 16], F32, stk)
            l2g = sb("l2g", [128, D], F32, stk)
            l2b = sb("l2b", [128, D], F32, stk)
            for dst, src in ((bm1, b_m1_pp), (bm2, b_m2_pp), (l2g, ln2g_bc), (l2b, ln2b_bc)):
                S.dma("sp", lambda e, dst=dst, src=src: e.dma_start(out=dst[:], in_=src), writes=[cst7])
            h2 = TP("h2", [128, 16, 512], BF16, 1, stk)
            aT = sb("aT", [128, 32, 512], BF16, stk)
            b_aT = S.bufs("aT", 32)
            ar = TP("ar", [128, 512], BF16, 3, stk)
            fTs = sb("fTs", [128, 16, 512], F32, stk)
            b_fT = S.bufs("fTs", 16)
            wm1 = TP("wm1", [128, 16, 256], BF16, 3, stk)
            fTb = sb("fTb", [128, 16, 512], BF16, stk)
            wm2 = TP("wm2", [128, 32, 128], BF16, 3, stk)
            x1in = TP("x1in", [128, D], F32, 2, stk)
            wk = TP("wk", [128, D], F32, 2, stk)
            s6 = TP("s6c", [128, 4, 6], F32, 2, stk)
            sm = TP("smc", [128, 8], F32, 2, stk)
            wl = []
            for tq in range(4):
                for hf_ in range(2):
                    for blk in range(16):
                        wl.append(("m1", hf_ * 16 + blk))
                    for fc in range(16):
                        wl.append(("m2", hf_, fc))
            cnt = {"m1": 0, "m2": 0}
            slot_of = []
            for ent in wl:
                slot_of.append(cnt[ent[0]])
                cnt[ent[0]] += 1
            issued = [0]

            def need(i, dist=2):
                while issued[0] <= min(i + dist, len(wl) - 1):
                    k = issued[0]
                    ent = wl[k]
                    if ent[0] == "m1":
                        t, b = wm1.get(slot_of[k])
                        dsc = wm1b_d[ent[1]]
                    else:
                        t, b = wm2.get(slot_of[k])
                        dsc = wm2b_d[ent[1] * 16 + ent[2]]
                    S.dma("pool", lambda e, t=t, dsc=dsc: e.dma_start(out=t[:], in_=dsc), writes=[b])
                    issued[0] += 1

            def load_x1(k):
                xt_, xb_ = x1in.get(k)
                S.dma("sp", lambda e, xt_=xt_, k=k: e.dma_start(out=xt_[:], in_=x1_d[k * 128:(k + 1) * 128, :]), writes=[xb_])

            st_ = {"wi": 0, "ai": 0}
            ht, hb_ = h2.get(0)

            def m1_half(tq, hf_):
                for blk in range(16):
                    need(st_["wi"])
                    wt_, wtb = wm1.get(slot_of[st_["wi"]])
                    st_["wi"] += 1
                    for fl in range(2):
                        fl_c = blk * 2 + fl
                        ffc = hf_ * 32 + fl_c
                        bi = fl_c % 2
                        for kc in range(16):
                            S.op("pe", lambda e, wt_=wt_, kc=kc, fl=fl, bi=bi: e.matmul(banks[bi][:, :], lhsT=wt_[:, kc, fl * 128:(fl + 1) * 128], rhs=ht[:, kc, :], start=(kc == 0), stop=(kc == 15)),
                                 reads=[wtb, hb_], writes=[bkb[bi]])
                        a, ab = ar.get(st_["ai"])
                        st_["ai"] += 1
                        S.op("act", lambda e, a=a, bi=bi, ffc=ffc: e.activation(out=a[:], in_=banks[bi][:, :], func=AF.Relu, bias=bm1[:, ffc:ffc + 1]), reads=[bkb[bi], cst7], writes=[ab])
                        S.op("dve", lambda e, a=a, fl_c=fl_c: e.tensor_tensor(out=aT[:, fl_c, :], in0=a[:], in1=a[:], op=ALU.mult), reads=[ab], writes=[b_aT[fl_c]])

            def m2_half(tq, hf_, hooks=None):
                for fc in range(16):
                    if hooks and fc % 3 == 1 and hooks:
                        hooks.pop(0)()
                    need(st_["wi"])
                    wt_, wtb = wm2.get(slot_of[st_["wi"]])
                    st_["wi"] += 1
                    bi = 2 + fc % 2
                    for kc in range(32):
                        S.op("pe", lambda e, wt_=wt_, kc=kc, bi=bi: e.matmul(banks[bi][:, :], lhsT=wt_[:, kc, :], rhs=aT[:, kc, :], start=(kc == 0), stop=(kc == 31)),
                             reads=[wtb, b_aT[kc]], writes=[bkb[bi]])
                    if hf_ == 0:
                        S.op("act", lambda e, fc=fc, bi=bi: e.activation(out=fTs[:, fc, :], in_=banks[bi][:, :], func=AF.Identity, bias=bm2[:, fc:fc + 1]), reads=[bkb[bi], cst7], writes=[b_fT[fc]])
                    else:
                        S.op("dve", lambda e, fc=fc, bi=bi: e.tensor_tensor(out=fTb[:, fc, :], in0=banks[bi][:, :], in1=fTs[:, fc, :], op=ALU.add), reads=[bkb[bi], b_fT[fc]], writes=[b_fT[fc]])

            def tail_A(tc):
                j = tc % 4
                xt_, xb_ = x1in.get(tc)
                for fc in range(16):
                    bi = 4 + fc // 4
                    S.op("pe", lambda e, fc=fc, j=j, bi=bi: e.transpose(out=bkbf(bi)[:, (fc % 4) * 128:(fc % 4 + 1) * 128], in_=fTb[:, fc, j * 128:(j + 1) * 128], identity=identb[:]),
                         reads=[b_fT[fc], b_cst], writes=[bkb[bi]])
                w, wb2 = wk.get(tc)
                for cb in range(4):
                    cs = slice(cb * 512, (cb + 1) * 512)
                    S.op("dve", lambda e, w=w, cb=cb, cs=cs: e.tensor_tensor(out=w[:, cs], in0=bkbf(4 + cb)[:, 0:512], in1=g2_bc[:, cs], op=ALU.mult), reads=[bkb[4 + cb], b_g2], writes=[wb2])
                S.op("dve", lambda e, xt_=xt_, w=w: e.scalar_tensor_tensor(out=w[:], in0=xt_[:], scalar=ALPHA, in1=w[:], op0=ALU.mult, op1=ALU.add), reads=[xb_, wb2], writes=[wb2])

            def tail_B(tc):
                w, wb2 = wk.get(tc)
                s6t, s6b_ = s6.get(tc)
                smt, smb = sm.get(tc)
                for cb in range(4):
                    S.op("dve", lambda e, s6t=s6t, w=w, cb=cb: e.bn_stats(out=s6t[:, cb, :], in_=w[:, cb * 512:(cb + 1) * 512]), reads=[wb2], writes=[s6b_])
                ln_tail(S, s6t, s6b_, smt, smb, eps_t, b_cst)
                S.op("act", lambda e, w=w, smt=smt: e.activation(out=w[:], in_=w[:], func=AF.Identity, scale=smt[:, 4:5], bias=smt[:, 5:6]), reads=[wb2, smb], writes=[wb2])
                S.op("dve", lambda e, w=w: e.tensor_tensor(out=w[:], in0=w[:], in1=l2g[:], op=ALU.mult), reads=[wb2, cst7], writes=[wb2])
                S.op("dve", lambda e, w=w: e.tensor_tensor(out=w[:], in0=w[:], in1=l2b[:], op=ALU.add), reads=[wb2, cst7], writes=[wb2])
                S.dma("sp", lambda e, w=w, tc=tc: e.dma_start(out=out[tc * 128:(tc + 1) * 128, :], in_=w[:]), reads=[wb2])

            def tail_steps(tq):
                base = tq * 4

                def s0():
                    tail_A(base)

                def mk(j):
                    def f():
                        if j + 1 < 4:
                            tail_A(base + j + 1)
                        tail_B(base + j)
                        if j + 2 < 4:
                            load_x1(base + j + 2)
                    return f

                return [s0] + [mk(j) for j in range(4)]

            for tq in range(4):
                S.dma("sp", lambda e, tq=tq: e.dma_start(out=ht[:], in_=h2T_d[:, :, tq * 512:(tq + 1) * 512].rearrange("f p t -> p f t")), writes=[hb_])
                hooks = None
                if tq > 0:
                    load_x1((tq - 1) * 4)
                    load_x1((tq - 1) * 4 + 1)
                    hooks = tail_steps(tq - 1)
                m1_half(tq, 0)
                m2_half(tq, 0, hooks)
                assert not hooks
                m1_half(tq, 1)
                m2_half(tq, 1)
            load_x1(12)
            load_x1(13)
            for f in tail_steps(3):
                f()
            S.barrier()
        return finish(nc, S)


def pipelined(n, load, compute, pf=1):
    for i in range(min(pf, n)):
        load(i)
    for i in range(n):
        if i + pf < n:
            load(i + pf)
        compute(i)


def ln_tail(S, s6, s6b, smt, smb, eps_t, b_cst):
    S.op("dve", lambda e: e.bn_aggr(out=smt[:, 0:2], in_=s6[:].rearrange("p a b -> p (a b)")), reads=[s6b], writes=[smb])
    S.op("dve", lambda e: e.tensor_scalar(out=smt[:, 2:3], in0=smt[:, 1:2], scalar1=EPS, scalar2=None, op0=ALU.add), reads=[smb], writes=[smb])
    S.op("pool", lambda e: e.tensor_tensor(out=smt[:, 4:5], in0=smt[:, 2:3], in1=eps_t[:, 1:2], op=ALU.pow), reads=[smb, b_cst], writes=[smb])
    S.op("dve", lambda e: e.tensor_scalar(out=smt[:, 5:6], in0=smt[:, 0:1], scalar1=-1.0, scalar2=smt[:, 4:5], op0=ALU.mult, op1=ALU.mult), reads=[smb], writes=[smb])


def finish(nc, S):
    S.barrier()
    with nc.Block() as block:
        S.emit(block)
    return nc


_CONST = {}


def _constants():
    if _CONST:
        return _CONST
    N, L = NFFT, S_LEN
    t = np.arange(L, dtype=np.int64)
    r = np.arange(N, dtype=np.int64)
    f = np.where(r < L, r, r - L)
    ang = 2.0 * np.pi * ((f[:, None] * t[None, :]) % N).astype(np.float64) / N
    cosm = np.cos(ang)
    sinm = np.sin(ang)
    is_re = (r < L)[:, None]
    nyq = (r == L)[:, None]
    alt = np.where(t % 2 == 0, 1.0, -1.0)[None, :]
    F = np.where(is_re, cosm, np.where(nyq, alt, -sinm))
    scale = np.where((f == 0), 1.0 / N, 2.0 / N)[:, None]
    G = F * scale
    FT = np.ascontiguousarray(F.T)

    def blkT(M):
        return np.ascontiguousarray(M.reshape(16, 128, 16, 256).transpose(2, 1, 0, 3)).astype(ml_dtypes.bfloat16)

    _CONST["FTb"] = blkT(FT)
    ftz = _CONST["FTb"][8].copy()
    ftz[:, :, 0] = 0
    _CONST["FTz"] = ftz
    ne = np.zeros((128, 16, 128), np.float32)
    ne[:, :, 0] = np.where((np.arange(16)[None, :] * 128 + np.arange(128)[:, None]) % 2 == 0, 1.0, -1.0)
    _CONST["nyqE"] = ne.astype(ml_dtypes.bfloat16)
    _CONST["GTb"] = np.ascontiguousarray(G.reshape(32, 128, 8, 256).transpose(2, 1, 0, 3)).astype(ml_dtypes.bfloat16)
    f32 = np.float32
    tt = np.linspace(0.0, 1.0, L, dtype=f32)[:, None]
    omega = (f32(2.0 * math.pi) * np.arange(L, dtype=f32)[:, None] / f32(L)).astype(f32)
    bands = np.linspace(1e-4, 15, 16, dtype=f32)[None, :]
    feats = np.concatenate([tt, np.cos(bands * omega), -np.sin(bands * omega)], axis=-1).astype(f32)
    _CONST["featsT"] = np.ascontiguousarray(feats.T)
    min_decay = math.log(1e-2) / 1.5
    max_decay = math.log(1e-2) / 0.3
    deltas = np.abs(np.linspace(min_decay, max_decay, HW, dtype=f32))
    _CONST["window"] = np.exp(-tt * deltas).astype(f32)
    _CONST["ident"] = np.eye(128, dtype=f32)
    _CONST["ones"] = np.ones((128, 128), dtype=f32)
    return _CONST


def _pp(v, n):
    return np.ascontiguousarray(np.asarray(v, np.float32).reshape(n, 128).T)


def _bc(v):
    v = np.asarray(v, np.float32)
    return np.ascontiguousarray(np.broadcast_to(v[None, :], (128, v.shape[0])))


def make_in_maps(inp):
    C = _constants()
    g = lambda k: np.asarray(inp[k], np.float32)
    b_in = g("b_in")[0]
    shared = {
        "w_ada": np.ascontiguousarray(g("w_ada")[0]),
        "w_in": np.ascontiguousarray(g("w_in")[0]),
        "b_in_pp": _pp(b_in, 72),
        "b_in_v_bc": _bc(b_in[HW:2 * HW]),
        "lng_bc": _bc(g("sgu_ln_g")[0]),
        "lnb_bc": _bc(g("sgu_ln_b")[0]),
        "wsT": np.ascontiguousarray(g("sgu_w")[0].transpose(2, 0, 1)),
        "bs_bc": _bc(g("sgu_b")[0].reshape(-1)),
        "cw_pp": np.ascontiguousarray(g("hy_conv_w")[0].reshape(3, 24, 128).transpose(2, 0, 1)),
        "cb_pp": _pp(g("hy_conv_b")[0], 24),
        "featsT": C["featsT"],
        "hw1": np.ascontiguousarray(g("hy_w1")[0]),
        "hb1_pp": np.ascontiguousarray(g("hy_b1")[0].reshape(64, 1)),
        "hfr_pp": np.ascontiguousarray(g("hy_freq")[0].reshape(64, 1)),
        "hw2": np.ascontiguousarray(g("hy_w2")[0]),
        "hb2_pp": np.ascontiguousarray(g("hy_b2")[0].reshape(64, 1)),
        "hw3": np.ascontiguousarray(g("hy_w3")[0]),
        "window": C["window"],
        "skip_bc": np.ascontiguousarray(np.broadcast_to(g("hy_skip")[0][None], (128, 2, HW))),
        "FTb": C["FTb"], "GTb": C["GTb"], "FTz": C["FTz"], "nyqE": C["nyqE"],
        "w_a": np.ascontiguousarray(g("w_branch_a")[0]),
        "w_b": np.ascontiguousarray(g("w_branch_b")[0]),
        "w_o": np.ascontiguousarray(g("w_o")[0]),
        "bo_bc": _bc(g("b_o")[0]),
        "ln1g_bc": _bc(g("ln1_g")[0]), "ln1b_bc": _bc(g("ln1_b")[0]),
        "ln1g_pp": _pp(g("ln1_g")[0], 16), "ln1b_pp": _pp(g("ln1_b")[0], 16),
        "w_m1": np.ascontiguousarray(g("w_m1")[0]),
        "b_m1_pp": _pp(g("b_m1")[0], 64),
        "w_m2": np.ascontiguousarray(g("w_m2")[0]),
        "b_m2_pp": _pp(g("b_m2")[0], 16),
        "ln2g_bc": _bc(g("ln2_g")[0]), "ln2b_bc": _bc(g("ln2_b")[0]),
        "ident": C["ident"], "ones": C["ones"],
    }
    acc0 = np.zeros((128, 6 * D), np.float32)
    acc0[0, :] = g("b_ada")[0]
    shared["acc0"] = acc0
    xs = g("x")
    cs = g("c")
    maps = []
    for b in range(NB):
        m = dict(shared)
        m["x"] = np.ascontiguousarray(xs[b])
        m["c_pp"] = _pp(cs[b], 16)
        maps.append(m)
    return maps


_PROG = {}


def kernel(**inputs):
    if "nc" not in _PROG:
        _PROG["nc"] = build_program()
    nc = _PROG["nc"]
    in_maps = make_in_maps(inputs)
    res = run_bass_kernel_spmd(nc, in_maps, core_ids=list(range(NB)))
    return np.stack([np.asarray(r["out"], np.float32) for r in res.results], axis=0)
```

```python
import contextlib
import math
import numpy as np
import ml_dtypes
import concourse.bass as bass
import concourse.mybir as mybir
from concourse.bass_utils import run_bass_kernel_spmd

F32 = mybir.dt.float32
BF16 = mybir.dt.bfloat16
AF = mybir.ActivationFunctionType
ALU = mybir.AluOpType
AX = mybir.AxisListType

D = 2048
S_LEN = 2048
NB = 8
HW = 1024
DFF = 8192
INW = 9216
NFFT = 4096
ALPHA = 2.0 ** 0.25
EPS = 1e-5
ENGS = ("pe", "act", "dve", "pool", "sp")


class Buf:
    __slots__ = ("name", "last_w", "readers")

    def __init__(self, name):
        self.name = name
        self.last_w = None
        self.readers = []


class Op:
    __slots__ = ("eng", "fn", "deps", "signal", "count", "is_dma", "sem_key", "dma_count", "waits", "wkey")

    def __init__(self, eng, fn, is_dma=False):
        self.eng = eng
        self.fn = fn
        self.deps = []
        self.signal = False
        self.count = 0
        self.is_dma = is_dma
        self.sem_key = None
        self.dma_count = 0
        self.waits = None
        self.wkey = None


class Sched:
    def __init__(self, nc, stack):
        self.nc = nc
        self.stack = stack
        self.ops = {e: [] for e in ENGS}
        self.esem = {e: stack.enter_context(nc.semaphore("s_" + e)) for e in ("pe", "act", "dve", "pool")}
        self.dsem = {}
        self.free_sems = {}
        self.nsem = 0
        self.all_bufs = []

    def buf(self, name):
        b = Buf(name)
        self.all_bufs.append(b)
        return b

    def bufs(self, name, n):
        return [self.buf(f"{name}{i}") for i in range(n)]

    def _track(self, op, reads, writes):
        deps = []
        for b in reads:
            if b.last_w is not None:
                deps.append(b.last_w)
        for b in writes:
            if b.last_w is not None:
                deps.append(b.last_w)
            deps.extend(b.readers)
        for b in reads:
            if not op.is_dma:
                b.readers = [r for r in b.readers if r.is_dma or r.eng != op.eng]
            b.readers.append(op)
        for b in writes:
            b.last_w = op
            b.readers = []
        seen = set()
        for d in deps:
            if id(d) in seen or d is op:
                continue
            seen.add(id(d))
            if (not d.is_dma) and (not op.is_dma) and d.eng == "pe" and op.eng == "pe":
                continue
            op.deps.append(d)
            if not d.is_dma:
                d.signal = True

    def op(self, eng, fn, reads=(), writes=()):
        o = Op(eng, fn)
        self._track(o, reads, writes)
        self.ops[eng].append(o)
        return o

    def dma(self, queue, fn, reads=(), writes=(), key=None, par=False):
        o = Op(queue, fn, is_dma=True)
        self._track(o, reads, writes)
        if par:
            wset = set(id(b) for b in writes)
            o.deps = [d for d in o.deps if not (d.is_dma and getattr(d, "wkey", None) == tuple(sorted(wset)))]
        o.wkey = tuple(sorted(id(b) for b in writes))
        if key is None:
            key = (list(writes) + list(reads))[0]
        kid = id(key)
        if kid not in self.dsem:
            fl = self.free_sems.setdefault(queue, [])
            if fl:
                sem, cnt = fl.pop()
            else:
                sem, cnt = self.stack.enter_context(self.nc.semaphore(f"d{self.nsem}")), 0
                self.nsem += 1
            self.dsem[kid] = [sem, cnt, queue]
        ent = self.dsem[kid]
        assert ent[2] == queue, "a DMA semaphore key must stay on one queue"
        ent[1] += 16
        o.sem_key = ent[0]
        o.dma_count = ent[1]
        self.ops[queue].append(o)
        return o

    def barrier(self):
        lasts = []
        for e in ("pe", "act", "dve", "pool"):
            for o in reversed(self.ops[e]):
                if not o.is_dma and o.fn is not None:
                    o.signal = True
                    lasts.append(o)
                    break
        dmas = [(ent[0], ent[1]) for ent in self.dsem.values() if ent[1] > 0]
        dmas += [(sem, cnt) for fl in self.free_sems.values() for sem, cnt in fl if cnt > 0]
        for e in ENGS:
            o = Op(e, None)
            o.deps = list(lasts)
            o.waits = dmas
            self.ops[e].append(o)
        for b in self.all_bufs:
            b.last_w = None
            b.readers = []
        for ent in self.dsem.values():
            self.free_sems.setdefault(ent[2], []).append((ent[0], ent[1]))
        self.dsem = {}

    def emit(self, block):
        for e in ("pe", "act", "dve", "pool"):
            c = 0
            for o in self.ops[e]:
                if o.is_dma or o.fn is None:
                    continue
                if o.signal:
                    c += 1
                    o.count = c
        handles = {"pe": block.tensor, "act": block.scalar, "dve": block.vector, "pool": block.gpsimd, "sp": block.sync}
        for e in ENGS:
            ops = self.ops[e]
            if not ops:
                continue

            def body(eng, ops=ops):
                waited = {}
                for o in ops:
                    need = {}
                    for d in o.deps:
                        if d.is_dma:
                            sem, val = d.sem_key, d.dma_count
                        else:
                            sem, val = self.esem[d.eng], d.count
                        k = id(sem)
                        if val > need.get(k, (None, 0))[1]:
                            need[k] = (sem, val)
                    if o.waits:
                        for sem, val in o.waits:
                            k = id(sem)
                            if val > need.get(k, (None, 0))[1]:
                                need[k] = (sem, val)
                    for k, (sem, val) in need.items():
                        if waited.get(k, 0) >= val:
                            continue
                        waited[k] = val
                        eng.wait_ge(sem, val)
                    if o.fn is None:
                        continue
                    ins = o.fn(eng)
                    if o.is_dma:
                        ins.then_inc(o.sem_key, 16)
                    elif o.signal:
                        ins.then_inc(self.esem[o.eng], 1)

            handles[e](body)


def build_program(stop=99, debug=False):
    nc = bass.Bass("TRN2", target_bir_lowering=False)

    def din(name, shape, dt=F32):
        return nc.dram_tensor(name, list(shape), dt, kind="ExternalInput").ap()

    def dscr(name, shape, dt):
        return nc.dram_tensor(name, list(shape), dt, kind="ExternalOutput" if debug else "Internal").ap()

    x = din("x", [S_LEN, D])
    c_pp = din("c_pp", [128, 16])
    w_ada = din("w_ada", [D, 6 * D])
    acc0 = din("acc0", [128, 6 * D])
    w_in = din("w_in", [D, INW])
    b_in_pp = din("b_in_pp", [128, 72])
    b_in_v_bc = din("b_in_v_bc", [128, HW])
    lng_bc = din("lng_bc", [128, HW])
    lnb_bc = din("lnb_bc", [128, HW])
    wsT = din("wsT", [128, 8, 128])
    bs_bc = din("bs_bc", [128, HW])
    cw_pp = din("cw_pp", [128, 3, 24])
    cb_pp = din("cb_pp", [128, 24])
    featsT = din("featsT", [33, S_LEN])
    hw1 = din("hw1", [33, 64])
    hb1_pp = din("hb1_pp", [64, 1])
    hfr_pp = din("hfr_pp", [64, 1])
    hw2 = din("hw2", [64, 64])
    hb2_pp = din("hb2_pp", [64, 1])
    hw3 = din("hw3", [64, 4096])
    window = din("window", [S_LEN, HW])
    skip_bc = din("skip_bc", [128, 2, HW])
    FTb = din("FTb", [16, 128, 16, 256], BF16)
    GTb = din("GTb", [8, 128, 32, 256], BF16)
    FTz = din("FTz", [128, 16, 256], BF16)
    nyqE = din("nyqE", [128, 16, 128], BF16)
    w_a = din("w_a", [HW, D])
    w_b = din("w_b", [HW, D])
    w_o = din("w_o", [D, D])
    bo_bc = din("bo_bc", [128, D])
    ln1g_bc = din("ln1g_bc", [128, D])
    ln1b_bc = din("ln1b_bc", [128, D])
    ln1g_pp = din("ln1g_pp", [128, 16])
    ln1b_pp = din("ln1b_pp", [128, 16])
    w_m1 = din("w_m1", [D, DFF])
    b_m1_pp = din("b_m1_pp", [128, 64])
    w_m2 = din("w_m2", [DFF, D])
    b_m2_pp = din("b_m2_pp", [128, 16])
    ln2g_bc = din("ln2g_bc", [128, D])
    ln2b_bc = din("ln2b_bc", [128, D])
    ident_d = din("ident", [128, 128])
    ones_d = din("ones", [128, 128])
    out = nc.dram_tensor("out", [S_LEN, D], F32, kind="ExternalOutput").ap()

    uT_d = dscr("uT_d", [8, 128, S_LEN], BF16)
    sguT_d = dscr("sguT_d", [8, 128, S_LEN], BF16)
    v_tm_d = dscr("v_tm_d", [S_LEN, HW], BF16)
    x1_tm_d = dscr("x1_tm_d", [S_LEN, HW], BF16)
    x2T_d = dscr("x2T_d", [8, 128, S_LEN], BF16)
    gT_d = dscr("gT_d", [32, 128, S_LEN], BF16)
    filt_d = dscr("filt_d", [S_LEN, 4096], BF16)
    K_d = dscr("K_d", [2, 32, 128, HW], F32)
    ybT_d = dscr("ybT_d", [8, 128, S_LEN], BF16)
    x1_d = dscr("x1_d", [S_LEN, D], F32)
    h2T_d = dscr("h2T_d", [16, 128, S_LEN], BF16)
    wm1b_d = nc.dram_tensor("wm1b_d", [32, 128, 16, 256], BF16).ap()
    wm2b_d = nc.dram_tensor("wm2b_d", [32, 128, 32, 128], BF16).ap()
    dbg_d = dscr("dbg_d", [128, 4 * 16 + 2 * D], F32) if debug else None
    hT_dbg = dscr("hT_dbg", [16, 128, S_LEN], BF16) if debug else None

    with contextlib.ExitStack() as st:
        S = Sched(nc, st)

        def sb(name, shape, dt, stack=st):
            return stack.enter_context(nc.sbuf_tensor("sb_" + name, list(shape), dt))

        banks = [st.enter_context(nc.psum_tensor(f"bank{i}", [128, 512], F32)) for i in range(8)]
        bkb = S.bufs("bank", 8)

        class TP:
            def __init__(self, name, shape, dt, n, stack):
                self.t = [sb(f"{name}_{i}", shape, dt, stack) for i in range(n)]
                self.b = S.bufs(name, n)
                self.n = n

            def get(self, i):
                return self.t[i % self.n], self.b[i % self.n]

        ident = sb("ident", [128, 128], F32)
        ones = sb("ones", [128, 128], F32)
        b_cst = S.buf("cst")
        S.dma("sp", lambda e: e.dma_start(out=ident[:], in_=ident_d), writes=[b_cst])
        S.dma("sp", lambda e: e.dma_start(out=ones[:], in_=ones_d), writes=[b_cst])
        binpp = sb("binpp", [128, 72], F32)
        S.dma("sp", lambda e: e.dma_start(out=binpp[:], in_=b_in_pp), writes=[b_cst])
        modpp = sb("modpp", [128, 4, 16], F32)
        b_modpp = S.buf("modpp")
        g2_bc = sb("g2_bc", [128, D], F32)
        eps_t = sb("eps_t", [128, 2], F32)
        identb = sb("identb", [128, 128], BF16)
        stG1 = contextlib.ExitStack()
        g1_bc = sb("g1_bc", [128, D], F32, stG1)
        b_g1 = S.buf("g1bc")
        b_g2 = S.buf("g2bc")
        S.op("dve", lambda e: e.memset(eps_t[:, 0:1], EPS), writes=[b_cst])
        S.op("dve", lambda e: e.memset(eps_t[:, 1:2], -0.5), writes=[b_cst])
        S.op("act", lambda e: e.activation(out=identb[:], in_=ident[:], func=AF.Identity), reads=[b_cst], writes=[b_cst])

        def bkbf(bi):
            return banks[bi][:, :].bitcast(BF16)

        W_in_v = w_in.rearrange("(kc p) n -> p kc n", p=128)

        stMod = contextlib.ExitStack()
        def mod_steps():
            acc = sb("acc", [128, 3 * D], F32, stMod)
            b_acc = S.buf("acc")
            cond = sb("cond", [128, 16], F32, stMod)
            cl = sb("cl", [128, 16], F32, stMod)
            b_cond = S.buf("cond")
            tbc = sb("tbc", [128, D], F32, stMod)
            b_tbc = S.buf("tbc")
            dg = sb("dg", [128, 16, 128], F32, stMod)
            b_dg = S.buf("dg")
            wt = TP("wada", [128, D], F32, 3, stMod)
            yield
            S.dma("pool", lambda e: e.dma_start(out=cl[:], in_=c_pp), writes=[b_cond])
            S.op("act", lambda e: e.activation(out=cond[:], in_=cl[:], func=AF.Silu), reads=[b_cond], writes=[b_cond])
            names = ["sh1", "sc1", "g1", "sh2", "sc2", "g2"]
            ppidx = {"sc1": 0, "sh1": 1, "sc2": 2, "sh2": 3}
            it = [0]
            for half in range(2):
                lo = half * 3 * D
                S.dma("pool", lambda e, lo=lo: e.dma_start(out=acc[:], in_=acc0[:, lo:lo + 3 * D]), writes=[b_acc])
                items = [(cb, kc) for cb in range(3) for kc in range(16)]

                def load(i, lo=lo, items=items):
                    cb, kc = items[i]
                    t, b = wt.get(it[0] + i)
                    S.dma("pool", lambda e, t=t, kc=kc, cb=cb: e.dma_start(out=t[:], in_=w_ada[kc * 128:(kc + 1) * 128, lo + cb * D: lo + (cb + 1) * D]), writes=[b])

                load(0)
                load(1)
                for i, (cb, kc) in enumerate(items):
                    if i + 2 < len(items):
                        load(i + 2)
                    t, b = wt.get(it[0] + i)
                    S.op("dve", lambda e, t=t, kc=kc, cb=cb: e.scalar_tensor_tensor(
                        out=acc[:, cb * D:(cb + 1) * D], in0=t[:], scalar=cond[:, kc:kc + 1], in1=acc[:, cb * D:(cb + 1) * D],
                        op0=ALU.mult, op1=ALU.add), reads=[b, b_cond, b_acc], writes=[b_acc])
                    yield
                it[0] += len(items)
                for _ in range(10):
                    yield
                for cb in range(3):
                    nm = names[half * 3 + cb]
                    dst, bd = {"g1": (g1_bc, b_g1), "g2": (g2_bc, b_g2)}.get(nm, (tbc, b_tbc))
                    for j in range(4):
                        bi = 6 + (j % 2)
                        S.op("pe", lambda e, bi=bi, cb=cb, j=j: e.matmul(banks[bi][:, :], lhsT=ones[:], rhs=acc[:, cb * D + j * 512: cb * D + (j + 1) * 512], start=True, stop=True),
                             reads=[b_acc, b_cst], writes=[bkb[bi]])
                        S.op("act", lambda e, dst=dst, j=j, bi=bi: e.activation(out=dst[:, j * 512:(j + 1) * 512], in_=banks[bi][:, :], func=AF.Identity),
                             reads=[bkb[bi]], writes=[bd])
                    if nm in ppidx:
                        idx = ppidx[nm]
                        for kc in range(16):
                            S.op("dve", lambda e, kc=kc: e.tensor_tensor(out=dg[:, kc, :], in0=tbc[:, kc * 128:(kc + 1) * 128], in1=ident[:], op=ALU.mult),
                                 reads=[b_tbc, b_cst], writes=[b_dg])
                        S.op("dve", lambda e, idx=idx: e.tensor_reduce(out=modpp[:, idx, :], in_=dg[:], axis=AX.X, op=ALU.add), reads=[b_dg], writes=[b_modpp])
                        if nm.startswith("sc"):
                            S.op("dve", lambda e, idx=idx: e.tensor_scalar(out=modpp[:, idx, :], in0=modpp[:, idx, :], scalar1=1.0, scalar2=None, op0=ALU.add),
                                 reads=[b_modpp], writes=[b_modpp])
                    yield

        mgen = mod_steps()

        def advance(k):
            for _ in range(k):
                try:
                    next(mgen)
                except StopIteration:
                    return False
            return True

        advance(1)
        with contextlib.ExitStack() as stk:
            cst4 = S.buf("cst4")
            fT = sb("fT", [33, S_LEN], F32, stk)
            w1t = sb("w1t", [33, 64], F32, stk)
            w2t = sb("w2t", [64, 64], F32, stk)
            w3f = sb("w3f", [64, 4096], F32, stk)
            w3b = sb("w3b", [64, 4096], BF16, stk)
            hv = sb("hv", [64, 8], F32, stk)
            for dst, src in ((fT[:], featsT), (w1t[:], hw1), (w2t[:], hw2), (w3f[:], hw3), (hv[:, 0:1], hb1_pp), (hv[:, 1:2], hfr_pp), (hv[:, 2:3], hb2_pp)):
                S.dma("sp", lambda e, dst=dst, src=src: e.dma_start(out=dst, in_=src), writes=[cst4])
            S.op("dve", lambda e: e.tensor_tensor(out=hv[:, 3:4], in0=hv[:, 0:1], in1=hv[:, 1:2], op=ALU.mult), reads=[cst4], writes=[cst4])
            S.op("dve", lambda e: e.tensor_tensor(out=hv[:, 4:5], in0=hv[:, 2:3], in1=hv[:, 1:2], op=ALU.mult), reads=[cst4], writes=[cst4])
            S.op("dve", lambda e: e.tensor_tensor(out=w3b[:, 0:2048], in0=w3f[:, 0:2048], in1=w3f[:, 2048:4096], op=ALU.add), reads=[cst4], writes=[cst4])
            S.op("dve", lambda e: e.tensor_tensor(out=w3b[:, 2048:4096], in0=w3f[:, 0:2048], in1=w3f[:, 2048:4096], op=ALU.subtract), reads=[cst4], writes=[cst4])
            arg = sb("arg", [64, S_LEN], F32, stk)
            wr1 = sb("wr1", [64, S_LEN], F32, stk)
            wr2 = sb("wr2", [64, S_LEN], F32, stk)
            h1 = sb("h1", [64, S_LEN], F32, stk)
            h2b = sb("h2b", [64, S_LEN], BF16, stk)
            b_arg = S.buf("arg")
            b_h1 = S.buf("h1")
            b_h2 = S.buf("h2")
            TWO_PI = 2.0 * math.pi

            def sin_layer(lhs, kdim, rhs_t, rhs_b, bias_col, dst, dst_b):
                for tt in range(4):
                    S.op("pe", lambda e, tt=tt: e.matmul(banks[tt][0:64, :], lhsT=lhs[0:kdim, :], rhs=rhs_t[0:kdim, tt * 512:(tt + 1) * 512], start=True, stop=True),
                         reads=[cst4, rhs_b], writes=[bkb[tt]])
                    S.op("act", lambda e, tt=tt: e.activation(out=arg[:, tt * 512:(tt + 1) * 512], in_=banks[tt][0:64, :], func=AF.Identity, scale=hv[:, 1:2], bias=hv[:, bias_col:bias_col + 1]),
                         reads=[bkb[tt], cst4], writes=[b_arg])
                S.op("dve", lambda e: e.tensor_scalar(out=wr1[:], in0=arg[:], scalar1=math.pi, scalar2=-TWO_PI, op0=ALU.is_gt, op1=ALU.mult), reads=[b_arg], writes=[b_arg])
                S.op("dve", lambda e: e.tensor_scalar(out=wr2[:], in0=arg[:], scalar1=-math.pi, scalar2=TWO_PI, op0=ALU.is_lt, op1=ALU.mult), reads=[b_arg], writes=[b_arg])
                S.op("dve", lambda e: e.tensor_tensor(out=wr1[:], in0=wr1[:], in1=wr2[:], op=ALU.add), reads=[b_arg], writes=[b_arg])
                S.op("dve", lambda e: e.tensor_tensor(out=arg[:], in0=arg[:], in1=wr1[:], op=ALU.add), reads=[b_arg], writes=[b_arg])
                S.op("act", lambda e: e.activation(out=dst[:], in_=arg[:], func=AF.Sin), reads=[b_arg], writes=[dst_b])

            advance(1)
            sin_layer(w1t, 33, fT, cst4, 3, h1, b_h1)
            sin_layer(w2t, 64, h1, b_h1, 4, h2b, b_h2)
            wint = TP("wint", [128, HW], F32, 2, stk)
            fo = TP("fo", [128, 4096], BF16, 2, stk)

            def load_win(tc):
                wt_, wb_ = wint.get(tc)
                S.dma("sp", lambda e, wt_=wt_, tc=tc: e.dma_start(out=wt_[:], in_=window[tc * 128:(tc + 1) * 128, :]), writes=[wb_])

            def comp_filt(tc):
                wt_, wb_ = wint.get(tc)
                f, fb = fo.get(tc)
                for blk in range(8):
                    bi = blk
                    S.op("pe", lambda e, tc=tc, blk=blk, bi=bi: e.matmul(banks[bi][:, :], lhsT=h2b[:, tc * 128:(tc + 1) * 128], rhs=w3b[:, blk * 512:(blk + 1) * 512], start=True, stop=True),
                         reads=[b_h2, cst4], writes=[bkb[bi]])
                    S.op("dve", lambda e, f=f, wt_=wt_, blk=blk, bi=bi: e.tensor_tensor(out=f[:, blk * 512:(blk + 1) * 512], in0=banks[bi][:, :], in1=wt_[:, (blk % 2) * 512:(blk % 2 + 1) * 512], op=ALU.mult),
                         reads=[bkb[bi], wb_], writes=[fb])
                S.dma("sp", lambda e, f=f, tc=tc: e.dma_start(out=filt_d[tc * 128:(tc + 1) * 128, :], in_=f[:]), reads=[fb])

            pipelined(16, load_win, comp_filt, pf=1)
            S.barrier()

        with contextlib.ExitStack() as stk:
            skb = sb("skb", [128, 2, HW], F32, stk)
            cst5 = S.buf("cst5")
            S.dma("sp", lambda e: e.dma_start(out=skb[:], in_=skip_bc), writes=[cst5])
            nyq = sb("nyq", [128, 16, 128], BF16, stk)
            S.dma("sp", lambda e: e.dma_start(out=nyq[:], in_=nyqE), writes=[cst5])
            hs = sb("hs", [128, 16, HW], BF16, stk)
            hd = sb("hd", [128, 16, HW], BF16, stk)
            b_hs = S.buf("hs")
            b_hd = S.buf("hd")
            ftA = TP("ftA", [128, 16, 256], BF16, 2, stk)
            ko = TP("ko", [128, HW], F32, 3, stk)
            fv = filt_d.rearrange("(tc p) c -> p tc c", p=128)
            items = [(n, j2) for n in range(2) for j2 in range(16)]

            def load_ft(i):
                n, j2 = items[i]
                a, ab = ftA.get(i)
                src = FTz if j2 == 8 else FTb[j2]
                S.dma("sp", lambda e, a=a, src=src: e.dma_start(out=a[:], in_=src), writes=[ab])

            def comp_k(i):
                n, j2 = items[i]
                if n == 0 and j2 == 0:
                    S.dma("sp", lambda e: e.dma_start(out=hs[:], in_=fv[:, :, 0:HW]), writes=[b_hs])
                    S.dma("sp", lambda e: e.dma_start(out=hd[:], in_=fv[:, :, 2048:2048 + HW]), writes=[b_hd])
                if n == 0 and j2 == 9:
                    S.dma("sp", lambda e: e.dma_start(out=hs[:], in_=fv[:, :, HW:2 * HW]), writes=[b_hs])
                if n == 1 and j2 == 0:
                    S.dma("sp", lambda e: e.dma_start(out=hd[:], in_=fv[:, :, 2048 + HW:2048 + 2 * HW]), writes=[b_hd])
                a, ab = ftA.get(i)
                for fl in range(2):
                    fch = j2 * 2 + fl
                    src = hs if fch < 16 else hd
                    pr = (i * 2 + fl) % 3
                    bset = [pr * 2, pr * 2 + 1]
                    for hh in range(2):
                        bi = bset[hh]
                        for tc in range(16):
                            S.op("pe", lambda e, a=a, tc=tc, fl=fl, hh=hh, bi=bi, src=src, fch=fch: e.matmul(banks[bi][:, :], lhsT=a[:, tc, fl * 128:(fl + 1) * 128], rhs=src[:, tc, hh * 512:(hh + 1) * 512], start=(tc == 0), stop=(tc == 15 and fch != 16)),
                                 reads=[ab, b_hs if fch < 16 else b_hd], writes=[bkb[bi]])
                        if fch == 16:
                            for tc in range(16):
                                S.op("pe", lambda e, tc=tc, hh=hh, bi=bi: e.matmul(banks[bi][:, :], lhsT=nyq[:, tc, :], rhs=hs[:, tc, hh * 512:(hh + 1) * 512], start=False, stop=(tc == 15)),
                                     reads=[cst5, b_hs], writes=[bkb[bi]])
                    k, kb = ko.get(i * 2 + fl)
                    for hh in range(2):
                        bi = bset[hh]
                        if fch < 16:
                            S.op("dve", lambda e, k=k, hh=hh, bi=bi, n=n: e.tensor_tensor(out=k[:, hh * 512:(hh + 1) * 512], in0=banks[bi][:, :], in1=skb[:, n, hh * 512:(hh + 1) * 512], op=ALU.add),
                                 reads=[bkb[bi], cst5], writes=[kb])
                        else:
                            S.op("act", lambda e, k=k, hh=hh, bi=bi: e.activation(out=k[:, hh * 512:(hh + 1) * 512], in_=banks[bi][:, :], func=AF.Identity),
                                 reads=[bkb[bi]], writes=[kb])
                    if fch == 16:
                        S.op("dve", lambda e, k=k, n=n: e.tensor_tensor(out=k[0:1, :], in0=k[0:1, :], in1=skb[0:1, n, :], op=ALU.add), reads=[kb, cst5], writes=[kb])
                    S.dma("sp", lambda e, k=k, n=n, fch=fch: e.dma_start(out=K_d[n, fch], in_=k[:]), reads=[kb])
                advance(4)

            pipelined(len(items), load_ft, comp_k, pf=1)
            while advance(8):
                pass
            S.barrier()
        stMod.close()
        if stop <= 0:
            return finish(nc, S)

        with contextlib.ExitStack() as stA:
            hT = sb("hT", [128, 16, S_LEN], BF16, stA)
            b_hT = [[S.buf(f"hT{fc}_{tt}") for tt in range(4)] for fc in range(16)]
            with contextlib.ExitStack() as stk:
                xt = TP("xt", [128, 4, D], BF16, 2, stk)
                xv = x.rearrange("(g j p) d -> g p j d", j=4, p=128)

                def load_x(tg):
                    t, b = xt.get(tg)
                    for j in range(4):
                        S.dma("pool", lambda e, t=t, tg=tg, j=j: e.dma_start(out=t[:, j, :], in_=xv[tg][:, j, :]), writes=[b], par=(j > 0))

                def comp_x(tg):
                    t, b = xt.get(tg)
                    for fc in range(16):
                        bi = fc % 8
                        for j in range(4):
                            S.op("pe", lambda e, t=t, j=j, fc=fc, bi=bi: e.transpose(
                                out=bkbf(bi)[:, j * 128:(j + 1) * 128], in_=t[:, j, fc * 128:(fc + 1) * 128], identity=identb[:]),
                                reads=[b, b_cst], writes=[bkb[bi]])
                        S.op("act", lambda e, fc=fc, tg=tg, bi=bi: e.activation(
                            out=hT[:, fc, tg * 512:(tg + 1) * 512], in_=bkbf(bi)[:, 0:512], func=AF.Identity,
                            scale=modpp[:, 0, fc:fc + 1], bias=modpp[:, 1, fc:fc + 1]),
                            reads=[bkb[bi], b_modpp], writes=[b_hT[fc][tg]])

                pipelined(4, load_x, comp_x, pf=1)
                S.barrier()
            if debug:
                bdbg = S.buf("dbg")
                S.dma("sp", lambda e: e.dma_start(out=dbg_d[:, 0:64], in_=modpp[:].rearrange("p a b -> p (a b)")), reads=[b_modpp], key=bdbg)
                S.dma("sp", lambda e: e.dma_start(out=dbg_d[:, 64:64 + D], in_=g1_bc[:]), reads=[b_g1], key=bdbg)
                S.dma("sp", lambda e: e.dma_start(out=dbg_d[:, 64 + D:64 + 2 * D], in_=g2_bc[:]), reads=[b_g2], key=bdbg)
                S.dma("sp", lambda e: e.dma_start(out=hT_dbg.rearrange("f p t -> p f t"), in_=hT[:]), reads=[b for row in b_hT for b in row], key=bdbg)
                S.barrier()
            if stop <= 1:
                return finish(nc, S)

            def gemm_ws(Wv, cols, n_kc, rhs, n_tt, N, epilogue, stk, nm, blk=512, nslots=3):
                wt = TP("w" + nm, [128, n_kc, blk], BF16, nslots, stk)
                blocks = []
                for c in cols:
                    if blocks and blocks[-1][0] + blk > c >= blocks[-1][0] and c == blocks[-1][1][-1] + 128:
                        blocks[-1][1].append(c)
                    else:
                        blocks.append((c, [c]))

                def load(i):
                    c0, cl = blocks[i]
                    t, b = wt.get(i)
                    w = len(cl) * 128
                    S.dma("pool", lambda e, t=t, c0=c0, w=w: e.dma_start(out=t[:, :, 0:w], in_=Wv[:, :, c0:c0 + w]), writes=[b])

                load(0)
                if len(blocks) > 1:
                    load(1)
                ci = 0
                pending = [None]
                for i, (c0, cl) in enumerate(blocks):
                    if i + 2 < len(blocks):
                        load(i + 2)
                    t, b = wt.get(i)
                    for c in cl:
                        bset = [(ci % (8 // n_tt)) * n_tt + tt for tt in range(n_tt)]
                        for kc in range(n_kc):
                            for tt in range(n_tt):
                                r_ap, r_b = rhs(kc, tt)
                                S.op("pe", lambda e, t=t, kc=kc, o=c - c0, bi=bset[tt], r_ap=r_ap: e.matmul(
                                    banks[bi][:, 0:N], lhsT=t[:, kc, o:o + 128], rhs=r_ap, start=(kc == 0), stop=(kc == n_kc - 1)),
                                    reads=[b, r_b], writes=[bkb[bset[tt]]])
                        if pending[0] is not None:
                            pending[0]()
                        pending[0] = epilogue(ci, c, bset)
                        ci += 1
                if pending[0] is not None:
                    pending[0]()

            def rhs_hT(kc, tt):
                return hT[:, kc, tt * 512:(tt + 1) * 512], b_hT[kc][tt]

            with contextlib.ExitStack() as stk:
                ut = TP("ut", [128, S_LEN], BF16, 2, stk)

                def ep_ug(ci, c, bset):
                    is_u = c < HW
                    func = AF.Gelu if is_u else AF.Sigmoid
                    dst = uT_d[c // 128] if is_u else gT_d[(c - 5120) // 128]
                    t, b = ut.get(ci)
                    for tt in range(4):
                        S.op("act", lambda e, t=t, tt=tt, bi=bset[tt], ch=c // 128, func=func: e.activation(
                            out=t[:, tt * 512:(tt + 1) * 512], in_=banks[bi][:, :], func=func, bias=binpp[:, ch:ch + 1]),
                            reads=[bkb[bset[tt]], b_cst], writes=[b])
                    S.dma("sp", lambda e, t=t, dst=dst: e.dma_start(out=dst, in_=t[:]), reads=[b])

                gemm_ws(W_in_v, [i * 128 for i in range(8)] + [5120 + i * 128 for i in range(32)], 16, rhs_hT, 4, 512, ep_ug, stk, "u")
                S.barrier()
            if stop <= 2:
                return finish(nc, S)

            with contextlib.ExitStack() as stk:
                Wv_t = sb("Wv_t", [128, 16, HW], BF16, stk)
                b_Wv = S.buf("Wv")
                for hh in range(2):
                    S.dma("pool", lambda e, hh=hh: e.dma_start(out=Wv_t[:, :, hh * 512:(hh + 1) * 512], in_=W_in_v[:, :, HW + hh * 512: HW + (hh + 1) * 512]),
                          writes=[b_Wv], par=(hh > 0))
                cst2 = S.buf("cst2")
                binv = sb("binv", [128, HW], F32, stk)
                lng = sb("lng", [128, HW], F32, stk)
                lnb = sb("lnb", [128, HW], F32, stk)
                bsb = sb("bsb", [128, HW], F32, stk)
                wsf = sb("wsf", [128, 8, 128], F32, stk)
                wsb = sb("wsb", [128, 8, 128], BF16, stk)
                for dst, src in ((binv, b_in_v_bc), (lng, lng_bc), (lnb, lnb_bc), (bsb, bs_bc), (wsf, wsT)):
                    S.dma("sp", lambda e, dst=dst, src=src: e.dma_start(out=dst[:], in_=src), writes=[cst2])
                S.op("act", lambda e: e.activation(out=wsb[:], in_=wsf[:], func=AF.Identity), reads=[cst2], writes=[cst2])
                zv = TP("zv", [128, HW], F32, 3, stk)
                gv = TP("gv", [128, HW], F32, 2, stk)
                st6 = TP("st6", [128, 2, 6], F32, 2, stk)
                sm = TP("sm", [128, 8], F32, 2, stk)
                vn0 = TP("vn0", [128, HW], F32, 2, stk)
                vn = TP("vn", [128, HW], BF16, 2, stk)
                utile = TP("utile", [128, 8, 128], BF16, 5, stk)
                sgt = TP("sgt", [128, 8, 128], F32, 2, stk)
                sgo = TP("sgo", [128, 8, 128], BF16, 2, stk)
                def load_u(t_):
                    ut_, ub_ = utile.get(t_)
                    S.dma("sp", lambda e, ut_=ut_, t_=t_: e.dma_start(out=ut_[:], in_=uT_d[:, :, t_ * 128:(t_ + 1) * 128].rearrange("g p t -> p g t")), writes=[ub_])

                def zmm(t_):
                    bz = [(t_ % 3) * 2, (t_ % 3) * 2 + 1]
                    for hh in range(2):
                        for kc in range(16):
                            S.op("pe", lambda e, hh=hh, kc=kc, bi=bz[hh], t_=t_: e.matmul(
                                banks[bi][:, :], lhsT=hT[:, kc, t_ * 128:(t_ + 1) * 128], rhs=Wv_t[:, kc, hh * 512:(hh + 1) * 512],
                                start=(kc == 0), stop=(kc == 15)), reads=[b_hT[kc][t_ // 4], b_Wv], writes=[bkb[bz[hh]]])

                def chain(t_):
                    bz = [(t_ % 3) * 2, (t_ % 3) * 2 + 1]
                    zt, zb = zv.get(t_)
                    gt_, gb_ = gv.get(t_)
                    s6, s6b = st6.get(t_)
                    smt, smb = sm.get(t_)
                    v0, v0b = vn0.get(t_)
                    vt, vb = vn.get(t_)
                    for hh in range(2):
                        S.op("dve", lambda e, hh=hh, zt=zt, bi=bz[hh]: e.tensor_tensor(out=zt[:, hh * 512:(hh + 1) * 512], in0=banks[bi][:, :], in1=binv[:, hh * 512:(hh + 1) * 512], op=ALU.add),
                             reads=[bkb[bz[hh]], cst2], writes=[zb])
                    S.op("act", lambda e, zt=zt, gt_=gt_: e.activation(out=gt_[:], in_=zt[:], func=AF.Gelu), reads=[zb], writes=[gb_])
                    yield
                    for hh in range(2):
                        S.op("dve", lambda e, hh=hh, s6=s6, gt_=gt_: e.bn_stats(out=s6[:, hh, :], in_=gt_[:, hh * 512:(hh + 1) * 512]), reads=[gb_], writes=[s6b])
                    S.op("dve", lambda e: e.bn_aggr(out=smt[:, 0:2], in_=s6[:].rearrange("p a b -> p (a b)")), reads=[s6b], writes=[smb])
                    S.op("dve", lambda e: e.tensor_scalar(out=smt[:, 2:3], in0=smt[:, 1:2], scalar1=EPS, scalar2=None, op0=ALU.add), reads=[smb], writes=[smb])
                    S.op("pool", lambda e: e.tensor_tensor(out=smt[:, 4:5], in0=smt[:, 2:3], in1=eps_t[:, 1:2], op=ALU.pow), reads=[smb, b_cst], writes=[smb])
                    yield
                    S.op("dve", lambda e: e.tensor_scalar(out=smt[:, 5:6], in0=smt[:, 0:1], scalar1=-1.0, scalar2=smt[:, 4:5], op0=ALU.mult, op1=ALU.mult), reads=[smb], writes=[smb])
                    S.op("act", lambda e, gt_=gt_, v0=v0, smt=smt: e.activation(out=v0[:], in_=gt_[:], func=AF.Identity, scale=smt[:, 4:5], bias=smt[:, 5:6]),
                         reads=[gb_, smb], writes=[v0b])
                    yield
                    S.op("dve", lambda e, v0=v0: e.tensor_tensor(out=v0[:], in0=v0[:], in1=lng[:], op=ALU.mult), reads=[v0b, cst2], writes=[v0b])
                    S.op("pool", lambda e, v0=v0, vt=vt: e.tensor_tensor(out=vt[:], in0=v0[:], in1=lnb[:], op=ALU.add), reads=[v0b, cst2], writes=[vb])
                    yield

                def sgu_tail(t_):
                    ut_, ub_ = utile.get(t_)
                    vt, vb = vn.get(t_)
                    bs_ = [6, 7]
                    for g in range(8):
                        bi = bs_[g // 4]
                        S.op("pe", lambda e, g=g, bi=bi, vt=vt: e.matmul(banks[bi][:, (g % 4) * 128:(g % 4 + 1) * 128], lhsT=vt[:, g * 128:(g + 1) * 128], rhs=wsb[:, g, :], start=True, stop=True),
                             reads=[vb, cst2], writes=[bkb[bi]])
                    sg, sgb = sgt.get(t_)
                    so, sob = sgo.get(t_)
                    for hh in range(2):
                        S.op("dve", lambda e, hh=hh, sg=sg, bi=bs_[hh]: e.tensor_tensor(
                            out=sg[:, hh * 4:(hh + 1) * 4, :].rearrange("p a b -> p (a b)"), in0=banks[bi][:, :], in1=bsb[:, hh * 512:(hh + 1) * 512], op=ALU.add),
                            reads=[bkb[bs_[hh]], cst2], writes=[sgb])
                    S.op("dve", lambda e, sg=sg, so=so, ut_=ut_: e.tensor_tensor(out=so[:], in0=sg[:], in1=ut_[:], op=ALU.mult), reads=[sgb, ub_], writes=[sob])
                    S.dma("sp", lambda e, so=so, t_=t_: e.dma_start(out=sguT_d[:, :, t_ * 128:(t_ + 1) * 128].rearrange("g p t -> p g t"), in_=so[:]), reads=[sob])

                load_u(0)
                load_u(1)
                zmm(0)
                zmm(1)
                for p in range(8):
                    t0_ = 2 * p
                    if t0_ + 2 < 16:
                        load_u(t0_ + 2)
                        load_u(t0_ + 3)
                    for _ in zip(chain(t0_), chain(t0_ + 1)):
                        pass
                    if t0_ + 2 < 16:
                        zmm(t0_ + 2)
                        zmm(t0_ + 3)
                    sgu_tail(t0_)
                    sgu_tail(t0_ + 1)
                S.barrier()
            if stop <= 3:
                return finish(nc, S)

            with contextlib.ExitStack() as stk:
                cwt = sb("cwt", [128, 3, 24], F32, stk)
                cbt = sb("cbt", [128, 24], F32, stk)
                cst3 = S.buf("cst3")
                S.dma("sp", lambda e: e.dma_start(out=cwt[:], in_=cw_pp), writes=[cst3])
                S.dma("sp", lambda e: e.dma_start(out=cbt[:], in_=cb_pp), writes=[cst3])
                zp = TP("zp", [128, S_LEN + 2], F32, 2, stk)
                for i in range(2):
                    S.op("dve", lambda e, i=i: e.memset(zp.t[i][:], 0.0), writes=[zp.b[i]])
                cvA = TP("cvA", [128, S_LEN], F32, 1, stk)
                cvB = TP("cvB", [128, S_LEN], F32, 1, stk)
                cvC = TP("cvC", [128, S_LEN], BF16, 2, stk)
                cvO = TP("cvO", [128, S_LEN], BF16, 1, stk)
                tmo = TP("tmo", [128, 16, 128], BF16, 2, stk)

                def ep_h(ci, c, bset):
                    ch = c // 128
                    hc = ch - 16
                    sel = hc // 8
                    cc = hc % 8
                    z, zb = zp.get(ci)
                    for tt in range(4):
                        S.op("act", lambda e, z=z, tt=tt, bi=bset[tt], ch=ch: e.activation(
                            out=z[:, 1 + tt * 512: 1 + (tt + 1) * 512], in_=banks[bi][:, :], func=AF.Identity, bias=binpp[:, ch:ch + 1]),
                            reads=[bkb[bset[tt]], b_cst], writes=[zb])
                    a, ab = cvA.get(ci)
                    bb_, bbb = cvB.get(ci)
                    S.op("act", lambda e, z=z, a=a, hc=hc: e.activation(out=a[:], in_=z[:, 1:S_LEN + 1], func=AF.Identity, scale=cwt[:, 1, hc:hc + 1], bias=cbt[:, hc:hc + 1]),
                         reads=[zb, cst3], writes=[ab])
                    S.op("dve", lambda e, z=z, a=a, bb_=bb_, hc=hc: e.scalar_tensor_tensor(out=bb_[:], in0=z[:, 0:S_LEN], scalar=cwt[:, 0, hc:hc + 1], in1=a[:], op0=ALU.mult, op1=ALU.add),
                         reads=[zb, ab, cst3], writes=[bbb])
                    if sel == 2:
                        o, ob = cvO.get(ci)
                        S.op("dve", lambda e, z=z, bb_=bb_, o=o, hc=hc: e.scalar_tensor_tensor(out=o[:], in0=z[:, 2:S_LEN + 2], scalar=cwt[:, 2, hc:hc + 1], in1=bb_[:], op0=ALU.mult, op1=ALU.add),
                             reads=[zb, bbb, cst3], writes=[ob])
                        S.dma("sp", lambda e, o=o, cc=cc: e.dma_start(out=x2T_d[cc], in_=o[:]), reads=[ob])
                    else:
                        cv, cvb = cvC.get(ci)
                        S.op("dve", lambda e, z=z, bb_=bb_, cv=cv, hc=hc: e.scalar_tensor_tensor(out=cv[:], in0=z[:, 2:S_LEN + 2], scalar=cwt[:, 2, hc:hc + 1], in1=bb_[:], op0=ALU.mult, op1=ALU.add),
                             reads=[zb, bbb, cst3], writes=[cvb])
                        tm, tmb = tmo.get(ci)

                        def later(cv=cv, cvb=cvb, tm=tm, tmb=tmb, bset=bset, sel=sel, cc=cc):
                            for j4 in range(4):
                                bi = bset[j4]
                                for jj in range(4):
                                    j = j4 * 4 + jj
                                    S.op("pe", lambda e, cv=cv, j=j, jj=jj, bi=bi: e.transpose(out=bkbf(bi)[:, jj * 128:(jj + 1) * 128], in_=cv[:, j * 128:(j + 1) * 128], identity=identb[:]),
                                         reads=[cvb, b_cst], writes=[bkb[bi]])
                                S.op("dve", lambda e, tm=tm, j4=j4, bi=bi: e.tensor_copy(out=tm[:, j4 * 4:(j4 + 1) * 4, :].rearrange("p a b -> p (a b)"), in_=bkbf(bi)[:, 0:512]),
                                     reads=[bkb[bi]], writes=[tmb])
                            dst = v_tm_d if sel == 0 else x1_tm_d
                            S.dma("sp", lambda e, tm=tm, dst=dst, cc=cc: e.dma_start(out=dst.rearrange("(j p) c -> p j c", p=128)[:, :, cc * 128:(cc + 1) * 128], in_=tm[:]), reads=[tmb])

                        return later
                        for j4 in range(4):
                            bi = bset[j4]
                            for jj in range(4):
                                j = j4 * 4 + jj
                                S.op("pe", lambda e, cv=cv, j=j, jj=jj, bi=bi: e.transpose(out=bkbf(bi)[:, jj * 128:(jj + 1) * 128], in_=cv[:, j * 128:(j + 1) * 128], identity=identb[:]),
                                     reads=[cvb, b_cst], writes=[bkb[bi]])
                            S.op("dve", lambda e, tm=tm, j4=j4, bi=bi: e.tensor_copy(out=tm[:, j4 * 4:(j4 + 1) * 4, :].rearrange("p a b -> p (a b)"), in_=bkbf(bi)[:, 0:512]),
                                 reads=[bkb[bi]], writes=[tmb])
                        dst = v_tm_d if sel == 0 else x1_tm_d
                        S.dma("sp", lambda e, tm=tm, dst=dst, cc=cc: e.dma_start(out=dst.rearrange("(j p) c -> p j c", p=128)[:, :, cc * 128:(cc + 1) * 128], in_=tm[:]), reads=[tmb])

                gemm_ws(W_in_v, [2048 + i * 128 for i in range(24)], 16, rhs_hT, 4, 512, ep_h, stk, "h")
                S.barrier()
            if stop <= 4:
                return finish(nc, S)

        if stop <= 5:
            return finish(nc, S)

        Wm1_v = w_m1.rearrange("(kc p) n -> p kc n", p=128)
        Wm2_v = w_m2.rearrange("(kc p) n -> p kc n", p=128)
        b_wconv = S.bufs("wconv", 2)

        wc_next = [0]

        def wconv_step(n):
            lo = wc_next[0]
            hi = min(64, lo + n)
            wc_next[0] = hi
            for k in range(lo, hi):
                if k < 32:
                    S.dma("pool", lambda e, k=k: e.dma_start(out=wm1b_d[k], in_=Wm1_v[:, :, k * 256:(k + 1) * 256]), writes=[b_wconv[k % 2]])
                else:
                    j = k - 32
                    S.dma("pool", lambda e, j=j: e.dma_start(out=wm2b_d[j], in_=Wm2_v[:, (j // 16) * 32:(j // 16 + 1) * 32, (j % 16) * 128:(j % 16 + 1) * 128]), writes=[b_wconv[k % 2]])

        with contextlib.ExitStack() as stC:
            Y = sb("Y", [128, 32, HW], BF16, stC)
            b_Y = [S.buf(f"Y{j}") for j in range(16)]
            utm = sb("utm", [128, 16, HW], BF16, stC)
            b_utm = [S.buf(f"utm{j}") for j in range(16)]

            def forward(n, stk):
                ftA = TP(f"cA{n}", [128, 16, 256], BF16, 2, stk)
                ftB = TP(f"cB{n}", [128, 16, 256], BF16, 2, stk)
                kre = TP(f"kre{n}", [128, 2, HW], F32, 2, stk)
                kim = TP(f"kim{n}", [128, 2, HW], F32, 2, stk)
                tmp = TP(f"ctmp{n}", [128, 4, 512], F32, 2, stk)

                def load(j2):
                    a, ab = ftA.get(j2)
                    b2, bb2 = ftB.get(j2)
                    kr, krb = kre.get(j2)
                    ki, kib = kim.get(j2)
                    S.dma("sp", lambda e, a=a, j2=j2: e.dma_start(out=a[:], in_=FTb[j2]), writes=[ab])
                    S.dma("sp", lambda e, b2=b2, j2=j2: e.dma_start(out=b2[:], in_=FTb[8 + j2]), writes=[bb2])
                    S.dma("sp", lambda e, kr=kr, j2=j2, n=n: e.dma_start(out=kr[:], in_=K_d[n, 2 * j2:2 * j2 + 2].rearrange("a p c -> p a c")), writes=[krb])
                    S.dma("sp", lambda e, ki=ki, j2=j2, n=n: e.dma_start(out=ki[:], in_=K_d[n, 16 + 2 * j2:16 + 2 * j2 + 2].rearrange("a p c -> p a c")), writes=[kib])

                def comp(j2):
                    wconv_step(2)
                    a, ab = ftA.get(j2)
                    b2, bb2 = ftB.get(j2)
                    kr, krb = kre.get(j2)
                    ki, kib = kim.get(j2)
                    for fl in range(2):
                        j = j2 * 2 + fl
                        base = (j % 2) * 4
                        for ri, (ft, ftb) in enumerate(((a, ab), (b2, bb2))):
                            for hh in range(2):
                                bi = base + ri * 2 + hh
                                for tc in range(16):
                                    S.op("pe", lambda e, ft=ft, tc=tc, fl=fl, hh=hh, bi=bi: e.matmul(banks[bi][:, :], lhsT=ft[:, tc, fl * 128:(fl + 1) * 128], rhs=utm[:, tc, hh * 512:(hh + 1) * 512], start=(tc == 0), stop=(tc == 15)),
                                         reads=[ftb, b_utm[tc]], writes=[bkb[bi]])
                        for hh in range(2):
                            t, tb = tmp.get(j * 2 + hh)
                            ure, uim = base + hh, base + 2 + hh
                            cs = slice(hh * 512, (hh + 1) * 512)
                            S.op("dve", lambda e, t=t, ure=ure, kr=kr, fl=fl, cs=cs: e.tensor_tensor(out=t[:, 0, :], in0=banks[ure][:, :], in1=kr[:, fl, cs], op=ALU.mult), reads=[bkb[ure], krb], writes=[tb])
                            S.op("dve", lambda e, t=t, uim=uim, ki=ki, fl=fl, cs=cs: e.tensor_tensor(out=t[:, 1, :], in0=banks[uim][:, :], in1=ki[:, fl, cs], op=ALU.mult), reads=[bkb[uim], kib], writes=[tb])
                            S.op("dve", lambda e, t=t, ure=ure, ki=ki, fl=fl, cs=cs: e.tensor_tensor(out=t[:, 2, :], in0=banks[ure][:, :], in1=ki[:, fl, cs], op=ALU.mult), reads=[bkb[ure], kib], writes=[tb])
                            S.op("dve", lambda e, t=t, uim=uim, kr=kr, fl=fl, cs=cs: e.tensor_tensor(out=t[:, 3, :], in0=banks[uim][:, :], in1=kr[:, fl, cs], op=ALU.mult), reads=[bkb[uim], krb], writes=[tb])
                            S.op("dve", lambda e, t=t, j=j, cs=cs: e.tensor_tensor(out=Y[:, j, cs], in0=t[:, 0, :], in1=t[:, 1, :], op=ALU.subtract), reads=[tb], writes=[b_Y[j]])
                            S.op("dve", lambda e, t=t, j=j, cs=cs: e.tensor_tensor(out=Y[:, 16 + j, cs], in0=t[:, 2, :], in1=t[:, 3, :], op=ALU.add), reads=[tb], writes=[b_Y[j]])
                            if j == 0:
                                S.op("dve", lambda e, ure=ure, kr=kr, cs=cs: e.tensor_tensor(out=Y[0:1, 0, cs], in0=banks[ure][0:1, :], in1=kr[0:1, 0, cs], op=ALU.mult), reads=[bkb[ure], krb, b_Y[0]], writes=[b_Y[0]])
                                S.op("dve", lambda e, uim=uim, ki=ki, cs=cs: e.tensor_tensor(out=Y[0:1, 16, cs], in0=banks[uim][0:1, :], in1=ki[0:1, 0, cs], op=ALU.mult), reads=[bkb[uim], kib, b_Y[0]], writes=[b_Y[0]])

                pipelined(8, load, comp, pf=1)
                S.barrier()

            with contextlib.ExitStack() as stk:
                S.dma("sp", lambda e: e.dma_start(out=utm[:], in_=v_tm_d.rearrange("(tc p) c -> p tc c", p=128)), writes=b_utm)
                forward(0, stk)
            with contextlib.ExitStack() as stk:
                gt = TP("gt1", [128, 32, 256], BF16, 2, stk)
                x1t = TP("x1t", [128, HW], BF16, 3, stk)

                def load_i1(tc):
                    if tc % 2 == 0:
                        g, gb = gt.get(tc // 2)
                        S.dma("sp", lambda e, g=g, tb_=tc // 2: e.dma_start(out=g[:], in_=GTb[tb_]), writes=[gb])
                    xt_, xb_ = x1t.get(tc)
                    S.dma("sp", lambda e, xt_=xt_, tc=tc: e.dma_start(out=xt_[:], in_=x1_tm_d[tc * 128:(tc + 1) * 128, :]), writes=[xb_])

                def comp_i1(tc):
                    wconv_step(1)
                    g, gb = gt.get(tc // 2)
                    tl = tc % 2
                    xt_, xb_ = x1t.get(tc)
                    for hh in range(2):
                        bi = (tc % 4) * 2 + hh
                        for fc in range(32):
                            S.op("pe", lambda e, g=g, fc=fc, tl=tl, hh=hh, bi=bi: e.matmul(banks[bi][:, :], lhsT=g[:, fc, tl * 128:(tl + 1) * 128], rhs=Y[:, fc, hh * 512:(hh + 1) * 512], start=(fc == 0), stop=(fc == 31)),
                                 reads=[gb, b_Y[fc % 16]], writes=[bkb[bi]])
                        S.op("dve", lambda e, xt_=xt_, tc=tc, hh=hh, bi=bi: e.tensor_tensor(out=utm[:, tc, hh * 512:(hh + 1) * 512], in0=banks[bi][:, :], in1=xt_[:, hh * 512:(hh + 1) * 512], op=ALU.mult),
                             reads=[bkb[bi], xb_], writes=[b_utm[tc]])

                pipelined(16, load_i1, comp_i1, pf=1)
                S.barrier()
            with contextlib.ExitStack() as stk:
                forward(1, stk)
            with contextlib.ExitStack() as stk:
                gt = TP("gt2_", [128, 32, 256], BF16, 2, stk)
                x2t = TP("x2t", [128, 256], BF16, 4, stk)
                ybo = TP("ybo", [128, 256], BF16, 3, stk)

                def load_i2(it):
                    tb_, cc = it // 8, it % 8
                    if cc == 0:
                        g, gb = gt.get(tb_)
                        S.dma("sp", lambda e, g=g, tb_=tb_: e.dma_start(out=g[:], in_=GTb[tb_]), writes=[gb])
                    xt_, xb_ = x2t.get(it)
                    S.dma("sp", lambda e, xt_=xt_, cc=cc, tb_=tb_: e.dma_start(out=xt_[:], in_=x2T_d[cc][:, tb_ * 256:(tb_ + 1) * 256]), writes=[xb_])

                def comp_i2(it):
                    if it % 4 == 0:
                        wconv_step(1)
                    tb_, cc = it // 8, it % 8
                    g, gb = gt.get(tb_)
                    xt_, xb_ = x2t.get(it)
                    bi = it % 8
                    for fc in range(32):
                        S.op("pe", lambda e, g=g, fc=fc, cc=cc, bi=bi: e.matmul(banks[bi][:, 0:256], lhsT=Y[:, fc, cc * 128:(cc + 1) * 128], rhs=g[:, fc, :], start=(fc == 0), stop=(fc == 31)),
                             reads=[gb, b_Y[fc % 16]], writes=[bkb[bi]])
                    yo, yob = ybo.get(it)
                    S.op("dve", lambda e, yo=yo, xt_=xt_, bi=bi: e.tensor_tensor(out=yo[:], in0=banks[bi][:, 0:256], in1=xt_[:], op=ALU.mult), reads=[bkb[bi], xb_], writes=[yob])
                    S.dma("sp", lambda e, yo=yo, cc=cc, tb_=tb_: e.dma_start(out=ybT_d[cc][:, tb_ * 256:(tb_ + 1) * 256], in_=yo[:]), reads=[yob])

                pipelined(64, load_i2, comp_i2, pf=3)
                wconv_step(64)
                S.barrier()
        if stop <= 10:
            return finish(nc, S)

        with contextlib.ExitStack() as stM:
            mT = sb("mT", [128, 16, S_LEN], BF16, stM)
            b_mT = [[S.buf(f"mT{fc}_{th}") for th in range(2)] for fc in range(16)]
            with contextlib.ExitStack() as stk:
                sgu = sb("sgu", [128, 8, S_LEN], BF16, stk)
                ybT = sb("ybT", [128, 8, S_LEN], BF16, stk)
                b_sg = S.buf("sgu")
                b_yb = S.buf("ybT")
                S.dma("sp", lambda e: e.dma_start(out=sgu[:], in_=sguT_d.rearrange("g p t -> p g t")), writes=[b_sg])
                S.dma("sp", lambda e: e.dma_start(out=ybT[:], in_=ybT_d.rearrange("g p t -> p g t")), writes=[b_yb])
                wa = TP("wa", [128, 8, 256], BF16, 2, stk)
                wb_ = TP("wbb", [128, 8, 256], BF16, 2, stk)
                ga = TP("ga", [128, S_LEN], BF16, 2, stk)
                gbt = TP("gbt", [128, S_LEN], BF16, 2, stk)
                mt = TP("mtmp", [128, 2, 512], F32, 3, stk)
                Wa_v = w_a.rearrange("(kc p) n -> p kc n", p=128)
                Wb_v = w_b.rearrange("(kc p) n -> p kc n", p=128)

                def load_ab(blk):
                    wat, wab = wa.get(blk)
                    wbt, wbb = wb_.get(blk)
                    S.dma("pool", lambda e, wat=wat, blk=blk: e.dma_start(out=wat[:], in_=Wa_v[:, :, blk * 256:(blk + 1) * 256]), writes=[wab])
                    S.dma("pool", lambda e, wbt=wbt, blk=blk: e.dma_start(out=wbt[:], in_=Wb_v[:, :, blk * 256:(blk + 1) * 256]), writes=[wbb])

                def load_g(fc):
                    gat, gab = ga.get(fc)
                    gbt_, gbb = gbt.get(fc)
                    S.dma("sp", lambda e, gat=gat, fc=fc: e.dma_start(out=gat[:], in_=gT_d[fc]), writes=[gab])
                    S.dma("sp", lambda e, gbt_=gbt_, fc=fc: e.dma_start(out=gbt_[:], in_=gT_d[16 + fc]), writes=[gbb])

                pidx = 0
                load_ab(0)
                load_g(0)
                for blk in range(8):
                    if blk + 1 < 8:
                        load_ab(blk + 1)
                    wat, wab = wa.get(blk)
                    wbt, wbb = wb_.get(blk)
                    for fl in range(2):
                        fc = blk * 2 + fl
                        if fc + 1 < 16:
                            load_g(fc + 1)
                        gat, gab = ga.get(fc)
                        gbt_, gbb = gbt.get(fc)
                        for th in range(2):
                            base = (pidx % 2) * 4
                            for (wt_, wtb, src, srcb, off) in ((wat, wab, sgu, b_sg, 0), (wbt, wbb, ybT, b_yb, 2)):
                                for kc in range(8):
                                    for tl in range(2):
                                        bi = base + off + tl
                                        tt = th * 2 + tl
                                        S.op("pe", lambda e, wt_=wt_, kc=kc, fl=fl, src=src, tt=tt, bi=bi: e.matmul(banks[bi][:, :], lhsT=wt_[:, kc, fl * 128:(fl + 1) * 128], rhs=src[:, kc, tt * 512:(tt + 1) * 512], start=(kc == 0), stop=(kc == 7)),
                                             reads=[wtb, srcb], writes=[bkb[bi]])
                            for tl in range(2):
                                tt = th * 2 + tl
                                t, tb = mt.get(pidx * 2 + tl)
                                cs = slice(tt * 512, (tt + 1) * 512)
                                S.op("dve", lambda e, t=t, gat=gat, cs=cs, bi=base + tl: e.tensor_tensor(out=t[:, 0, :], in0=banks[bi][:, :], in1=gat[:, cs], op=ALU.mult), reads=[bkb[base + tl], gab], writes=[tb])
                                S.op("dve", lambda e, t=t, gbt_=gbt_, cs=cs, bi=base + 2 + tl: e.tensor_tensor(out=t[:, 1, :], in0=banks[bi][:, :], in1=gbt_[:, cs], op=ALU.mult), reads=[bkb[base + 2 + tl], gbb], writes=[tb])
                                S.op("dve", lambda e, t=t, fc=fc, cs=cs: e.tensor_tensor(out=mT[:, fc, cs], in0=t[:, 0, :], in1=t[:, 1, :], op=ALU.add), reads=[tb], writes=[b_mT[fc][th]])
                            pidx += 1
                S.barrier()
            if stop <= 11:
                return finish(nc, S)
            with contextlib.ExitStack() as stk:
                cst6 = S.buf("cst6")
                g1bo = sb("g1bo", [128, D], F32, stk)
                S.dma("sp", lambda e: e.dma_start(out=g1bo[:], in_=bo_bc), writes=[cst6])
                S.op("dve", lambda e: e.tensor_tensor(out=g1bo[:], in0=g1bo[:], in1=g1_bc[:], op=ALU.mult), reads=[cst6, b_g1], writes=[cst6])
                wo = TP("wo", [128, 16, 512], BF16, 2, stk)
                xbk = TP("xbk", [128, 4, 512], F32, 3, stk)
                tbk = TP("tbk", [128, 4, 512], F32, 2, stk)
                Wo_v = w_o.rearrange("(kc p) n -> p kc n", p=128)
                xq = x.rearrange("(q j p) d -> q p j d", j=4, p=128)
                x1q = x1_d.rearrange("(q j p) d -> q p j d", j=4, p=128)

                def load_wo(cb):
                    t, b = wo.get(cb)
                    S.dma("pool", lambda e, t=t, cb=cb: e.dma_start(out=t[:], in_=Wo_v[:, :, cb * 512:(cb + 1) * 512]), writes=[b])

                def load_xb(it):
                    cb, q = it // 4, it % 4
                    xt_, xb_ = xbk.get(it)
                    S.dma("sp", lambda e, xt_=xt_, q=q, cb=cb: e.dma_start(out=xt_[:], in_=xq[q][:, :, cb * 512:(cb + 1) * 512]), writes=[xb_])

                def comp_xb(it):
                    cb, q = it // 4, it % 4
                    if q == 0 and cb + 1 < 4:
                        load_wo(cb + 1)
                    wt_, wtb = wo.get(cb)
                    cs = slice(cb * 512, (cb + 1) * 512)
                    xt_, xb_ = xbk.get(it)
                    tt2, tb2 = tbk.get(it)
                    for j in range(4):
                        tc = q * 4 + j
                        bi = (it * 4 + j) % 8
                        for kc in range(16):
                            S.op("pe", lambda e, kc=kc, tc=tc, wt_=wt_, bi=bi: e.matmul(banks[bi][:, :], lhsT=mT[:, kc, tc * 128:(tc + 1) * 128], rhs=wt_[:, kc, :], start=(kc == 0), stop=(kc == 15)),
                                 reads=[b_mT[kc][tc // 8], wtb], writes=[bkb[bi]])
                        S.op("dve", lambda e, tt2=tt2, bi=bi, cs=cs, j=j: e.tensor_tensor(out=tt2[:, j, :], in0=banks[bi][:, :], in1=g1_bc[:, cs], op=ALU.mult), reads=[bkb[bi], b_g1], writes=[tb2])
                        S.op("dve", lambda e, xt_=xt_, cs=cs, j=j: e.scalar_tensor_tensor(out=xt_[:, j, :], in0=xt_[:, j, :], scalar=ALPHA, in1=g1bo[:, cs], op0=ALU.mult, op1=ALU.add), reads=[xb_, cst6], writes=[xb_])
                    S.op("dve", lambda e, tt2=tt2, xt_=xt_: e.tensor_tensor(out=tt2[:], in0=tt2[:], in1=xt_[:], op=ALU.add), reads=[tb2, xb_], writes=[tb2])
                    S.dma("sp", lambda e, tt2=tt2, q=q, cs=cs: e.dma_start(out=x1q[q][:, :, cs], in_=tt2[:]), reads=[tb2])

                load_wo(0)
                pipelined(16, load_xb, comp_xb, pf=2)
                S.barrier()
        if stop <= 12:
            return finish(nc, S)

        with contextlib.ExitStack() as stk:
            cst6 = S.buf("cst6c")
            l1g = sb("l1g", [128, D], F32, stk)
            l1b = sb("l1b", [128, D], F32, stk)
            lpp = sb("lpp", [128, 4, 16], F32, stk)
            S.dma("sp", lambda e: e.dma_start(out=l1g[:], in_=ln1g_bc), writes=[cst6])
            S.dma("sp", lambda e: e.dma_start(out=l1b[:], in_=ln1b_bc), writes=[cst6])
            S.dma("sp", lambda e: e.dma_start(out=lpp[:, 0, :], in_=ln1g_pp), writes=[cst6])
            S.dma("sp", lambda e: e.dma_start(out=lpp[:, 1, :], in_=ln1b_pp), writes=[cst6])
            S.op("dve", lambda e: e.tensor_tensor(out=lpp[:, 2, :], in0=lpp[:, 0, :], in1=modpp[:, 2, :], op=ALU.mult), reads=[cst6, b_modpp], writes=[cst6])
            S.op("dve", lambda e: e.tensor_tensor(out=lpp[:, 3, :], in0=lpp[:, 1, :], in1=modpp[:, 2, :], op=ALU.mult), reads=[cst6, b_modpp], writes=[cst6])
            S.op("dve", lambda e: e.tensor_tensor(out=lpp[:, 3, :], in0=lpp[:, 3, :], in1=modpp[:, 3, :], op=ALU.add), reads=[cst6, b_modpp], writes=[cst6])
            rr = TP("rr", [128, D], F32, 3, stk)
            rn = TP("rn", [128, D], F32, 2, stk)
            x1o = TP("x1o", [128, D], F32, 2, stk)
            h2q = TP("h2q", [128, D], BF16, 8, stk)
            s6 = TP("s6b", [128, 4, 6], F32, 2, stk)
            sm = TP("smb", [128, 8], F32, 3, stk)
            h2o = TP("h2o", [128, 16, 512], BF16, 2, stk)

            def load_r(tc):
                r, rb = rr.get(tc)
                S.dma("sp", lambda e, r=r, tc=tc: e.dma_start(out=r[:], in_=x1_d[tc * 128:(tc + 1) * 128, :]), writes=[rb])

            def stage_a1(tc):
                r, rb = rr.get(tc)
                s6t, s6b_ = s6.get(tc)
                smt, smb = sm.get(tc)
                for cb in range(4):
                    S.op("dve", lambda e, s6t=s6t, r=r, cb=cb: e.bn_stats(out=s6t[:, cb, :], in_=r[:, cb * 512:(cb + 1) * 512]), reads=[rb], writes=[s6b_])
                S.op("dve", lambda e: e.bn_aggr(out=smt[:, 0:2], in_=s6t[:].rearrange("p a b -> p (a b)")), reads=[s6b_], writes=[smb])
                S.op("dve", lambda e: e.tensor_scalar(out=smt[:, 2:3], in0=smt[:, 1:2], scalar1=EPS, scalar2=None, op0=ALU.add), reads=[smb], writes=[smb])
                S.op("pool", lambda e: e.tensor_tensor(out=smt[:, 4:5], in0=smt[:, 2:3], in1=eps_t[:, 1:2], op=ALU.pow), reads=[smb, b_cst], writes=[smb])

            def stage_a2(tc):
                r, rb = rr.get(tc)
                smt, smb = sm.get(tc)
                rnt, rnb = rn.get(tc)
                hq, hqb = h2q.get(tc)
                S.op("dve", lambda e: e.tensor_scalar(out=smt[:, 5:6], in0=smt[:, 0:1], scalar1=-1.0, scalar2=smt[:, 4:5], op0=ALU.mult, op1=ALU.mult), reads=[smb], writes=[smb])
                S.op("act", lambda e, r=r, rnt=rnt, smt=smt: e.activation(out=rnt[:], in_=r[:], func=AF.Identity, scale=smt[:, 4:5], bias=smt[:, 5:6]), reads=[rb, smb], writes=[rnb])
                S.op("act", lambda e, r=r, hq=hq, smt=smt: e.activation(out=hq[:], in_=r[:], func=AF.Identity, scale=smt[:, 4:5], bias=smt[:, 5:6]), reads=[rb, smb], writes=[hqb])

            def stage_x(tc):
                rnt, rnb = rn.get(tc)
                xo, xob = x1o.get(tc)
                S.op("dve", lambda e, rnt=rnt, xo=xo: e.tensor_tensor(out=xo[:], in0=rnt[:], in1=l1g[:], op=ALU.mult), reads=[rnb, cst6], writes=[xob])
                S.op("dve", lambda e, xo=xo: e.tensor_tensor(out=xo[:], in0=xo[:], in1=l1b[:], op=ALU.add), reads=[xob, cst6], writes=[xob])
                S.dma("sp", lambda e, xo=xo, tc=tc: e.dma_start(out=x1_d[tc * 128:(tc + 1) * 128, :], in_=xo[:]), reads=[xob])

            def group(g):
                ho, hob = h2o.get(g)
                for j in range(4):
                    hq, hqb = h2q.get(g * 4 + j)
                    for fc in range(16):
                        bi = fc // 2
                        S.op("pe", lambda e, hq=hq, fc=fc, bi=bi, j=j: e.transpose(out=bkbf(bi)[:, (fc % 2) * 512 + j * 128:(fc % 2) * 512 + (j + 1) * 128], in_=hq[:, fc * 128:(fc + 1) * 128], identity=identb[:]),
                             reads=[hqb, b_cst], writes=[bkb[bi]])
                for fc in range(16):
                    bi = fc // 2
                    S.op("act", lambda e, ho=ho, fc=fc, bi=bi: e.activation(out=ho[:, fc, :], in_=bkbf(bi)[:, (fc % 2) * 512:(fc % 2 + 1) * 512], func=AF.Identity, scale=lpp[:, 2, fc:fc + 1], bias=lpp[:, 3, fc:fc + 1]),
                         reads=[bkb[bi], cst6], writes=[hob])
                S.dma("sp", lambda e, ho=ho, g=g: e.dma_start(out=h2T_d[:, :, g * 512:(g + 1) * 512].rearrange("f p t -> p f t"), in_=ho[:]), reads=[hob])

            load_r(0)
            load_r(1)
            stage_a1(0)
            stage_a2(0)
            for tc in range(16):
                if tc + 2 < 16:
                    load_r(tc + 2)
                if tc + 1 < 16:
                    stage_a1(tc + 1)
                stage_x(tc)
                if tc + 1 < 16:
                    stage_a2(tc + 1)
                if tc % 4 == 3:
                    group(tc // 4)
            S.barrier()
        if stop <= 13:
            return finish(nc, S)

        stG1.close()
        with contextlib.ExitStack() as stk:
            cst7 = S.buf("cst7")
            bm1 = sb("bm1", [128, 64], F32, stk)
            bm2 = sb("bm2", [128, 16], F32, stk)
            l2g = sb("l2g", [128, D], F32, stk)
            l2b = sb("l2b", [128, D], F32, stk)
            for dst, src in ((bm1, b_m1_pp), (bm2, b_m2_pp), (l2g, ln2g_bc), (l2b, ln2b_bc)):
                S.dma("sp", lambda e, dst=dst, src=src: e.dma_start(out=dst[:], in_=src), writes=[cst7])
            h2 = TP("h2", [128, 16, 512], BF16, 1, stk)
            aT = sb("aT", [128, 32, 512], BF16, stk)
            b_aT = S.bufs("aT", 32)
            ar = TP("ar", [128, 512], BF16, 3, stk)
            fTs = sb("fTs", [128, 16, 512], F32, stk)
            b_fT = S.bufs("fTs", 16)
            wm1 = TP("wm1", [128, 16, 256], BF16, 3, stk)
            fTb = sb("fTb", [128, 16, 512], BF16, stk)
            wm2 = TP("wm2", [128, 32, 128], BF16, 3, stk)
            x1in = TP("x1in", [128, D], F32, 2, stk)
            wk = TP("wk", [128, D], F32, 2, stk)
            s6 = TP("s6c", [128, 4, 6], F32, 2, stk)
            sm = TP("smc", [128, 8], F32, 2, stk)
            wl = []
            for tq in range(4):
                for hf_ in range(2):
                    for blk in range(16):
                        wl.append(("m1", hf_ * 16 + blk))
                    for fc in range(16):
                        wl.append(("m2", hf_, fc))
            cnt = {"m1": 0, "m2": 0}
            slot_of = []
            for ent in wl:
                slot_of.append(cnt[ent[0]])
                cnt[ent[0]] += 1
            issued = [0]

            def need(i, dist=2):
                while issued[0] <= min(i + dist, len(wl) - 1):
                    k = issued[0]
                    ent = wl[k]
                    if ent[0] == "m1":
                        t, b = wm1.get(slot_of[k])
                        dsc = wm1b_d[ent[1]]
                    else:
                        t, b = wm2.get(slot_of[k])
                        dsc = wm2b_d[ent[1] * 16 + ent[2]]
                    S.dma("pool", lambda e, t=t, dsc=dsc: e.dma_start(out=t[:], in_=dsc), writes=[b])
                    issued[0] += 1

            def load_x1(k):
                xt_, xb_ = x1in.get(k)
                S.dma("sp", lambda e, xt_=xt_, k=k: e.dma_start(out=xt_[:], in_=x1_d[k * 128:(k + 1) * 128, :]), writes=[xb_])

            st_ = {"wi": 0, "ai": 0}
            ht, hb_ = h2.get(0)

            def m1_half(tq, hf_):
                for blk in range(16):
                    need(st_["wi"])
                    wt_, wtb = wm1.get(slot_of[st_["wi"]])
                    st_["wi"] += 1
                    for fl in range(2):
                        fl_c = blk * 2 + fl
                        ffc = hf_ * 32 + fl_c
                        bi = fl_c % 2
                        for kc in range(16):
                            S.op("pe", lambda e, wt_=wt_, kc=kc, fl=fl, bi=bi: e.matmul(banks[bi][:, :], lhsT=wt_[:, kc, fl * 128:(fl + 1) * 128], rhs=ht[:, kc, :], start=(kc == 0), stop=(kc == 15)),
                                 reads=[wtb, hb_], writes=[bkb[bi]])
                        a, ab = ar.get(st_["ai"])
                        st_["ai"] += 1
                        S.op("act", lambda e, a=a, bi=bi, ffc=ffc: e.activation(out=a[:], in_=banks[bi][:, :], func=AF.Relu, bias=bm1[:, ffc:ffc + 1]), reads=[bkb[bi], cst7], writes=[ab])
                        S.op("dve", lambda e, a=a, fl_c=fl_c: e.tensor_tensor(out=aT[:, fl_c, :], in0=a[:], in1=a[:], op=ALU.mult), reads=[ab], writes=[b_aT[fl_c]])

            def m2_half(tq, hf_, hooks=None):
                for fc in range(16):
                    if hooks and fc % 3 == 1 and hooks:
                        hooks.pop(0)()
                    need(st_["wi"])
                    wt_, wtb = wm2.get(slot_of[st_["wi"]])
                    st_["wi"] += 1
                    bi = 2 + fc % 2
                    for kc in range(32):
                        S.op("pe", lambda e, wt_=wt_, kc=kc, bi=bi: e.matmul(banks[bi][:, :], lhsT=wt_[:, kc, :], rhs=aT[:, kc, :], start=(kc == 0), stop=(kc == 31)),
                             reads=[wtb, b_aT[kc]], writes=[bkb[bi]])
                    if hf_ == 0:
                        S.op("act", lambda e, fc=fc, bi=bi: e.activation(out=fTs[:, fc, :], in_=banks[bi][:, :], func=AF.Identity, bias=bm2[:, fc:fc + 1]), reads=[bkb[bi], cst7], writes=[b_fT[fc]])
                    else:
                        S.op("dve", lambda e, fc=fc, bi=bi: e.tensor_tensor(out=fTb[:, fc, :], in0=banks[bi][:, :], in1=fTs[:, fc, :], op=ALU.add), reads=[bkb[bi], b_fT[fc]], writes=[b_fT[fc]])

            def tail_A(tc):
                j = tc % 4
                xt_, xb_ = x1in.get(tc)
                for fc in range(16):
                    bi = 4 + fc // 4
                    S.op("pe", lambda e, fc=fc, j=j, bi=bi: e.transpose(out=bkbf(bi)[:, (fc % 4) * 128:(fc % 4 + 1) * 128], in_=fTb[:, fc, j * 128:(j + 1) * 128], identity=identb[:]),
                         reads=[b_fT[fc], b_cst], writes=[bkb[bi]])
                w, wb2 = wk.get(tc)
                for cb in range(4):
                    cs = slice(cb * 512, (cb + 1) * 512)
                    S.op("dve", lambda e, w=w, cb=cb, cs=cs: e.tensor_tensor(out=w[:, cs], in0=bkbf(4 + cb)[:, 0:512], in1=g2_bc[:, cs], op=ALU.mult), reads=[bkb[4 + cb], b_g2], writes=[wb2])
                S.op("dve", lambda e, xt_=xt_, w=w: e.scalar_tensor_tensor(out=w[:], in0=xt_[:], scalar=ALPHA, in1=w[:], op0=ALU.mult, op1=ALU.add), reads=[xb_, wb2], writes=[wb2])

            def tail_B(tc):
                w, wb2 = wk.get(tc)
                s6t, s6b_ = s6.get(tc)
                smt, smb = sm.get(tc)
                for cb in range(4):
                    S.op("dve", lambda e, s6t=s6t, w=w, cb=cb: e.bn_stats(out=s6t[:, cb, :], in_=w[:, cb * 512:(cb + 1) * 512]), reads=[wb2], writes=[s6b_])
                ln_tail(S, s6t, s6b_, smt, smb, eps_t, b_cst)
                S.op("act", lambda e, w=w, smt=smt: e.activation(out=w[:], in_=w[:], func=AF.Identity, scale=smt[:, 4:5], bias=smt[:, 5:6]), reads=[wb2, smb], writes=[wb2])
                S.op("dve", lambda e, w=w: e.tensor_tensor(out=w[:], in0=w[:], in1=l2g[:], op=ALU.mult), reads=[wb2, cst7], writes=[wb2])
                S.op("dve", lambda e, w=w: e.tensor_tensor(out=w[:], in0=w[:], in1=l2b[:], op=ALU.add), reads=[wb2, cst7], writes=[wb2])
                S.dma("sp", lambda e, w=w, tc=tc: e.dma_start(out=out[tc * 128:(tc + 1) * 128, :], in_=w[:]), reads=[wb2])

            def tail_steps(tq):
                base = tq * 4

                def s0():
                    tail_A(base)

                def mk(j):
                    def f():
                        if j + 1 < 4:
                            tail_A(base + j + 1)
                        tail_B(base + j)
                        if j + 2 < 4:
                            load_x1(base + j + 2)
                    return f

                return [s0] + [mk(j) for j in range(4)]

            for tq in range(4):
                S.dma("sp", lambda e, tq=tq: e.dma_start(out=ht[:], in_=h2T_d[:, :, tq * 512:(tq + 1) * 512].rearrange("f p t -> p f t")), writes=[hb_])
                hooks = None
                if tq > 0:
                    load_x1((tq - 1) * 4)
                    load_x1((tq - 1) * 4 + 1)
                    hooks = tail_steps(tq - 1)
                m1_half(tq, 0)
                m2_half(tq, 0, hooks)
                assert not hooks
                m1_half(tq, 1)
                m2_half(tq, 1)
            load_x1(12)
            load_x1(13)
            for f in tail_steps(3):
                f()
            S.barrier()
        return finish(nc, S)


def pipelined(n, load, compute, pf=1):
    for i in range(min(pf, n)):
        load(i)
    for i in range(n):
        if i + pf < n:
            load(i + pf)
        compute(i)


def ln_tail(S, s6, s6b, smt, smb, eps_t, b_cst):
    S.op("dve", lambda e: e.bn_aggr(out=smt[:, 0:2], in_=s6[:].rearrange("p a b -> p (a b)")), reads=[s6b], writes=[smb])
    S.op("dve", lambda e: e.tensor_scalar(out=smt[:, 2:3], in0=smt[:, 1:2], scalar1=EPS, scalar2=None, op0=ALU.add), reads=[smb], writes=[smb])
    S.op("pool", lambda e: e.tensor_tensor(out=smt[:, 4:5], in0=smt[:, 2:3], in1=eps_t[:, 1:2], op=ALU.pow), reads=[smb, b_cst], writes=[smb])
    S.op("dve", lambda e: e.tensor_scalar(out=smt[:, 5:6], in0=smt[:, 0:1], scalar1=-1.0, scalar2=smt[:, 4:5], op0=ALU.mult, op1=ALU.mult), reads=[smb], writes=[smb])


def finish(nc, S):
    S.barrier()
    with nc.Block() as block:
        S.emit(block)
    return nc


_CONST = {}


def _constants():
    if _CONST:
        return _CONST
    N, L = NFFT, S_LEN
    t = np.arange(L, dtype=np.int64)
    r = np.arange(N, dtype=np.int64)
    f = np.where(r < L, r, r - L)
    ang = 2.0 * np.pi * ((f[:, None] * t[None, :]) % N).astype(np.float64) / N
    cosm = np.cos(ang)
    sinm = np.sin(ang)
    is_re = (r < L)[:, None]
    nyq = (r == L)[:, None]
    alt = np.where(t % 2 == 0, 1.0, -1.0)[None, :]
    F = np.where(is_re, cosm, np.where(nyq, alt, -sinm))
    scale = np.where((f == 0), 1.0 / N, 2.0 / N)[:, None]
    G = F * scale
    FT = np.ascontiguousarray(F.T)

    def blkT(M):
        return np.ascontiguousarray(M.reshape(16, 128, 16, 256).transpose(2, 1, 0, 3)).astype(ml_dtypes.bfloat16)

    _CONST["FTb"] = blkT(FT)
    ftz = _CONST["FTb"][8].copy()
    ftz[:, :, 0] = 0
    _CONST["FTz"] = ftz
    ne = np.zeros((128, 16, 128), np.float32)
    ne[:, :, 0] = np.where((np.arange(16)[None, :] * 128 + np.arange(128)[:, None]) % 2 == 0, 1.0, -1.0)
    _CONST["nyqE"] = ne.astype(ml_dtypes.bfloat16)
    _CONST["GTb"] = np.ascontiguousarray(G.reshape(32, 128, 8, 256).transpose(2, 1, 0, 3)).astype(ml_dtypes.bfloat16)
    f32 = np.float32
    tt = np.linspace(0.0, 1.0, L, dtype=f32)[:, None]
    omega = (f32(2.0 * math.pi) * np.arange(L, dtype=f32)[:, None] / f32(L)).astype(f32)
    bands = np.linspace(1e-4, 15, 16, dtype=f32)[None, :]
    feats = np.concatenate([tt, np.cos(bands * omega), -np.sin(bands * omega)], axis=-1).astype(f32)
    _CONST["featsT"] = np.ascontiguousarray(feats.T)
    min_decay = math.log(1e-2) / 1.5
    max_decay = math.log(1e-2) / 0.3
    deltas = np.abs(np.linspace(min_decay, max_decay, HW, dtype=f32))
    _CONST["window"] = np.exp(-tt * deltas).astype(f32)
    _CONST["ident"] = np.eye(128, dtype=f32)
    _CONST["ones"] = np.ones((128, 128), dtype=f32)
    return _CONST


def _pp(v, n):
    return np.ascontiguousarray(np.asarray(v, np.float32).reshape(n, 128).T)


def _bc(v):
    v = np.asarray(v, np.float32)
    return np.ascontiguousarray(np.broadcast_to(v[None, :], (128, v.shape[0])))


def make_in_maps(inp):
    C = _constants()
    g = lambda k: np.asarray(inp[k], np.float32)
    b_in = g("b_in")[0]
    shared = {
        "w_ada": np.ascontiguousarray(g("w_ada")[0]),
        "w_in": np.ascontiguousarray(g("w_in")[0]),
        "b_in_pp": _pp(b_in, 72),
        "b_in_v_bc": _bc(b_in[HW:2 * HW]),
        "lng_bc": _bc(g("sgu_ln_g")[0]),
        "lnb_bc": _bc(g("sgu_ln_b")[0]),
        "wsT": np.ascontiguousarray(g("sgu_w")[0].transpose(2, 0, 1)),
        "bs_bc": _bc(g("sgu_b")[0].reshape(-1)),
        "cw_pp": np.ascontiguousarray(g("hy_conv_w")[0].reshape(3, 24, 128).transpose(2, 0, 1)),
        "cb_pp": _pp(g("hy_conv_b")[0], 24),
        "featsT": C["featsT"],
        "hw1": np.ascontiguousarray(g("hy_w1")[0]),
        "hb1_pp": np.ascontiguousarray(g("hy_b1")[0].reshape(64, 1)),
        "hfr_pp": np.ascontiguousarray(g("hy_freq")[0].reshape(64, 1)),
        "hw2": np.ascontiguousarray(g("hy_w2")[0]),
        "hb2_pp": np.ascontiguousarray(g("hy_b2")[0].reshape(64, 1)),
        "hw3": np.ascontiguousarray(g("hy_w3")[0]),
        "window": C["window"],
        "skip_bc": np.ascontiguousarray(np.broadcast_to(g("hy_skip")[0][None], (128, 2, HW))),
        "FTb": C["FTb"], "GTb": C["GTb"], "FTz": C["FTz"], "nyqE": C["nyqE"],
        "w_a": np.ascontiguousarray(g("w_branch_a")[0]),
        "w_b": np.ascontiguousarray(g("w_branch_b")[0]),
        "w_o": np.ascontiguousarray(g("w_o")[0]),
        "bo_bc": _bc(g("b_o")[0]),
        "ln1g_bc": _bc(g("ln1_g")[0]), "ln1b_bc": _bc(g("ln1_b")[0]),
        "ln1g_pp": _pp(g("ln1_g")[0], 16), "ln1b_pp": _pp(g("ln1_b")[0], 16),
        "w_m1": np.ascontiguousarray(g("w_m1")[0]),
        "b_m1_pp": _pp(g("b_m1")[0], 64),
        "w_m2": np.ascontiguousarray(g("w_m2")[0]),
        "b_m2_pp": _pp(g("b_m2")[0], 16),
        "ln2g_bc": _bc(g("ln2_g")[0]), "ln2b_bc": _bc(g("ln2_b")[0]),
        "ident": C["ident"], "ones": C["ones"],
    }
    acc0 = np.zeros((128, 6 * D), np.float32)
    acc0[0, :] = g("b_ada")[0]
    shared["acc0"] = acc0
    xs = g("x")
    cs = g("c")
    maps = []
    for b in range(NB):
        m = dict(shared)
        m["x"] = np.ascontiguousarray(xs[b])
        m["c_pp"] = _pp(cs[b], 16)
        maps.append(m)
    return maps


_PROG = {}


def kernel(**inputs):
    if "nc" not in _PROG:
        _PROG["nc"] = build_program()
    nc = _PROG["nc"]
    in_maps = make_in_maps(inputs)
    res = run_bass_kernel_spmd(nc, in_maps, core_ids=list(range(NB)))
    return np.stack([np.asarray(r["out"], np.float32) for r in res.results], axis=0)
```
